# Optimizing a Trainium2 kernel written in Bass

```python
import jax, jax.numpy as jnp
from jax import lax
import numpy as np

D_MODEL = 1024
BATCH = 8
SEQ = 4096
DEPTH = 1
DEC_BATCH = 8
DEC_SEQ = 16
PAST_LEN = 2048

CHUNK = 64
EPS = 1e-6
D_INNER = 2 * D_MODEL
SSD_HEAD_DIM = 64
SSD_HEADS = D_INNER // SSD_HEAD_DIM
SSD_GROUPS = 8
SSD_HPG = SSD_HEADS // SSD_GROUPS
SSD_STATE = 128
CONV_WIDTH = 4
CONV_DIM = D_INNER + 2 * SSD_GROUPS * SSD_STATE
SSD_NORM_GROUP = D_INNER // SSD_GROUPS
POOL_WINDOWS = (2, 4, 8, 16)
POOL_GROUPS = len(POOL_WINDOWS)
D_POOL = D_MODEL
POOL_GROUP_DIM = D_POOL // POOL_GROUPS
POOL_HIST = max(POOL_WINDOWS) - 1
D_FF = 4 * D_MODEL
N_BRANCH = 2
SPLIT_Z = D_INNER
SPLIT_XBC = SPLIT_Z + CONV_DIM
SPLIT_DT = SPLIT_XBC + SSD_HEADS
SPLIT_POOL = SPLIT_DT + D_POOL
D_IN_PROJ = SPLIT_POOL + N_BRANCH * D_MODEL

kernel_name = 'hybrid_ssd_pool_streaming_encoder_step'


def rms_norm(x, g):
    xf = x.astype(jnp.float32)
    y = xf * lax.rsqrt(jnp.mean(xf * xf, axis=-1, keepdims=True) + EPS)
    return (y * g.astype(jnp.float32)).astype(x.dtype)


def causal_dwconv(hist, u, w, b):
    L = u.shape[1]
    up = jnp.concatenate([hist, u], axis=1)
    out = b + up[:, 0:L] * w[0]
    for k in range(1, CONV_WIDTH):
        out = out + up[:, k:k + L] * w[k]
    return out, up[:, -(CONV_WIDTH - 1):]


def ssd_chunked(x, dt, a, bm, cm, h0):
    f32 = jnp.float32
    b, L = x.shape[:2]
    cl = CHUNK if L % CHUNK == 0 else L
    nc = L // cl
    G, J, P, N = SSD_GROUPS, SSD_HPG, SSD_HEAD_DIM, SSD_STATE
    xr = (x.astype(f32) * dt[..., None]).reshape(b, nc, cl, G, J, P)
    a_cs = jnp.cumsum((dt * a).reshape(b, nc, cl, G, J), axis=2)
    br = bm.astype(f32).reshape(b, nc, cl, G, N)
    cr = cm.astype(f32).reshape(b, nc, cl, G, N)
    seg = a_cs[:, :, :, None] - a_cs[:, :, None, :]
    causal = jnp.tril(jnp.ones((cl, cl), dtype=bool))[None, None, :, :, None, None]
    decay = jnp.exp(jnp.where(causal, seg, -jnp.inf))
    cb = jnp.einsum('bclgn,bcsgn->bclsg', cr, br)
    y_diag = jnp.einsum('bclsg,bclsgj,bcsgjp->bclgjp', cb, decay, xr)
    decay_to_end = jnp.exp(a_cs[:, :, -1:] - a_cs)
    states = jnp.einsum('bclgn,bclgj,bclgjp->bcgjpn', br, decay_to_end, xr)
    chunk_decay = jnp.exp(a_cs[:, :, -1])

    def step(h, inp):
        s_c, d_c = inp
        return h * d_c[..., None, None] + s_c, h

    h_final, h_prev = lax.scan(step, h0.astype(f32),
                               (jnp.moveaxis(states, 1, 0), jnp.moveaxis(chunk_decay, 1, 0)))
    h_prev = jnp.moveaxis(h_prev, 0, 1)
    y_off = jnp.einsum('bclgn,bcgjpn,bclgj->bclgjp', cr, h_prev, jnp.exp(a_cs))
    y = (y_diag + y_off).reshape(b, L, G, J, P)
    return y, h_final


def multiscale_pool(hist, p, pos0):
    f32 = jnp.float32
    b, L, _ = p.shape
    pp = jnp.concatenate([hist, p], axis=1).astype(f32)
    cs = jnp.concatenate([jnp.zeros_like(pp[:, :1]), jnp.cumsum(pp, axis=1)], axis=1)
    pos = pos0 + jnp.arange(L)
    outs = []
    for gi, w in enumerate(POOL_WINDOWS):
        sl = slice(gi * POOL_GROUP_DIM, (gi + 1) * POOL_GROUP_DIM)
        hi = cs[:, POOL_HIST + 1:POOL_HIST + 1 + L, sl]
        lo = cs[:, POOL_HIST + 1 - w:POOL_HIST + 1 - w + L, sl]
        cnt = jnp.minimum(pos + 1, w).astype(f32)[None, :, None]
        outs.append((hi - lo) / cnt)
    mean = jnp.concatenate(outs, axis=-1)
    return (mean - p.astype(f32)).astype(p.dtype), pp[:, -POOL_HIST:].astype(p.dtype)


def encoder_layer(x, c, conv_hist, pool_hist, h0, pos0, w_ada, b_ada, g_pre_mix, g_post_mix,
                  g_pre_mlp, g_post_mlp, w_in, conv_w, conv_b, dt_bias, a_log, d_skip, g_ssd_norm,
                  w_ssd_out, w_pool_group, pool_scale, w_o, w_up, w_down):
    f32 = jnp.float32
    b, L, _ = x.shape
    mod = (jax.nn.silu(c) @ w_ada + b_ada)[:, None, :]
    sh1, sc1, gt1, sh2, sc2, gt2 = jnp.split(mod, 6, axis=-1)
    u = rms_norm(x, g_pre_mix) * (1 + sc1) + sh1
    proj = u @ w_in
    z, xbc, dt_raw, p, gates = jnp.split(proj, [SPLIT_Z, SPLIT_XBC, SPLIT_DT, SPLIT_POOL], axis=-1)
    xbc_c, conv_state = causal_dwconv(conv_hist, xbc, conv_w, conv_b)
    xbc_c = jax.nn.silu(xbc_c)
    xs, bm, cm = jnp.split(xbc_c, [D_INNER, D_INNER + SSD_GROUPS * SSD_STATE], axis=-1)
    xs = xs.reshape(b, L, SSD_GROUPS, SSD_HPG, SSD_HEAD_DIM)
    bm = bm.reshape(b, L, SSD_GROUPS, SSD_STATE)
    cm = cm.reshape(b, L, SSD_GROUPS, SSD_STATE)
    dt = jax.nn.softplus(dt_raw.astype(f32) + dt_bias.astype(f32)).reshape(b, L, SSD_GROUPS, SSD_HPG)
    a = -jnp.exp(a_log.astype(f32)).reshape(SSD_GROUPS, SSD_HPG)
    h0g = h0.reshape(b, SSD_GROUPS, SSD_HPG, SSD_HEAD_DIM, SSD_STATE)
    y, h_final = ssd_chunked(xs, dt, a, bm, cm, h0g)
    y = y + xs.astype(f32) * d_skip.astype(f32).reshape(SSD_GROUPS, SSD_HPG)[..., None]
    yg = (y.reshape(b, L, D_INNER) * jax.nn.silu(z.astype(f32))).reshape(b, L, SSD_GROUPS, SSD_NORM_GROUP)
    yg = yg * lax.rsqrt(jnp.mean(yg * yg, axis=-1, keepdims=True) + EPS)
    y_ssd = (yg.reshape(b, L, D_INNER) * g_ssd_norm.astype(f32)).astype(x.dtype) @ w_ssd_out
    pm, pool_state = multiscale_pool(pool_hist, p, pos0)
    y_pool = jnp.einsum('blgc,gcd->blgd', pm.reshape(b, L, POOL_GROUPS, POOL_GROUP_DIM),
                        w_pool_group).reshape(b, L, D_POOL) * pool_scale
    gate_ssd, gate_pool = jnp.split(jax.nn.sigmoid(gates.astype(f32)).astype(x.dtype), N_BRANCH, axis=-1)
    mix = (gate_ssd * y_ssd + gate_pool * y_pool) @ w_o
    x = x + gt1 * rms_norm(mix, g_post_mix)
    v = rms_norm(x, g_pre_mlp) * (1 + sc2) + sh2
    hdn = jnp.square(jax.nn.relu(v @ w_up))
    x = x + gt2 * rms_norm(hdn @ w_down, g_post_mlp)
    new_h = h_final.reshape(b, SSD_HEADS, SSD_HEAD_DIM, SSD_STATE).astype(x.dtype)
    return x, new_h, conv_state, pool_state


def setup_inputs(seed: int = 0) -> dict:
    key = jax.random.key(seed)
    ks = jax.random.split(key, 32)
    nrm = jax.random.normal
    f32 = jnp.float32
    dt0 = jnp.exp(jax.random.uniform(ks[13], (DEPTH, SSD_HEADS), f32, np.log(1e-3), np.log(1e-1)))
    return {
        'x_prompt': nrm(ks[0], (BATCH, SEQ, D_MODEL), f32),
        'x_sample': nrm(ks[1], (DEC_BATCH, DEC_SEQ, D_MODEL), f32),
        'state_ssm': 0.5 * nrm(ks[2], (DEPTH, DEC_BATCH, SSD_HEADS, SSD_HEAD_DIM, SSD_STATE), f32),
        'state_conv': nrm(ks[3], (DEPTH, DEC_BATCH, CONV_WIDTH - 1, CONV_DIM), f32),
        'state_pool': nrm(ks[4], (DEPTH, DEC_BATCH, POOL_HIST, D_POOL), f32),
        'c_prompt': nrm(ks[5], (BATCH, D_MODEL), f32),
        'c_sample': nrm(ks[6], (DEC_BATCH, D_MODEL), f32),
        'w_ada': 0.5 * D_MODEL ** -0.5 * nrm(ks[7], (DEPTH, D_MODEL, 6 * D_MODEL), f32),
        'b_ada': 0.01 * nrm(ks[8], (DEPTH, 6 * D_MODEL), f32),
        'g_pre_mix': 1.0 + 0.05 * nrm(ks[9], (DEPTH, D_MODEL), f32),
        'g_post_mix': 1.0 + 0.05 * nrm(ks[10], (DEPTH, D_MODEL), f32),
        'g_pre_mlp': 1.0 + 0.05 * nrm(ks[11], (DEPTH, D_MODEL), f32),
        'g_post_mlp': 1.0 + 0.05 * nrm(ks[12], (DEPTH, D_MODEL), f32),
        'w_in': D_MODEL ** -0.5 * nrm(ks[14], (DEPTH, D_MODEL, D_IN_PROJ), f32),
        'conv_w': 0.5 * nrm(ks[15], (DEPTH, CONV_WIDTH, CONV_DIM), f32),
        'conv_b': 0.01 * nrm(ks[16], (DEPTH, CONV_DIM), f32),
        'dt_bias': dt0 + jnp.log(-jnp.expm1(-dt0)),
        'a_log': jnp.log(jax.random.uniform(ks[17], (DEPTH, SSD_HEADS), f32, 1.0, 16.0)),
        'd_skip': 1.0 + 0.1 * nrm(ks[18], (DEPTH, SSD_HEADS), f32),
        'g_ssd_norm': 1.0 + 0.05 * nrm(ks[19], (DEPTH, D_INNER), f32),
        'w_ssd_out': D_INNER ** -0.5 * nrm(ks[20], (DEPTH, D_INNER, D_MODEL), f32),
        'w_pool_group': POOL_GROUP_DIM ** -0.5 * nrm(ks[21], (DEPTH, POOL_GROUPS, POOL_GROUP_DIM, POOL_GROUP_DIM), f32),
        'pool_scale': 1.0 + 0.1 * nrm(ks[22], (DEPTH, D_POOL), f32),
        'w_o': D_MODEL ** -0.5 * nrm(ks[23], (DEPTH, D_MODEL, D_MODEL), f32),
        'w_up': D_MODEL ** -0.5 * nrm(ks[24], (DEPTH, D_MODEL, D_FF), f32),
        'w_down': D_FF ** -0.5 * nrm(ks[25], (DEPTH, D_FF, D_MODEL), f32),
    }


def reference(x_prompt, x_sample, state_ssm, state_conv, state_pool, c_prompt, c_sample,
              w_ada, b_ada, g_pre_mix, g_post_mix, g_pre_mlp, g_post_mlp, w_in, conv_w, conv_b,
              dt_bias, a_log, d_skip, g_ssd_norm, w_ssd_out, w_pool_group, pool_scale, w_o, w_up, w_down):
    yp, ys = x_prompt, x_sample
    bp = x_prompt.shape[0]
    ssm_p, conv_p, pool_p, ssm_s, conv_s, pool_s = [], [], [], [], [], []
    for l in range(DEPTH):
        lw = dict(w_ada=w_ada[l], b_ada=b_ada[l], g_pre_mix=g_pre_mix[l], g_post_mix=g_post_mix[l],
                  g_pre_mlp=g_pre_mlp[l], g_post_mlp=g_post_mlp[l], w_in=w_in[l], conv_w=conv_w[l],
                  conv_b=conv_b[l], dt_bias=dt_bias[l], a_log=a_log[l], d_skip=d_skip[l],
                  g_ssd_norm=g_ssd_norm[l], w_ssd_out=w_ssd_out[l], w_pool_group=w_pool_group[l],
                  pool_scale=pool_scale[l], w_o=w_o[l], w_up=w_up[l], w_down=w_down[l])
        zc = jnp.zeros((bp, CONV_WIDTH - 1, CONV_DIM), yp.dtype)
        zp = jnp.zeros((bp, POOL_HIST, D_POOL), yp.dtype)
        zh = jnp.zeros((bp, SSD_HEADS, SSD_HEAD_DIM, SSD_STATE), yp.dtype)
        yp, h_p, c_p, p_p = encoder_layer(yp, c_prompt, zc, zp, zh, 0, **lw)
        ys, h_s, c_s, p_s = encoder_layer(ys, c_sample, state_conv[l], state_pool[l], state_ssm[l], PAST_LEN, **lw)
        ssm_p.append(h_p); conv_p.append(c_p); pool_p.append(p_p)
        ssm_s.append(h_s); conv_s.append(c_s); pool_s.append(p_s)
    return (yp, ys, jnp.stack(ssm_p), jnp.stack(conv_p), jnp.stack(pool_p),
            jnp.stack(ssm_s), jnp.stack(conv_s), jnp.stack(pool_s))
```

```python
import numpy as np
from contextlib import ExitStack
import concourse.bass as bass
import concourse.mybir as mybir
from concourse.bass_utils import run_bass_kernel_spmd

F32 = mybir.dt.float32
BF16 = mybir.dt.bfloat16
AF = mybir.ActivationFunctionType
ALU = mybir.AluOpType

D = 1024
SEQ = 4096
DEC = 16
DI = 2048
NH = 32
HD = 64
NG = 8
NS = 128
CONVD = 4096
DPROJ = 9248
DFF = 4096
EPS = 1e-6
PAST = 2048
TSUB = 128
NSUB = 2
TT_MAX = TSUB * NSUB
WSLOT_ELEMS = 4096
NWSLOT = 3

C_BADA, C_GPM, C_GPL, C_CW, C_CB, C_GSN, C_PSC, NCOL = 0, 48, 56, 64, 192, 224, 240, 248


class Res:
    __slots__ = ("w", "r")

    def __init__(self):
        self.w = None
        self.r = {}


class Sig:
    def __init__(self, sem, name):
        self.sem = sem
        self.cnt = 0
        self.name = name


class Buf:
    def __init__(self, t, res=None, psum=False):
        self.t = t
        self.res = res if res is not None else Res()
        self.psum = psum

    def __getitem__(self, idx):
        return self.t[idx]


class Ring:
    def __init__(self, bufs):
        self.bufs = bufs
        self.i = 0

    def take(self):
        b = self.bufs[self.i % len(self.bufs)]
        self.i += 1
        return b


class K:
    def __init__(self, nc, es):
        self.nc = nc
        self.es = es
        self.eng = {"pe": nc.tensor, "act": nc.scalar, "dve": nc.vector, "pool": nc.gpsimd, "sp": nc.sync}
        self.sig = {}
        for e in self.eng:
            self.sig[e] = Sig(es.enter_context(nc.semaphore("s_" + e)), e)
        self.known = {e: {} for e in self.eng}
        self.nbuf = 0
        self.dsigs = {}
        self.tag = None

    def sb(self, shape, dt, name=None):
        self.nbuf += 1
        t = self.es.enter_context(self.nc.sbuf_tensor("s_" + (name or f"sb{self.nbuf}"), list(shape), dt))
        return Buf(t)

    def ps(self, shape, dt, name=None):
        self.nbuf += 1
        t = self.es.enter_context(self.nc.psum_tensor(name or f"ps{self.nbuf}", list(shape), dt))
        return Buf(t, psum=True)

    def ring(self, n, shape, dt, name):
        return Ring([self.sb(shape, dt, f"{name}{i}") for i in range(n)])

    def dsig_ring(self, n, name):
        sigs = [Sig(self.es.enter_context(self.nc.semaphore(f"d_{name}{i}")), f"{name}{i}") for i in range(n)]
        self.dsigs[name] = sigs
        return Ring(sigs)

    def _waits(self, e, reads, writes):
        needs = {}

        def add(sigv, same_ok):
            s, v = sigv
            if s is self.sig[e] and not same_ok and e == "pe":
                return
            if needs.get(s, 0) < v:
                needs[s] = v

        for b in reads:
            if b.res.w is not None:
                add(b.res.w, True)
            if b.psum:
                for s, v in b.res.r.items():
                    if s is not self.sig[e]:
                        add((s, v), True)
        for b in writes:
            if b.res.w is not None:
                add(b.res.w, False)
            for s, v in b.res.r.items():
                add((s, v), False)
        kn = self.known[e]
        eng = self.eng[e]
        for s, v in needs.items():
            if kn.get(s, 0) >= v:
                continue
            eng.wait_ge(s.sem, v)
            kn[s] = v

    def op(self, e, fn, reads=(), writes=()):
        reads = [b for b in reads if b is not None]
        writes = [b for b in writes if b is not None]
        self._waits(e, reads, writes)
        inst = fn(self.eng[e])
        sg = self.sig[e]
        sg.cnt += 1
        inst.then_inc(sg.sem, 1)
        for b in reads:
            b.res.r[sg] = sg.cnt
        for b in writes:
            b.res.w = (sg, sg.cnt)
            b.res.r = {}
        return inst

    def pe_quiet(self, fn, reads=(), writes=()):
        reads = [b for b in reads if b is not None]
        writes = [b for b in writes if b is not None]
        self._waits("pe", reads, writes)
        fn(self.eng["pe"])
        sg = self.sig["pe"]
        nxt = sg.cnt + 1
        for b in reads:
            b.res.r[sg] = nxt
        for b in writes:
            b.res.w = (sg, nxt)
            b.res.r = {}

    def dma(self, q, ring, out, in_, reads=(), writes=()):
        reads = [b for b in reads if b is not None]
        writes = [b for b in writes if b is not None]
        self._waits(q, reads, writes)
        sg = ring.take()
        kn = self.known[q]
        if kn.get(sg, 0) < sg.cnt:
            self.eng[q].wait_ge(sg.sem, sg.cnt)
            kn[sg] = sg.cnt
        inst = self.eng[q].dma_start(out=out, in_=in_)
        sg.cnt += 16
        inst.then_inc(sg.sem, 16)
        for b in reads:
            b.res.r[sg] = sg.cnt
        for b in writes:
            b.res.w = (sg, sg.cnt)
            b.res.r = {}

    def finish(self):
        sp = self.eng["sp"]
        for sigs in self.dsigs.values():
            for sg in sigs:
                if sg.cnt > 0:
                    sp.wait_ge(sg.sem, sg.cnt)
        for e, sg in self.sig.items():
            if e != "sp" and sg.cnt > 0:
                sp.wait_ge(sg.sem, sg.cnt)


def build_nc(n_ptiles=SEQ // TT_MAX, do_sample=True):
    nc = bass.Bass("TRN2", target_bir_lowering=False)
    es = ExitStack()
    k = K(nc, es)
    NP_TOK = n_ptiles * TT_MAX

    def din(name, shape, dt=F32):
        return nc.dram_tensor(name, list(shape), dt, kind="ExternalInput").ap()

    def dout(name, shape, dt=F32):
        return nc.dram_tensor(name, list(shape), dt, kind="ExternalOutput").ap()

    def dscr(name, shape, dt=BF16):
        return nc.dram_tensor(name, list(shape), dt, kind="Internal").ap()

    xp_d = din("xp", [NP_TOK, D])
    xs_d = din("xs", [DEC, D])
    ccol_d = din("ccol", [128, 8, 2])
    colp_d = din("colp", [128, NCOL])
    rbada_d = din("rbada", [128, 2048])
    rgpost_d = din("rgpost", [128, 2048])
    rhead_d = din("rhead", [128, 96])
    rgsn_d = din("rgsn", [128, DI])
    h0T_d = din("h0T", [128, DI])
    convT_d = din("convT", [128, 32, 3])
    poolT_d = din("poolT", [128, 8, 15])
    w_ada_d = din("w_ada", [D, 6 * D])
    w_in_d = din("w_in", [D, DPROJ])
    w_so_d = din("w_so", [DI, D])
    w_pool_d = din("w_pool", [D, 256])
    w_o_d = din("w_o", [D, D])
    w_up_d = din("w_up", [D, DFF])
    w_down_d = din("w_down", [DFF, D])

    yp_d = dout("yp", [NP_TOK, D])
    ys_d = dout("ys", [DEC, D])
    hTp_d = dout("hTp", [128, DI])
    hTs_d = dout("hTs", [128, DI])
    convp_d = dout("convp", [128, 32, 3])
    convs_d = dout("convs", [128, 32, 3])
    poolp_d = dout("poolp", [128, 8, 15])
    pools_d = dout("pools", [128, 8, 15])

    wb_ada = dscr("wb_ada", [D, 6 * D])
    wb_in = dscr("wb_in", [D, DPROJ])
    wb_so = dscr("wb_so", [DI, D])
    wb_pool = dscr("wb_pool", [D, 256])
    wb_o = dscr("wb_o", [D, D])
    wb_up = dscr("wb_up", [D, DFF])
    wb_down = dscr("wb_down", [DFF, D])

    ld_ring = k.dsig_ring(12, "ld")
    st_ring = k.dsig_ring(8, "st")
    cast_ring = k.dsig_ring(20, "cast")
    sst_ring = k.dsig_ring(6, "sst")

    def cast(name, src, dst, rows, cols, inner, rstep):
        for r0 in range(0, rows, rstep):
            s_ = src[r0:r0 + rstep, :].rearrange("r (a b) -> r a b", b=inner)
            d_ = dst[r0:r0 + rstep, :].rearrange("r (a b) -> r a b", b=inner)
            sg = cast_ring.take()
            eng = k.eng["pool"]
            kn = k.known["pool"]
            if kn.get(sg, 0) < sg.cnt:
                eng.wait_ge(sg.sem, sg.cnt)
                kn[sg] = sg.cnt
            inst = eng.dma_start(out=d_, in_=s_)
            sg.cnt += 16
            inst.then_inc(sg.sem, 16)
            cast_parts.setdefault(name, []).append((sg, sg.cnt))

    cast_parts = {}

    ident_b = k.sb([128, 128], BF16, "ident_b")
    ones_f = k.sb([128, 128], F32, "ones_f")
    ones_b = k.sb([128, 128], BF16, "ones_b")
    U_f = k.sb([128, 128], F32, "U_f")
    L_f = k.sb([128, 128], F32, "L_f")
    colp = k.sb([128, NCOL], F32, "colp")
    rhead = k.sb([128, 96], F32, "rhead")
    arow = k.sb([128, 32], F32, "arow")
    icnt = k.sb([128, 4, 16], F32, "icnt")
    wpool = k.sb([128, 8, 256], BF16, "wpool")
    ccol = k.sb([128, 8, 2], F32, "ccol")
    scol = k.sb([128, 8, 2], F32, "scol")
    modc = k.sb([128, 32, 2], F32, "modc")
    A1 = k.sb([128, 8, 2], F32, "A1")
    B1 = k.sb([128, 8, 2], F32, "B1")
    A2 = k.sb([128, 8, 2], F32, "A2")
    B2 = k.sb([128, 8, 2], F32, "B2")
    G1 = k.sb([128, D], F32, "G1")
    G2 = k.sb([128, D], F32, "G2")
    gs_scr = nc.dram_tensor("gs_scr", [2, 128, D], F32, kind="Internal").ap()
    gs_res = Buf(None)

    psf = Ring([k.ps([128, 512], F32, f"psf{i}") for i in range(7)])
    psx = k.ps([128, 512], F32, "psx")

    def psb_view(b):
        return b.t[:].bitcast(BF16)

    wslots = Ring([k.sb([128, WSLOT_ELEMS], BF16, f"wsl{i}") for i in range(NWSLOT)])
    xbuf = Ring([k.sb([128, D], F32, f"xb{i}") for i in range(2 * NSUB)])
    f32tmp = Ring([k.sb([128, D], F32, f"ft{i}") for i in range(2)])
    xnbuf = Ring([k.sb([128, D], BF16, f"xn{i}") for i in range(2)])
    junk_r = Ring([k.sb([128, D], BF16, f"junk{i}") for i in range(1)])
    chunk = Ring([k.sb([128, TT_MAX], BF16, f"ch{i}") for i in range(40)])
    uT_sets = Ring([[k.sb([128, TT_MAX], BF16, f"uT{a}_{i}") for i in range(8)] for a in range(2)])
    xs_tok = [k.sb([128, DI], BF16, f"xstok{i}") for i in range(NSUB)]
    B_tok = [k.sb([128, NG * NS], BF16, f"btok{i}") for i in range(NSUB)]
    zs_tok = [k.sb([128, DI], BF16, f"zstok{i}") for i in range(NSUB)]
    pbring = Ring([k.sb([128, TT_MAX + 3], BF16, f"pb{i}") for i in range(8)])
    xf_r = Ring([k.sb([128, TT_MAX], BF16, f"xf{i}") for i in range(8)])
    dgring = Ring([k.sb([128, 4, 128], BF16, f"dg{i}") for i in range(8)])
    dg_ready = {}
    hist = k.sb([128, 32, 3], BF16, "hist")
    convout = k.sb([128, 32, 3], F32, "convout")
    pp = [k.sb([128, 15 + TT_MAX], F32, f"pp{i}") for i in range(8)]
    psc = Ring([k.sb([128, 15 + TT_MAX], F32, f"psc{i}") for i in range(3)])
    small = Ring([k.sb([128, 32], F32, f"sm{i}") for i in range(32)])
    dtbuf = [k.sb([128, 32], F32, f"dt{i}") for i in range(NSUB)]
    cbm_r = Ring([k.sb([128, 128], F32, f"cbm{i}") for i in range(3)])
    W_r = Ring([k.sb([128, 128], F32, f"Wh{i}") for i in range(12)])
    M_r = Ring([k.sb([128, 4, 128], BF16, f"Mg{i}") for i in range(3)])
    xg_r = Ring([k.sb([128, 3, 256], BF16, f"xg{i}") for i in range(4)])
    ygT_all = k.sb([128, 16, TT_MAX], BF16, "ygT_all")
    gsn_bc = k.sb([128, DI], BF16, "gsn_bc")
    tmp256 = Ring([k.sb([128, 256], F32, f"t256_{i}") for i in range(2)])
    ybf_r = Ring([k.sb([128, 256], BF16, f"ybf{i}") for i in range(3)])
    yz_r = Ring([k.sb([128, DI], BF16, f"yz{i}") for i in range(2)])
    for b_ in yz_r.bufs:
        b_.subs = [Buf(b_.t) for _ in range(NG)]
    hst = k.sb([128, DI], F32, "hst")
    hbf = [k.sb([128, 256], BF16, f"hbf{g}") for g in range(NG)]
    ab_r = Ring([k.sb([128, TT_MAX], F32, f"ab{i}") for i in range(2)])
    relu_r = Ring([k.sb([128, TT_MAX], BF16, f"rl{i}") for i in range(3)])

    hst_res = [Buf(None) for _ in range(NG)]

    def memset(e, buf, val):
        k.op(e, lambda eng: eng.memset(buf[:], val), writes=[buf])

    memset("pool", ones_f, 1.0)
    memset("pool", ones_b, 1.0)
    k.op("pool", lambda eng: eng.affine_select(out=ident_b[:], in_=ones_b[:], pattern=[[-1, 128]], compare_op=ALU.is_equal,
                                               fill=0.0, base=0, channel_multiplier=1), reads=[ones_b], writes=[ident_b])
    k.op("pool", lambda eng: eng.affine_select(out=U_f[:], in_=ones_f[:], pattern=[[1, 128]], compare_op=ALU.is_ge,
                                               fill=0.0, base=0, channel_multiplier=-1), reads=[ones_f], writes=[U_f])
    k.op("pool", lambda eng: eng.affine_select(out=L_f[:], in_=ones_f[:], pattern=[[-1, 128]], compare_op=ALU.is_gt,
                                               fill=0.0, base=0, channel_multiplier=1), reads=[ones_f], writes=[L_f])

    cast("w_ada", w_ada_d, wb_ada, D, 6 * D, 2048, 256)
    cast("w_in", w_in_d, wb_in, D, DPROJ, 1156, 256)
    cast("w_pool", w_pool_d, wb_pool, D, 256, 256, 1024)
    cast("w_so", w_so_d, wb_so, DI, D, 1024, 1024)
    cast("w_o", w_o_d, wb_o, D, D, 1024, 1024)
    cast("w_up", w_up_d, wb_up, D, DFF, 2048, 512)
    cast("w_down", w_down_d, wb_down, DFF, D, 1024, 1024)

    def wait_cast(q, name):
        kn = k.known[q]
        for sg, v in cast_parts[name]:
            if kn.get(sg, 0) < v:
                k.eng[q].wait_ge(sg.sem, v)
                kn[sg] = v

    k.dma("sp", ld_ring, colp[:], colp_d, writes=[colp])
    k.dma("sp", ld_ring, rhead[:], rhead_d, writes=[rhead])
    k.dma("sp", ld_ring, ccol[:], ccol_d, writes=[ccol])
    k.dma("pool", st_ring, gsn_bc[:], rgsn_d, writes=[gsn_bc])
    k.op("act", lambda eng: eng.activation(out=arow[:], in_=rhead[:, 32:64], func=AF.Exp), reads=[rhead], writes=[arow])
    k.op("dve", lambda eng: eng.tensor_scalar(out=arow[:], in0=arow[:], scalar1=-1.0, scalar2=None, op0=ALU.mult),
         reads=[arow], writes=[arow])
    iot = small.take()
    k.op("pool", lambda eng: eng.iota(iot[:, 0:16], pattern=[[1, 16]], base=1, channel_multiplier=0,
                                      allow_small_or_imprecise_dtypes=True), writes=[iot])
    for gi, w in enumerate((2, 4, 8, 16)):
        t_ = small.take()
        k.op("dve", lambda eng, t_=t_, w=w: eng.tensor_scalar(out=t_[:, 0:16], in0=iot[:, 0:16], scalar1=float(w), scalar2=None,
                                                              op0=ALU.min), reads=[iot], writes=[t_])
        k.op("dve", lambda eng, t_=t_, gi=gi: eng.reciprocal(out=icnt[:, gi, :], in_=t_[:, 0:16]), reads=[t_], writes=[icnt])

    k.op("act", lambda eng: eng.activation(out=scol[:], in_=ccol[:], func=AF.Silu), reads=[ccol], writes=[scol])
    scol_b = k.sb([128, 8, 2], BF16, "scol_b")
    k.op("dve", lambda eng: eng.tensor_copy(out=scol_b[:], in_=scol[:]), reads=[scol], writes=[scol_b])
    screp = []
    for s in range(2):
        v = xs_tok[s].t[:, 0:1024].rearrange("p (a b) -> p a b", b=128)
        screp.append(v)
        k.op("dve", lambda eng, s=s, v=v: eng.tensor_copy(out=v, in_=scol[:, :, s:s + 1].broadcast_to([128, 8, 128])),
             reads=[scol], writes=[xs_tok[s]])
    ps_col = psx
    col_chunks = list(range(0, 16)) + list(range(24, 40))
    colidx = {ch: mi for mi, ch in enumerate(col_chunks)}
    wb_ada_v = wb_ada.rearrange("(kc p) c -> p kc c", p=128)
    gload = {0: (f32tmp.bufs[0], f32tmp.bufs[1]), 1: (xbuf.bufs[0], xbuf.bufs[1])}
    for which in range(2):
        rbt, rgt = gload[which]
        k.dma("sp", ld_ring, rbt[:], rbada_d[:, which * D:(which + 1) * D], writes=[rbt])
        k.dma("sp", ld_ring, rgt[:], rgpost_d[:, which * D:(which + 1) * D], writes=[rgt])
    wait_cast("sp", "w_ada")
    for blk in range(12):
        wsl = wslots.take()
        wv = wsl.t[:, 0:4096].rearrange("p (a b) -> p a b", b=512)
        k.dma("sp", ld_ring, wv, wb_ada_v[:, :, blk * 512:(blk + 1) * 512], writes=[wsl])
        if blk * 4 in colidx:
            for cc in range(4):
                mi = colidx[blk * 4 + cc]
                for kc in range(8):
                    fn = lambda eng, cc=cc, mi=mi, kc=kc: eng.matmul(ps_col[:, mi * 2:mi * 2 + 2], lhsT=wv[:, kc, cc * 128:(cc + 1) * 128],
                                                                     rhs=scol_b[:, kc, :], start=(kc == 0), stop=(kc == 7))
                    if kc < 7:
                        k.pe_quiet(fn, reads=[wsl, scol_b], writes=[ps_col])
                    else:
                        k.op("pe", fn, reads=[wsl, scol_b], writes=[ps_col])
        else:
            which, q = (0, blk - 4) if blk < 6 else (1, blk - 10)
            rbt, rgt = gload[which]
            for s in range(2):
                psr = psf.take()
                for kc in range(8):
                    fn = lambda eng, s=s, kc=kc, psr=psr: eng.matmul(psr[:, 0:512], lhsT=screp[s][:, kc, :], rhs=wv[:, kc, :],
                                                                     start=(kc == 0), stop=(kc == 7))
                    if kc < 7:
                        k.pe_quiet(fn, reads=[wsl, xs_tok[s]], writes=[psr])
                    else:
                        k.op("pe", fn, reads=[wsl, xs_tok[s]], writes=[psr])
                if s == 0:
                    Gb = G1 if which == 0 else G2
                    dst = Gb[:, q * 512:(q + 1) * 512]
                    wr = [Gb]
                else:
                    dst = hst[:, which * D + q * 512:which * D + (q + 1) * 512]
                    wr = hst_res
                k.op("dve", lambda eng, psr=psr, dst=dst, q=q, rbt=rbt: eng.tensor_tensor(out=dst, in0=psr[:, 0:512],
                                                                                          in1=rbt[:, q * 512:(q + 1) * 512], op=ALU.add),
                     reads=[psr, rbt], writes=wr)
                k.op("dve", lambda eng, dst=dst, q=q, rgt=rgt: eng.tensor_tensor(out=dst, in0=dst, in1=rgt[:, q * 512:(q + 1) * 512], op=ALU.mult),
                     reads=wr + [rgt], writes=wr)
    k.dma("sp", ld_ring, gs_scr.rearrange("w p d -> p w d"), hst[:].rearrange("p (w d) -> p w d", w=2), reads=hst_res, writes=[gs_res])
    for qi, (c0, b0) in enumerate(((0, 0), (8, 8), (16, 24), (24, 32))):
        k.op("dve", lambda eng, c0=c0, b0=b0: eng.tensor_tensor(
            out=modc[:, c0:c0 + 8, :], in0=ps_col[:, c0 * 2:(c0 + 8) * 2].rearrange("p (a b) -> p a b", b=2),
            in1=colp[:, C_BADA + b0:C_BADA + b0 + 8].unsqueeze(2).broadcast_to([128, 8, 2]), op=ALU.add),
            reads=[ps_col, colp], writes=[modc])
    for (Aout, Bout, sh0, sc0, gcol) in ((A1, B1, 0, 8, C_GPM), (A2, B2, 16, 24, C_GPL)):
        k.op("dve", lambda eng, Aout=Aout, sc0=sc0: eng.tensor_scalar(out=Aout[:], in0=modc[:, sc0:sc0 + 8, :], scalar1=1.0, scalar2=None,
                                                                       op0=ALU.add), reads=[modc], writes=[Aout])
        k.op("dve", lambda eng, Aout=Aout, gcol=gcol: eng.tensor_tensor(out=Aout[:], in0=Aout[:],
                                                                         in1=colp[:, gcol:gcol + 8].unsqueeze(2).broadcast_to([128, 8, 2]),
                                                                         op=ALU.mult), reads=[Aout, colp], writes=[Aout])
        k.op("dve", lambda eng, Bout=Bout, sh0=sh0: eng.tensor_copy(out=Bout[:], in_=modc[:, sh0:sh0 + 8, :]), reads=[modc], writes=[Bout])

    wait_cast("sp", "w_pool")
    k.dma("sp", ld_ring, wpool[:], wb_pool.rearrange("(a p) c -> p a c", p=128), writes=[wpool])

    wb_in_v = wb_in.rearrange("(kc p) c -> p kc c", p=128)
    wb_so_v = wb_so.rearrange("(kc p) c -> p kc c", p=128)
    wb_o_v = wb_o.rearrange("(kc p) c -> p kc c", p=128)
    wb_up_v = wb_up.rearrange("(kc p) c -> p kc c", p=128)
    wb_down_v = wb_down.rearrange("(kc p) c -> p kc c", p=128)

    Zb, MIDb, ENDb = [], [], []
    for cb in range(4):
        Zb.append(("w_in", wb_in_v[:, :, cb * 512:(cb + 1) * 512], 8, 512))
    for cb in range(8):
        MIDb.append(("w_in", wb_in_v[:, :, 2048 + cb * 512:2048 + (cb + 1) * 512], 8, 512))
    MIDb.append(("w_in", wb_in_v[:, :, 6144:6176], 8, 32))
    for cb in range(2):
        MIDb.append(("w_in", wb_in_v[:, :, 6176 + cb * 512:6176 + (cb + 1) * 512], 8, 512))
    for cb in range(4):
        MIDb.append(("w_in", wb_in_v[:, :, 7200 + cb * 512:7200 + (cb + 1) * 512], 8, 512))
    for cb in range(4):
        MIDb.append(("w_so", wb_so_v[:, :, cb * 256:(cb + 1) * 256], 16, 256))
    for cb in range(2):
        MIDb.append(("w_o", wb_o_v[:, :, cb * 512:(cb + 1) * 512], 8, 512))
    for cb in range(8):
        ENDb.append(("w_up", wb_up_v[:, :, cb * 512:(cb + 1) * 512], 8, 512))
    for cb in range(4):
        for kh in range(2):
            ENDb.append(("w_down", wb_down_v[:, kh * 16:(kh + 1) * 16, cb * 256:(cb + 1) * 256], 16, 256))
    n_pass = n_ptiles + (1 if do_sample else 0)
    wseq = list(Zb)
    for n_ in range(n_pass):
        wseq += MIDb
        if n_ + 1 < n_pass:
            wseq += Zb
        wseq += ENDb
    wstate = {"issued": 0, "next": 0, "bufs": {}, "seen": set()}
    total_blocks = len(wseq)

    def w_issue(upto):
        while wstate["issued"] < min(upto, total_blocks):
            i = wstate["issued"]
            name, src, nk, ncol = wseq[i]
            if name not in wstate["seen"]:
                wstate["seen"].add(name)
                wait_cast("sp", name)
            sl = wslots.take()
            dst = sl[:, 0:nk * ncol].rearrange("p (a b) -> p a b", b=ncol)
            k.dma("sp", ld_ring, dst, src, writes=[sl])
            wstate["bufs"][i] = (sl, nk, ncol)
            wstate["issued"] += 1

    def wnext():
        i = wstate["next"]
        wstate["next"] += 1
        w_issue(i + NWSLOT)
        sl, nk, ncol = wstate["bufs"].pop(i)
        view = sl[:, 0:nk * ncol].rearrange("p (a b) -> p a b", b=ncol)
        return sl, view

    def mm_group(out_ap, pairs, reads, psbuf):
        n = len(pairs)
        for i, (l, r) in enumerate(pairs):
            fn = lambda eng, l=l, r=r, i=i: eng.matmul(out_ap, lhsT=l, rhs=r, start=(i == 0), stop=(i == n - 1))
            if i < n - 1:
                k.pe_quiet(fn, reads=reads, writes=[psbuf])
            else:
                k.op("pe", fn, reads=reads, writes=[psbuf])

    def rstd_from_ss(ss_ap, ss_buf, n_feat, width, nt):
        ln_ = small.take()
        k.op("act", lambda eng: eng.activation(out=ln_[0:nt, 0:width], in_=ss_ap, func=AF.Ln, bias=EPS, scale=1.0 / n_feat),
             reads=[ss_buf], writes=[ln_])
        r_ = small.take()
        k.op("act", lambda eng: eng.activation(out=r_[0:nt, 0:width], in_=ln_[0:nt, 0:width], func=AF.Exp, scale=-0.5),
             reads=[ln_], writes=[r_])
        return r_

    def norm_phase1(xbs_, subs_):
        n = len(subs_)
        sss, rs, xns = [], [], []
        for j in range(n):
            nt = subs_[j]
            ss = small.take()
            junk_act = junk_r.take()
            k.op("act", lambda eng: eng.activation(out=junk_act[0:nt, :], in_=xbs_[j][0:nt, :], func=AF.Square, accum_out=ss[0:nt, 0:1]),
                 reads=[xbs_[j]], writes=[junk_act, ss])
            sss.append(ss)
        for j in range(n):
            nt = subs_[j]
            rs.append(rstd_from_ss(sss[j][0:nt, 0:1], sss[j], D, 1, nt))
        for j in range(n):
            nt = subs_[j]
            xn = xnbuf.take()
            k.op("dve", lambda eng: eng.tensor_scalar(out=xn[0:nt, :], in0=xbs_[j][0:nt, :], scalar1=rs[j][0:nt, 0:1], scalar2=None, op0=ALU.mult),
                 reads=[xbs_[j], rs[j]], writes=[xn])
            xns.append(xn)
        return xns

    def norm_phase2(xns, subs_, offs_, Acol, Bcol, s, dstT):
        n = len(subs_)
        pbs_ = []
        for j in range(n):
            nt = subs_[j]
            pb = psf.take()
            pv = psb_view(pb)
            for kc in range(8):
                k.op("pe", lambda eng: eng.transpose(out=pv[:, kc * 128:kc * 128 + nt], in_=xns[j][0:nt, kc * 128:(kc + 1) * 128],
                                                     identity=ident_b[0:nt, 0:nt]), reads=[xns[j], ident_b], writes=[pb])
            pbs_.append(pb)
        for j in range(n):
            nt = subs_[j]
            pv = psb_view(pbs_[j])
            c0 = offs_[j]
            for kc in range(8):
                k.op("dve", lambda eng: eng.tensor_scalar(out=dstT[kc][:, c0:c0 + nt], in0=pv[:, kc * 128:kc * 128 + nt],
                                                          scalar1=Acol[:, kc, s:s + 1], scalar2=Bcol[:, kc, s:s + 1],
                                                          op0=ALU.mult, op1=ALU.add),
                     reads=[pbs_[j], Acol, Bcol], writes=[dstT[kc]])

    def build_dg_block(cb):
        if cb in dg_ready:
            return
        dgs = []
        for cc in range(4):
            c = cb * 4 + cc
            dg = dgring.take()
            k.op("dve", lambda eng: eng.tensor_tensor(out=dg[:, :, :], in0=ident_b[:, :].unsqueeze(1).broadcast_to([128, 4, 128]),
                                                      in1=colp[:, C_CW + c * 4:C_CW + c * 4 + 4].unsqueeze(2).broadcast_to([128, 4, 128]),
                                                      op=ALU.mult), reads=[ident_b, colp], writes=[dg])
            dgs.append(dg)
        dg_ready[cb] = dgs

    def prep_load(sp):
        subs = sp["subs"]
        offs = [sum(subs[:j]) for j in range(len(subs))]
        sp["xbs"] = []
        for j, nt in enumerate(subs):
            xb = xbuf.take()
            k.dma("sp", ld_ring, xb[0:nt, :], sp["x_rows"][offs[j]:offs[j] + nt, :], writes=[xb])
            sp["xbs"].append(xb)

    def prep_ew(sp):
        sp["xns"] = norm_phase1(sp["xbs"], sp["subs"])

    def prep_pe(sp):
        subs = sp["subs"]
        offs = [sum(subs[:j]) for j in range(len(subs))]
        sp["uT"] = uT_sets.take()
        norm_phase2(sp["xns"], subs, offs, A1, B1, sp["s"], sp["uT"])

    def zproj(sp):
        subs = sp["subs"]
        offs = [sum(subs[:j]) for j in range(len(subs))]
        uT = sp["uT"]
        for cb in range(4):
            sl, W = wnext()
            for j, nt in enumerate(subs):
                ps = psf.take()
                mm_group(ps[0:nt, 0:512], [(uT[kc][:, offs[j]:offs[j] + nt], W[:, kc, :]) for kc in range(8)], [sl] + uT, ps)
                k.op("act", lambda eng, ps=ps, j=j, nt=nt, cb=cb: eng.activation(out=zs_tok[j][0:nt, cb * 512:(cb + 1) * 512], in_=ps[0:nt, 0:512],
                                                                                  func=AF.Silu), reads=[ps], writes=[zs_tok[j]])
        sp["ready"] = True

    def run_tile(sp, nxt_sp):
        s, y_rows, subs, first, last, pos0_is_zero = sp["s"], sp["y_rows"], sp["subs"], sp["first"], sp["last"], sp["pos0"]
        nsub = len(subs)
        TT = sum(subs)
        offs = [sum(subs[:j]) for j in range(nsub)]
        if not sp.get("ready"):
            prep_load(sp)
            prep_ew(sp)
            prep_pe(sp)
            zproj(sp)
        if nxt_sp is not None:
            prep_load(nxt_sp)
        xbs = sp["xbs"]
        uT = sp["uT"]

        BT = [chunk.take() for _ in range(NG)]
        CT = [chunk.take() for _ in range(NG)]
        def emit_transposes(cb_, srcs_):
            for j, nt in enumerate(subs):
                pb = psf.take()
                pv = psb_view(pb)
                for cc in range(4):
                    k.op("pe", lambda eng: eng.transpose(out=pv[0:nt, cc * 128:(cc + 1) * 128], in_=srcs_[cc][:, offs[j]:offs[j] + nt],
                                                         identity=ident_b[:, :]), reads=[srcs_[cc], ident_b], writes=[pb])
                dst = xs_tok[j][0:nt, cb_ * 512:(cb_ + 1) * 512] if cb_ < 4 else B_tok[j][0:nt, (cb_ - 4) * 512:(cb_ - 3) * 512]
                dbuf = xs_tok[j] if cb_ < 4 else B_tok[j]
                k.op("dve", lambda eng: eng.tensor_copy(out=dst, in_=pv[0:nt, 0:512]), reads=[pb], writes=[dbuf])

        def xbc_proj(cb):
            sl, W = wnext()
            pbs = []
            for cc in range(4):
                c = cb * 4 + cc
                ps = psf.take()
                mm_group(ps[:, 0:TT], [(W[:, kc, cc * 128:(cc + 1) * 128], uT[kc][:, 0:TT]) for kc in range(8)], [sl] + uT, ps)
                Pb = pbring.take()
                k.op("pool", lambda eng: eng.tensor_copy(out=Pb[:, 0:3], in_=hist[:, c, :]), reads=[hist], writes=[Pb])
                k.op("act", lambda eng: eng.activation(out=Pb[:, 3:3 + TT], in_=ps[:, 0:TT], func=AF.Copy),
                     reads=[ps], writes=[Pb])
                k.op("pool", lambda eng: eng.tensor_copy(out=hist[:, c, :], in_=Pb[:, TT:TT + 3]), reads=[Pb], writes=[hist])
                if last:
                    k.op("dve", lambda eng: eng.tensor_copy(out=convout[:, c, :], in_=ps[:, TT - 3:TT]),
                         reads=[ps], writes=[convout])
                pbs.append(Pb)
            return pbs

        def xbc_conv(cb, pbs):
            dgs = dg_ready.pop(cb)
            srcs = []
            for cc in range(4):
                c = cb * 4 + cc
                if c < 16:
                    dstb = xf_r.take()
                elif c < 24:
                    dstb = BT[c - 16]
                else:
                    dstb = CT[c - 24]
                ps3 = psf.take()
                mm_group(ps3[:, 0:TT], [(dgs[cc][:, kk, :], pbs[cc][:, kk:kk + TT]) for kk in range(4)], [pbs[cc], dgs[cc]], ps3)
                k.op("act", lambda eng: eng.activation(out=dstb[:, 0:TT], in_=ps3[:, 0:TT], func=AF.Silu,
                                                       bias=colp[:, C_CB + c:C_CB + c + 1]),
                     reads=[ps3, colp], writes=[dstb])
                srcs.append(dstb)
            return srcs

        pend_pb = {}
        pend_src = {}
        for it_ in range(8 + 2):
            if it_ < 8:
                build_dg_block(it_)
                pend_pb[it_] = xbc_proj(it_)
            if 0 <= it_ - 1 < 8:
                pend_src[it_ - 1] = xbc_conv(it_ - 1, pend_pb.pop(it_ - 1))
            if 0 <= it_ - 2 < 6:
                emit_transposes(it_ - 2, pend_src.pop(it_ - 2))
        sl, W = wnext()
        for j, nt in enumerate(subs):
            ps = psf.take()
            mm_group(ps[0:nt, 0:32], [(uT[kc][:, offs[j]:offs[j] + nt], W[:, kc, :]) for kc in range(8)], [sl] + uT, ps)
            t1 = small.take()
            k.op("dve", lambda eng, ps=ps, t1=t1, nt=nt: eng.tensor_tensor(out=t1[0:nt, :], in0=ps[0:nt, 0:32], in1=rhead[0:nt, 0:32], op=ALU.add),
                 reads=[ps, rhead], writes=[t1])
            t2 = small.take()
            k.op("act", lambda eng, t1=t1, t2=t2, nt=nt: eng.activation(out=t2[0:nt, :], in_=t1[0:nt, :], func=AF.Exp), reads=[t1], writes=[t2])
            k.op("act", lambda eng, t2=t2, j=j, nt=nt: eng.activation(out=dtbuf[j][0:nt, :], in_=t2[0:nt, :], func=AF.Ln, bias=1.0),
                 reads=[t2], writes=[dtbuf[j]])
        pmT = [chunk.take() for _ in range(8)]
        for cb in range(2):
            sl, W = wnext()
            for cc in range(4):
                pc = cb * 4 + cc
                gi = pc // 2
                w = (2, 4, 8, 16)[gi]
                ps = psf.take()
                mm_group(ps[:, 0:TT], [(W[:, kc, cc * 128:(cc + 1) * 128], uT[kc][:, 0:TT]) for kc in range(8)], [sl] + uT, ps)
                ppb = pp[pc]
                k.op("act", lambda eng, ps=ps, ppb=ppb: eng.activation(out=ppb[:, 15:15 + TT], in_=ps[:, 0:TT], func=AF.Copy),
                     reads=[ps], writes=[ppb])
                cur = ppb
                step = 1
                lo = 0
                while step < w:
                    nxt = psc.take()
                    lo2 = lo + step
                    k.op("pool", lambda eng, cur=cur, nxt=nxt, lo2=lo2, step=step: eng.tensor_tensor(
                        out=nxt[:, lo2:15 + TT], in0=cur[:, lo2:15 + TT], in1=cur[:, lo2 - step:15 + TT - step], op=ALU.add),
                        reads=[cur], writes=[nxt])
                    cur = nxt
                    lo = lo2
                    step *= 2
                k.op("dve", lambda eng, cur=cur, ppb=ppb, pc=pc, w=w: eng.scalar_tensor_tensor(
                    out=pmT[pc][:, 0:TT], in0=cur[:, 15:15 + TT], scalar=1.0 / w, in1=ppb[:, 15:15 + TT], op0=ALU.mult, op1=ALU.subtract),
                    reads=[cur, ppb], writes=[pmT[pc]])
                if pos0_is_zero:
                    t_ = small.take()
                    k.op("dve", lambda eng, cur=cur, t_=t_, gi=gi: eng.tensor_tensor(out=t_[:, 0:16], in0=cur[:, 15:31], in1=icnt[:, gi, :], op=ALU.mult),
                         reads=[cur, icnt], writes=[t_])
                    k.op("dve", lambda eng, t_=t_, ppb=ppb, pc=pc: eng.tensor_tensor(out=pmT[pc][:, 0:16], in0=t_[:, 0:16], in1=ppb[:, 15:31], op=ALU.subtract),
                         reads=[t_, ppb], writes=[pmT[pc]])
                k.op("pool", lambda eng, ppb=ppb: eng.tensor_copy(out=ppb[:, 0:15], in_=ppb[:, TT:TT + 15]), reads=[ppb], writes=[ppb])
        gT = [chunk.take() for _ in range(16)]
        for cb in range(4):
            sl, W = wnext()
            for cc in range(4):
                gc = cb * 4 + cc
                ps = psf.take()
                mm_group(ps[:, 0:TT], [(W[:, kc, cc * 128:(cc + 1) * 128], uT[kc][:, 0:TT]) for kc in range(8)], [sl] + uT, ps)
                k.op("act", lambda eng, ps=ps, gc=gc: eng.activation(out=gT[gc][:, 0:TT], in_=ps[:, 0:TT], func=AF.Sigmoid),
                     reads=[ps], writes=[gT[gc]])

        chunks = []
        for j, nt in enumerate(subs):
            o = offs[j]
            dtj = dtbuf[j]
            da = small.take()
            k.op("dve", lambda eng: eng.tensor_tensor(out=da[0:nt, :], in0=dtj[0:nt, :], in1=arow[0:nt, :], op=ALU.mult),
                 reads=[dtj, arow], writes=[da])
            pss = psf.take()
            k.op("pe", lambda eng: eng.matmul(pss[0:nt, 0:32], lhsT=U_f[0:nt, 0:nt], rhs=da[0:nt, :], start=True, stop=True),
                 reads=[U_f, da], writes=[pss])
            k.op("pe", lambda eng: eng.matmul(pss[:, 32:64], lhsT=ones_f[0:nt, :], rhs=da[0:nt, :], start=True, stop=True),
                 reads=[ones_f, da], writes=[pss])
            eacs = small.take()
            k.op("act", lambda eng: eng.activation(out=eacs[0:nt, :], in_=pss[0:nt, 0:32], func=AF.Exp), reads=[pss], writes=[eacs])
            cdB = small.take()
            k.op("act", lambda eng: eng.activation(out=cdB[:, :], in_=pss[:, 32:64], func=AF.Exp), reads=[pss], writes=[cdB])
            acs = small.take()
            k.op("act", lambda eng: eng.activation(out=acs[0:nt, :], in_=pss[0:nt, 0:32], func=AF.Copy), reads=[pss], writes=[acs])
            dif = small.take()
            k.op("dve", lambda eng: eng.tensor_tensor(out=dif[0:nt, :], in0=pss[0:nt, 32:64], in1=acs[0:nt, :], op=ALU.subtract),
                 reads=[pss, acs], writes=[dif])
            dte = small.take()
            k.op("act", lambda eng: eng.activation(out=dte[0:nt, :], in_=dif[0:nt, :], func=AF.Exp), reads=[dif], writes=[dte])
            wts = small.take()
            k.op("dve", lambda eng: eng.tensor_tensor(out=wts[0:nt, :], in0=dte[0:nt, :], in1=dtj[0:nt, :], op=ALU.mult),
                 reads=[dte, dtj], writes=[wts])
            chunks.append(dict(nt=nt, o=o, dtj=dtj, da=da, eacs=eacs, cdB=cdB, wts=wts, yz=yz_r.take(), ssg=small.take(), j=j))

        items = [(j, g) for j in range(nsub) for g in range(NG)]
        ist = {}

        def S0(it):
            j, g = it
            c = chunks[j]
            nt = c["nt"]
            Ws = []
            for hh in range(4):
                h = g * 4 + hh
                Wh = W_r.take()
                if hh < 3:
                    k.op("act", lambda eng: eng.activation(out=Wh[0:nt, 0:nt], in_=L_f[0:nt, 0:nt], func=AF.Copy, scale=c["da"][0:nt, h:h + 1]),
                         reads=[L_f, c["da"]], writes=[Wh])
                else:
                    k.op("dve", lambda eng: eng.tensor_scalar(out=Wh[0:nt, 0:nt], in0=L_f[0:nt, 0:nt], scalar1=c["da"][0:nt, h:h + 1],
                                                              scalar2=None, op0=ALU.mult), reads=[L_f, c["da"]], writes=[Wh])
                Ws.append(Wh)
            xg = xg_r.take()
            xs_g = xs_tok[j][0:nt, g * 256:(g + 1) * 256].rearrange("p (h d) -> p h d", d=HD)
            for qi, src in enumerate((c["wts"], rhead, c["dtj"])):
                col0 = 64 if qi == 1 else 0
                k.op("pool",
                     lambda eng: eng.tensor_tensor(out=xg[0:nt, qi, :].rearrange("p (h d) -> p h d", d=HD), in0=xs_g,
                                                   in1=src[0:nt, col0 + g * 4:col0 + (g + 1) * 4].unsqueeze(2).broadcast_to([nt, 4, HD]),
                                                   op=ALU.mult), reads=[xs_tok[j], src], writes=[xg])
            gs_ = slice(g * 256, (g + 1) * 256)
            hres = hst_res[g]
            k.op("pool", lambda eng: eng.tensor_tensor(out=hst[:, gs_].rearrange("p (h d) -> p h d", d=HD),
                                                       in0=hst[:, gs_].rearrange("p (h d) -> p h d", d=HD),
                                                       in1=c["cdB"][:, g * 4:(g + 1) * 4].unsqueeze(2).broadcast_to([128, 4, HD]), op=ALU.mult),
                 reads=[hres, c["cdB"]], writes=[hres])
            ist[it] = dict(W=Ws, xg=xg)

        def S1a(it):
            j, g = it
            c = chunks[j]
            nt, o = c["nt"], c["o"]
            pcb = ps_cb.take()
            k.op("pe", lambda eng: eng.matmul(pcb[0:nt, 0:nt], lhsT=BT[g][:, o:o + nt], rhs=CT[g][:, o:o + nt], start=True, stop=True),
                 reads=[BT[g], CT[g]], writes=[pcb])
            cbm = cbm_r.take()
            k.op("dve", lambda eng: eng.tensor_tensor(out=cbm[0:nt, 0:nt], in0=pcb[0:nt, 0:nt], in1=U_f[0:nt, 0:nt], op=ALU.mult),
                 reads=[pcb, U_f], writes=[cbm])
            ist[it].update(cbm=cbm)

        def S1b(it):
            j, g = it
            c = chunks[j]
            nt = c["nt"]
            pseg = ps_seg.take()
            for hh in range(4):
                Wh = ist[it]["W"][hh]
                k.op("pe", lambda eng: eng.matmul(pseg[0:nt, hh * 128:hh * 128 + nt], lhsT=Wh[0:nt, 0:nt], rhs=U_f[0:nt, 0:nt],
                                                  start=True, stop=True), reads=[Wh, U_f], writes=[pseg])
            ist[it].update(pseg=pseg)

        def S2a(it):
            j, g = it
            nt = chunks[j]["nt"]
            pseg = ist[it]["pseg"]
            segv = pseg[0:nt, :].rearrange("p (h l) -> p h l", l=128)[:, :, 0:nt]
            k.op("act", lambda eng: eng.activation(out=segv, in_=segv, func=AF.Exp), reads=[pseg], writes=[pseg])

        def S2b(it):
            j, g = it
            nt = chunks[j]["nt"]
            pseg, cbm = ist[it]["pseg"], ist[it]["cbm"]
            segv = pseg[0:nt, :].rearrange("p (h l) -> p h l", l=128)[:, :, 0:nt]
            Mg = M_r.take()
            k.op("dve", lambda eng: eng.tensor_tensor(out=Mg[0:nt, :, 0:nt], in0=segv,
                                                      in1=cbm[0:nt, 0:nt].unsqueeze(1).broadcast_to([nt, 4, nt]), op=ALU.mult),
                 reads=[pseg, cbm], writes=[Mg])
            ist[it].update(Mg=Mg)

        def S3(it):
            j, g = it
            c = chunks[j]
            nt, o = c["nt"], c["o"]
            Mg, xg = ist[it]["Mg"], ist[it]["xg"]
            yz, ssg, eacs, cdB = c["yz"], c["ssg"], c["eacs"], c["cdB"]
            gs = slice(g * 256, (g + 1) * 256)
            py = ps_y.take()
            k.pe_quiet(lambda eng: eng.matmul(py[0:nt, 0:256], lhsT=ident_b[0:nt, 0:nt], rhs=xg[0:nt, 1, :], start=True, stop=False),
                       reads=[ident_b, xg], writes=[py])
            for hh in range(4):
                fn = lambda eng: eng.matmul(py[0:nt, hh * 64:(hh + 1) * 64], lhsT=Mg[0:nt, hh, 0:nt], rhs=xg[0:nt, 2, hh * 64:(hh + 1) * 64],
                                            start=False, stop=(hh == 3))
                if hh < 3:
                    k.pe_quiet(fn, reads=[Mg, xg], writes=[py])
                else:
                    k.op("pe", fn, reads=[Mg, xg], writes=[py])
            k.op("pe", lambda eng: eng.matmul(py[0:nt, 256:512], lhsT=CT[g][:, o:o + nt], rhs=hbf[g][:, :], start=True, stop=True),
                 reads=[CT[g], hbf[g]], writes=[py])
            pst = ps_st.take()
            k.op("pe", lambda eng: eng.matmul(pst[:, 0:256], lhsT=B_tok[j][0:nt, g * 128:(g + 1) * 128], rhs=xg[0:nt, 0, :], start=True, stop=True),
                 reads=[B_tok[j], xg], writes=[pst])
            t2 = tmp256.take()
            k.op("dve", lambda eng: eng.tensor_tensor(out=t2[0:nt, :].rearrange("p (h d) -> p h d", d=HD),
                                                      in0=py[0:nt, 256:512].rearrange("p (h d) -> p h d", d=HD),
                                                      in1=eacs[0:nt, g * 4:(g + 1) * 4].unsqueeze(2).broadcast_to([nt, 4, HD]), op=ALU.mult),
                 reads=[py, eacs], writes=[t2])
            ybf = ybf_r.take()
            k.op("dve", lambda eng: eng.tensor_tensor(out=ybf[0:nt, :], in0=py[0:nt, 0:256], in1=t2[0:nt, :], op=ALU.add),
                 reads=[py, t2], writes=[ybf])
            k.op("dve", lambda eng: eng.tensor_tensor(out=yz[0:nt, gs], in0=ybf[0:nt, :], in1=zs_tok[j][0:nt, gs], op=ALU.mult),
                 reads=[ybf, zs_tok[j]], writes=[yz.subs[g]])
            ist[it].update(pst=pst)

        def S4a(it):
            j, g = it
            pst = ist[it]["pst"]
            gs = slice(g * 256, (g + 1) * 256)
            hres = hst_res[g]
            k.op("dve", lambda eng: eng.tensor_tensor(out=hst[:, gs], in0=pst[:, 0:256], in1=hst[:, gs], op=ALU.add),
                 reads=[pst, hres], writes=[hres])

        def S4b(it):
            j, g = it
            c = chunks[j]
            nt = c["nt"]
            yz, ssg = c["yz"], c["ssg"]
            gs = slice(g * 256, (g + 1) * 256)
            hres = hst_res[g]
            junk_act = junk_r.take()
            k.op("act", lambda eng: eng.activation(out=junk_act[0:nt, 0:256], in_=yz[0:nt, gs], func=AF.Square, accum_out=ssg[0:nt, g:g + 1]),
                 reads=[yz.subs[g]], writes=[junk_act, ssg])
            k.op("act", lambda eng: eng.activation(out=hbf[g][:, :], in_=hst[:, gs], func=AF.Copy), reads=[hres], writes=[hbf[g]])
            del ist[it]
            if g == NG - 1:
                epilogue(c)

        def epilogue(c):
            nt, o, yz, ssg = c["nt"], c["o"], c["yz"], c["ssg"]
            rg = rstd_from_ss(ssg[0:nt, 0:8], ssg, 256, 8, nt)
            yg = yz
            for g in range(NG):
                gs = slice(g * 256, (g + 1) * 256)
                k.op("dve", lambda eng: eng.scalar_tensor_tensor(out=yg[0:nt, gs], in0=yz[0:nt, gs], scalar=rg[0:nt, g:g + 1],
                                                                 in1=gsn_bc[0:nt, gs], op0=ALU.mult, op1=ALU.mult),
                     reads=[yz.subs[g], rg, gsn_bc], writes=[yz.subs[g]])
            for half in range(2):
                pb = ps_y.take()
                pv = psb_view(pb)
                for q in range(8):
                    fc = half * 8 + q
                    k.op("pe", lambda eng: eng.transpose(out=pv[:, q * 128:q * 128 + nt], in_=yg[0:nt, fc * 128:(fc + 1) * 128],
                                                         identity=ident_b[0:nt, 0:nt]), reads=[yz.subs[fc // 2], ident_b], writes=[pb])
                k.op("act", lambda eng: eng.activation(
                    out=ygT_all[:, half * 8:(half + 1) * 8, o:o + nt],
                    in_=pv.rearrange("p (q t) -> p q t", t=128)[:, :, 0:nt], func=AF.Copy), reads=[pb], writes=[ygT_all])

        n_it = len(items)
        ps_cb = Ring(psf.bufs[0:1])
        ps_seg = Ring(psf.bufs[1:3])
        ps_y = Ring(psf.bufs[3:5])
        ps_st = Ring(psf.bufs[5:8])
        def at(fn, idx):
            if 0 <= idx < n_it:
                fn(items[idx])

        for i in range(n_it + 4):
            at(S2a, i - 2)
            at(S4a, i - 4)
            at(S0, i)
            at(S1a, i - 1)
            at(S2b, i - 2)
            at(S3, i - 3)
            at(S4b, i - 4)
            at(S1b, i - 1)

        if nxt_sp is not None:
            prep_ew(nxt_sp)
        mixT = [chunk.take() for _ in range(8)]
        for cb in range(4):
            sl, W = wnext()
            for dl in range(2):
                dc = cb * 2 + dl
                gi = dc // 2
                ps = psf.take()
                mm_group(ps[:, 0:TT], [(W[:, kc, dl * 128:(dl + 1) * 128], ygT_all[:, kc, 0:TT]) for kc in range(16)], [sl, ygT_all], ps)
                ps2 = psf.take()
                mm_group(ps2[:, 0:TT], [(wpool[:, gi * 2 + kc, (dc % 2) * 128:(dc % 2 + 1) * 128], pmT[gi * 2 + kc][:, 0:TT]) for kc in range(2)],
                         [wpool, pmT[gi * 2], pmT[gi * 2 + 1]], ps2)
                a_ = ab_r.take()
                k.op("dve", lambda eng, ps=ps, a_=a_, dc=dc: eng.tensor_tensor(out=a_[:, 0:TT], in0=ps[:, 0:TT], in1=gT[dc][:, 0:TT], op=ALU.mult),
                     reads=[ps, gT[dc]], writes=[a_])
                b_ = ab_r.take()
                k.op("dve", lambda eng, ps2=ps2, b_=b_, dc=dc: eng.scalar_tensor_tensor(
                    out=b_[:, 0:TT], in0=ps2[:, 0:TT], scalar=colp[:, C_PSC + dc:C_PSC + dc + 1], in1=gT[8 + dc][:, 0:TT],
                    op0=ALU.mult, op1=ALU.mult), reads=[ps2, colp, gT[8 + dc]], writes=[b_])
                k.op("pool", lambda eng, a_=a_, b_=b_, dc=dc: eng.tensor_tensor(out=mixT[dc][:, 0:TT], in0=a_[:, 0:TT], in1=b_[:, 0:TT], op=ALU.add),
                     reads=[a_, b_], writes=[mixT[dc]])
        pso = [[None, None] for _ in subs]
        for ob in range(2):
            sl, W = wnext()
            for j, nt in enumerate(subs):
                ps = psf.take()
                mm_group(ps[0:nt, 0:512], [(mixT[kc][:, offs[j]:offs[j] + nt], W[:, kc, :]) for kc in range(8)], [sl] + mixT, ps)
                pso[j][ob] = ps
        for j, nt in enumerate(subs):
            ssA = small.take()
            for ob in range(2):
                junk_act = junk_r.take()
                k.op("act", lambda eng, ob=ob: eng.activation(out=junk_act[0:nt, 0:512], in_=pso[j][ob][0:nt, 0:512], func=AF.Square,
                                                              accum_out=ssA[0:nt, ob:ob + 1]), reads=[pso[j][ob]], writes=[junk_act, ssA])
            ss = small.take()
            k.op("dve", lambda eng: eng.tensor_tensor(out=ss[0:nt, 0:1], in0=ssA[0:nt, 0:1], in1=ssA[0:nt, 1:2], op=ALU.add),
                 reads=[ssA], writes=[ss])
            r_ = rstd_from_ss(ss[0:nt, 0:1], ss, D, 1, nt)
            tmp = f32tmp.take()
            for ob in range(2):
                k.op("dve", lambda eng, ob=ob: eng.scalar_tensor_tensor(
                    out=tmp[0:nt, ob * 512:(ob + 1) * 512], in0=pso[j][ob][0:nt, 0:512], scalar=r_[0:nt, 0:1],
                    in1=G1[0:nt, ob * 512:(ob + 1) * 512], op0=ALU.mult, op1=ALU.mult),
                    reads=[pso[j][ob], r_, G1], writes=[tmp])
            k.op("pool", lambda eng: eng.tensor_tensor(out=xbs[j][0:nt, :], in0=xbs[j][0:nt, :], in1=tmp[0:nt, :], op=ALU.add),
                 reads=[xbs[j], tmp], writes=[xbs[j]])

        if nxt_sp is not None:
            prep_pe(nxt_sp)
        vxn = norm_phase1(xbs, subs)
        if nxt_sp is not None:
            zproj(nxt_sp)
        vT = [chunk.take() for _ in range(8)]
        norm_phase2(vxn, subs, offs, A2, B2, s, vT)
        hdnT = [chunk.take() for _ in range(32)]
        if nxt_sp is not None:
            build_dg_block(0)
            build_dg_block(1)
        for cb in range(8):
            sl, W = wnext()
            for cc in range(4):
                fc = cb * 4 + cc
                ps = psf.take()
                mm_group(ps[:, 0:TT], [(W[:, kc, cc * 128:(cc + 1) * 128], vT[kc][:, 0:TT]) for kc in range(8)], [sl] + vT, ps)
                rl = relu_r.take()
                k.op("act", lambda eng, ps=ps, rl=rl: eng.activation(out=rl[:, 0:TT], in_=ps[:, 0:TT], func=AF.Relu), reads=[ps], writes=[rl])
                k.op("pool", lambda eng, rl=rl, fc=fc: eng.tensor_tensor(out=hdnT[fc][:, 0:TT], in0=rl[:, 0:TT], in1=rl[:, 0:TT], op=ALU.mult),
                     reads=[rl], writes=[hdnT[fc]])
        dnbuf = [f32tmp.take() for _ in subs]
        for cb in range(4):
            psd = [psf.take() for _ in subs]
            for kh in range(2):
                sl, W = wnext()
                for j, nt in enumerate(subs):
                    for kk in range(16):
                        fc = kh * 16 + kk
                        fn = lambda eng, j=j, nt=nt, kk=kk, fc=fc: eng.matmul(psd[j][0:nt, 0:256], lhsT=hdnT[fc][:, offs[j]:offs[j] + nt], rhs=W[:, kk, :],
                                                                              start=(kh == 0 and kk == 0), stop=(kh == 1 and kk == 15))
                        if kk < 15:
                            k.pe_quiet(fn, reads=[sl, hdnT[fc]], writes=[psd[j]])
                        else:
                            k.op("pe", fn, reads=[sl, hdnT[fc]], writes=[psd[j]])
            for j, nt in enumerate(subs):
                k.op("act", lambda eng, j=j, nt=nt: eng.activation(out=dnbuf[j][0:nt, cb * 256:(cb + 1) * 256], in_=psd[j][0:nt, 0:256], func=AF.Copy),
                     reads=[psd[j]], writes=[dnbuf[j]])
        for j, nt in enumerate(subs):
            ss = small.take()
            junk_act = junk_r.take()
            k.op("act", lambda eng: eng.activation(out=junk_act[0:nt, :], in_=dnbuf[j][0:nt, :], func=AF.Square, accum_out=ss[0:nt, 0:1]),
                 reads=[dnbuf[j]], writes=[junk_act, ss])
            r_ = rstd_from_ss(ss[0:nt, 0:1], ss, D, 1, nt)
            k.op("dve", lambda eng: eng.scalar_tensor_tensor(out=dnbuf[j][0:nt, :], in0=dnbuf[j][0:nt, :], scalar=r_[0:nt, 0:1], in1=G2[0:nt, :],
                                                             op0=ALU.mult, op1=ALU.mult), reads=[dnbuf[j], r_, G2], writes=[dnbuf[j]])
            k.op("dve", lambda eng: eng.tensor_tensor(out=xbs[j][0:nt, :], in0=xbs[j][0:nt, :], in1=dnbuf[j][0:nt, :], op=ALU.add),
                 reads=[xbs[j], dnbuf[j]], writes=[xbs[j]])
            k.dma("sp", sst_ring, y_rows[offs[j]:offs[j] + nt, :], xbs[j][0:nt, :], reads=[xbs[j]])

    def init_state(zero, s):
        if zero:
            for g in range(NG):
                k.op("pool", lambda eng, g=g: eng.memset(hst[:, g * 256:(g + 1) * 256], 0.0), writes=[hst_res[g]])
                k.op("pool", lambda eng, g=g: eng.memset(hbf[g][:, :], 0.0), writes=[hbf[g]])
            k.op("pool", lambda eng: eng.memset(hist[:], 0.0), writes=[hist])
            for pc in range(8):
                k.op("pool", lambda eng, pc=pc: eng.memset(pp[pc][:, 0:15], 0.0), writes=[pp[pc]])
        else:
            k.dma("sp", ld_ring, hst[:], h0T_d, writes=hst_res)
            for g in range(NG):
                k.op("act", lambda eng, g=g: eng.activation(out=hbf[g][:, :], in_=hst[:, g * 256:(g + 1) * 256], func=AF.Copy),
                     reads=[hst_res[g]], writes=[hbf[g]])
            k.dma("sp", ld_ring, convout[:], convT_d, writes=[convout])
            k.op("dve", lambda eng: eng.tensor_copy(out=hist[:], in_=convout[:]), reads=[convout], writes=[hist])
            for pc in range(8):
                k.dma("sp", ld_ring, pp[pc][:, 0:15], poolT_d[:, pc, :], writes=[pp[pc]])

    def store_state(hT_out, conv_out_d, pool_out_d):
        k.dma("pool", st_ring, hT_out, hst[:], reads=hst_res)
        k.dma("pool", st_ring, conv_out_d, convout[:], reads=[convout])
        for pc in range(8):
            k.dma("pool", st_ring, pool_out_d[:, pc, :], pp[pc][:, 0:15], reads=[pp[pc]])

    psf.bufs.append(psx)
    specs = []
    for ti in range(n_ptiles):
        specs.append(dict(s=0, x_rows=xp_d[ti * TT_MAX:(ti + 1) * TT_MAX, :], y_rows=yp_d[ti * TT_MAX:(ti + 1) * TT_MAX, :],
                          subs=[TSUB] * NSUB, first=(ti == 0), last=(ti == n_ptiles - 1), pos0=(ti == 0)))
    if do_sample:
        specs.append(dict(s=1, x_rows=xs_d, y_rows=ys_d, subs=[DEC], first=True, last=True, pos0=False))
    init_state(True, 0)
    for ti in range(n_ptiles):
        run_tile(specs[ti], specs[ti + 1] if ti + 1 < len(specs) else None)
    store_state(hTp_d, convp_d, poolp_d)
    if do_sample:
        k.dma("sp", ld_ring, G1[:], gs_scr[0], reads=[gs_res], writes=[G1])
        k.dma("sp", ld_ring, G2[:], gs_scr[1], reads=[gs_res], writes=[G2])
        init_state(False, 1)
        run_tile(specs[-1], None)
        store_state(hTs_d, convs_d, pools_d)
    k.finish()
    es.close()
    return nc


def make_in_maps(inputs, n_ptiles=SEQ // TT_MAX):
    f = lambda a: np.ascontiguousarray(np.asarray(a, dtype=np.float32))
    ntok = n_ptiles * TT_MAX

    def col(v, n):
        return f(v).reshape(n, 128).T

    colp = np.zeros((128, NCOL), np.float32)
    colp[:, C_BADA:C_BADA + 48] = col(inputs["b_ada"][0], 48)
    colp[:, C_GPM:C_GPM + 8] = col(inputs["g_pre_mix"][0], 8)
    colp[:, C_GPL:C_GPL + 8] = col(inputs["g_pre_mlp"][0], 8)
    cw = f(inputs["conv_w"][0])
    colp[:, C_CW:C_CW + 128] = cw.reshape(4, 32, 128).transpose(2, 1, 0).reshape(128, 128)
    colp[:, C_CB:C_CB + 32] = col(inputs["conv_b"][0], 32)
    colp[:, C_GSN:C_GSN + 16] = col(inputs["g_ssd_norm"][0], 16)
    colp[:, C_PSC:C_PSC + 8] = col(inputs["pool_scale"][0], 8)
    b_ada = f(inputs["b_ada"][0])
    rbada = np.broadcast_to(np.concatenate([b_ada[2048:3072], b_ada[5120:6144]])[None, :], (128, 2048))
    rgpost = np.broadcast_to(np.concatenate([f(inputs["g_post_mix"][0]), f(inputs["g_post_mlp"][0])])[None, :], (128, 2048))
    rhead = np.broadcast_to(np.concatenate([f(inputs["dt_bias"][0]), f(inputs["a_log"][0]), f(inputs["d_skip"][0])])[None, :], (128, 96))
    rgsn = np.broadcast_to(f(inputs["g_ssd_norm"][0])[None, :], (128, DI))
    shared = dict(
        colp=f(colp), rbada=f(rbada), rgpost=f(rgpost), rhead=f(rhead), rgsn=f(rgsn),
        w_ada=f(inputs["w_ada"][0]), w_in=f(inputs["w_in"][0]), w_so=f(inputs["w_ssd_out"][0]),
        w_pool=f(inputs["w_pool_group"][0]).reshape(1024, 256), w_o=f(inputs["w_o"][0]),
        w_up=f(inputs["w_up"][0]), w_down=f(inputs["w_down"][0]),
    )
    maps = []
    for i in range(8):
        m = dict(shared)
        m["xp"] = f(inputs["x_prompt"][i][:ntok])
        m["xs"] = f(inputs["x_sample"][i])
        cc = np.stack([col(inputs["c_prompt"][i], 8), col(inputs["c_sample"][i], 8)], axis=-1)
        m["ccol"] = f(cc)
        m["h0T"] = f(np.asarray(inputs["state_ssm"][0, i]).reshape(NH * HD, NS).T)
        m["convT"] = f(np.asarray(inputs["state_conv"][0, i]).reshape(3, 32, 128).transpose(2, 1, 0))
        m["poolT"] = f(np.asarray(inputs["state_pool"][0, i]).reshape(15, 8, 128).transpose(2, 1, 0))
        maps.append(m)
    return maps


def gather(results, n_ptiles=SEQ // TT_MAX):
    ntok = n_ptiles * TT_MAX
    yp = np.stack([r["yp"] for r in results]).reshape(8, ntok, D)
    ys = np.stack([r["ys"] for r in results]).reshape(8, DEC, D)

    def ssm(key):
        return np.stack([np.asarray(r[key]).reshape(128, NH, HD).transpose(1, 2, 0) for r in results])[None]

    def conv(key):
        return np.stack([np.asarray(r[key]).reshape(128, 32, 3).transpose(2, 1, 0).reshape(3, CONVD) for r in results])[None]

    def pool(key):
        return np.stack([np.asarray(r[key]).reshape(128, 8, 15).transpose(2, 1, 0).reshape(15, D) for r in results])[None]

    outs = (yp, ys, ssm("hTp"), conv("convp"), pool("poolp"), ssm("hTs"), conv("convs"), pool("pools"))
    return tuple(np.ascontiguousarray(o, dtype=np.float32) for o in outs)


_NC_CACHE = {}


def kernel(**inputs):
    if "nc" not in _NC_CACHE:
        _NC_CACHE["nc"] = build_nc()
    nc = _NC_CACHE["nc"]
    in_maps = make_in_maps(inputs)
    res = run_bass_kernel_spmd(nc, in_maps, core_ids=list(range(8)))
    return gather(res.results)
```

```python
import numpy as np
from contextlib import ExitStack
import concourse.bass as bass
import concourse.mybir as mybir
from concourse.bass_utils import run_bass_kernel_spmd

F32 = mybir.dt.float32
BF16 = mybir.dt.bfloat16
AF = mybir.ActivationFunctionType
ALU = mybir.AluOpType

D = 1024
SEQ = 4096
DEC = 16
DI = 2048
NH = 32
HD = 64
NG = 8
NS = 128
CONVD = 4096
DPROJ = 9248
DFF = 4096
EPS = 1e-6
PAST = 2048
TSUB = 128
NSUB = 2
TT_MAX = TSUB * NSUB
WSLOT_ELEMS = 4096
NWSLOT = 3

C_BADA, C_GPM, C_GPL, C_CW, C_CB, C_GSN, C_PSC, NCOL = 0, 48, 56, 64, 192, 224, 240, 248


class Res:
    __slots__ = ("w", "r")

    def __init__(self):
        self.w = None
        self.r = {}


class Sig:
    def __init__(self, sem, name):
        self.sem = sem
        self.cnt = 0
        self.name = name


class Buf:
    def __init__(self, t, res=None, psum=False):
        self.t = t
        self.res = res if res is not None else Res()
        self.psum = psum

    def __getitem__(self, idx):
        return self.t[idx]


class Ring:
    def __init__(self, bufs):
        self.bufs = bufs
        self.i = 0

    def take(self):
        b = self.bufs[self.i % len(self.bufs)]
        self.i += 1
        return b


class K:
    def __init__(self, nc, es):
        self.nc = nc
        self.es = es
        self.eng = {"pe": nc.tensor, "act": nc.scalar, "dve": nc.vector, "pool": nc.gpsimd, "sp": nc.sync}
        self.sig = {}
        for e in self.eng:
            self.sig[e] = Sig(es.enter_context(nc.semaphore("s_" + e)), e)
        self.known = {e: {} for e in self.eng}
        self.nbuf = 0
        self.dsigs = {}
        self.tag = None

    def sb(self, shape, dt, name=None):
        self.nbuf += 1
        t = self.es.enter_context(self.nc.sbuf_tensor("s_" + (name or f"sb{self.nbuf}"), list(shape), dt))
        return Buf(t)

    def ps(self, shape, dt, name=None):
        self.nbuf += 1
        t = self.es.enter_context(self.nc.psum_tensor(name or f"ps{self.nbuf}", list(shape), dt))
        return Buf(t, psum=True)

    def ring(self, n, shape, dt, name):
        return Ring([self.sb(shape, dt, f"{name}{i}") for i in range(n)])

    def dsig_ring(self, n, name):
        sigs = [Sig(self.es.enter_context(self.nc.semaphore(f"d_{name}{i}")), f"{name}{i}") for i in range(n)]
        self.dsigs[name] = sigs
        return Ring(sigs)

    def _waits(self, e, reads, writes):
        needs = {}

        def add(sigv, same_ok):
            s, v = sigv
            if s is self.sig[e] and not same_ok and e == "pe":
                return
            if needs.get(s, 0) < v:
                needs[s] = v

        for b in reads:
            if b.res.w is not None:
                add(b.res.w, True)
            if b.psum:
                for s, v in b.res.r.items():
                    if s is not self.sig[e]:
                        add((s, v), True)
        for b in writes:
            if b.res.w is not None:
                add(b.res.w, False)
            for s, v in b.res.r.items():
                add((s, v), False)
        kn = self.known[e]
        eng = self.eng[e]
        for s, v in needs.items():
            if kn.get(s, 0) >= v:
                continue
            eng.wait_ge(s.sem, v)
            kn[s] = v

    def op(self, e, fn, reads=(), writes=()):
        reads = [b for b in reads if b is not None]
        writes = [b for b in writes if b is not None]
        self._waits(e, reads, writes)
        inst = fn(self.eng[e])
        sg = self.sig[e]
        sg.cnt += 1
        inst.then_inc(sg.sem, 1)
        for b in reads:
            b.res.r[sg] = sg.cnt
        for b in writes:
            b.res.w = (sg, sg.cnt)
            b.res.r = {}
        return inst

    def pe_quiet(self, fn, reads=(), writes=()):
        reads = [b for b in reads if b is not None]
        writes = [b for b in writes if b is not None]
        self._waits("pe", reads, writes)
        fn(self.eng["pe"])
        sg = self.sig["pe"]
        nxt = sg.cnt + 1
        for b in reads:
            b.res.r[sg] = nxt
        for b in writes:
            b.res.w = (sg, nxt)
            b.res.r = {}

    def dma(self, q, ring, out, in_, reads=(), writes=()):
        reads = [b for b in reads if b is not None]
        writes = [b for b in writes if b is not None]
        self._waits(q, reads, writes)
        sg = ring.take()
        kn = self.known[q]
        if kn.get(sg, 0) < sg.cnt:
            self.eng[q].wait_ge(sg.sem, sg.cnt)
            kn[sg] = sg.cnt
        inst = self.eng[q].dma_start(out=out, in_=in_)
        sg.cnt += 16
        inst.then_inc(sg.sem, 16)
        for b in reads:
            b.res.r[sg] = sg.cnt
        for b in writes:
            b.res.w = (sg, sg.cnt)
            b.res.r = {}

    def finish(self):
        sp = self.eng["sp"]
        for sigs in self.dsigs.values():
            for sg in sigs:
                if sg.cnt > 0:
                    sp.wait_ge(sg.sem, sg.cnt)
        for e, sg in self.sig.items():
            if e != "sp" and sg.cnt > 0:
                sp.wait_ge(sg.sem, sg.cnt)


def build_nc(n_ptiles=SEQ // TT_MAX, do_sample=True):
    nc = bass.Bass("TRN2", target_bir_lowering=False)
    es = ExitStack()
    k = K(nc, es)
    NP_TOK = n_ptiles * TT_MAX

    def din(name, shape, dt=F32):
        return nc.dram_tensor(name, list(shape), dt, kind="ExternalInput").ap()

    def dout(name, shape, dt=F32):
        return nc.dram_tensor(name, list(shape), dt, kind="ExternalOutput").ap()

    def dscr(name, shape, dt=BF16):
        return nc.dram_tensor(name, list(shape), dt, kind="Internal").ap()

    xp_d = din("xp", [NP_TOK, D])
    xs_d = din("xs", [DEC, D])
    ccol_d = din("ccol", [128, 8, 2])
    colp_d = din("colp", [128, NCOL])
    rbada_d = din("rbada", [128, 2048])
    rgpost_d = din("rgpost", [128, 2048])
    rhead_d = din("rhead", [128, 96])
    rgsn_d = din("rgsn", [128, DI])
    h0T_d = din("h0T", [128, DI])
    convT_d = din("convT", [128, 32, 3])
    poolT_d = din("poolT", [128, 8, 15])
    w_ada_d = din("w_ada", [D, 6 * D])
    w_in_d = din("w_in", [D, DPROJ])
    w_so_d = din("w_so", [DI, D])
    w_pool_d = din("w_pool", [D, 256])
    w_o_d = din("w_o", [D, D])
    w_up_d = din("w_up", [D, DFF])
    w_down_d = din("w_down", [DFF, D])

    yp_d = dout("yp", [NP_TOK, D])
    ys_d = dout("ys", [DEC, D])
    hTp_d = dout("hTp", [128, DI])
    hTs_d = dout("hTs", [128, DI])
    convp_d = dout("convp", [128, 32, 3])
    convs_d = dout("convs", [128, 32, 3])
    poolp_d = dout("poolp", [128, 8, 15])
    pools_d = dout("pools", [128, 8, 15])

    wb_ada = dscr("wb_ada", [D, 6 * D])
    wb_in = dscr("wb_in", [D, DPROJ])
    wb_so = dscr("wb_so", [DI, D])
    wb_pool = dscr("wb_pool", [D, 256])
    wb_o = dscr("wb_o", [D, D])
    wb_up = dscr("wb_up", [D, DFF])
    wb_down = dscr("wb_down", [DFF, D])

    ld_ring = k.dsig_ring(12, "ld")
    st_ring = k.dsig_ring(8, "st")
    cast_ring = k.dsig_ring(20, "cast")
    sst_ring = k.dsig_ring(6, "sst")

    def cast(name, src, dst, rows, cols, inner, rstep):
        for r0 in range(0, rows, rstep):
            s_ = src[r0:r0 + rstep, :].rearrange("r (a b) -> r a b", b=inner)
            d_ = dst[r0:r0 + rstep, :].rearrange("r (a b) -> r a b", b=inner)
            sg = cast_ring.take()
            eng = k.eng["pool"]
            kn = k.known["pool"]
            if kn.get(sg, 0) < sg.cnt:
                eng.wait_ge(sg.sem, sg.cnt)
                kn[sg] = sg.cnt
            inst = eng.dma_start(out=d_, in_=s_)
            sg.cnt += 16
            inst.then_inc(sg.sem, 16)
            cast_parts.setdefault(name, []).append((sg, sg.cnt))

    cast_parts = {}

    ident_b = k.sb([128, 128], BF16, "ident_b")
    ones_f = k.sb([128, 128], F32, "ones_f")
    ones_b = k.sb([128, 128], BF16, "ones_b")
    U_f = k.sb([128, 128], F32, "U_f")
    L_f = k.sb([128, 128], F32, "L_f")
    colp = k.sb([128, NCOL], F32, "colp")
    rhead = k.sb([128, 96], F32, "rhead")
    arow = k.sb([128, 32], F32, "arow")
    icnt = k.sb([128, 4, 16], F32, "icnt")
    wpool = k.sb([128, 8, 256], BF16, "wpool")
    ccol = k.sb([128, 8, 2], F32, "ccol")
    scol = k.sb([128, 8, 2], F32, "scol")
    modc = k.sb([128, 32, 2], F32, "modc")
    A1 = k.sb([128, 8, 2], F32, "A1")
    B1 = k.sb([128, 8, 2], F32, "B1")
    A2 = k.sb([128, 8, 2], F32, "A2")
    B2 = k.sb([128, 8, 2], F32, "B2")
    G1 = k.sb([128, D], F32, "G1")
    G2 = k.sb([128, D], F32, "G2")
    gs_scr = nc.dram_tensor("gs_scr", [2, 128, D], F32, kind="Internal").ap()
    gs_res = Buf(None)

    psf = Ring([k.ps([128, 512], F32, f"psf{i}") for i in range(7)])
    psx = k.ps([128, 512], F32, "psx")

    def psb_view(b):
        return b.t[:].bitcast(BF16)

    wslots = Ring([k.sb([128, WSLOT_ELEMS], BF16, f"wsl{i}") for i in range(NWSLOT)])
    xbuf = Ring([k.sb([128, D], F32, f"xb{i}") for i in range(2 * NSUB)])
    f32tmp = Ring([k.sb([128, D], F32, f"ft{i}") for i in range(2)])
    xnbuf = Ring([k.sb([128, D], BF16, f"xn{i}") for i in range(2)])
    junk_r = Ring([k.sb([128, D], BF16, f"junk{i}") for i in range(1)])
    chunk = Ring([k.sb([128, TT_MAX], BF16, f"ch{i}") for i in range(40)])
    uT_sets = Ring([[k.sb([128, TT_MAX], BF16, f"uT{a}_{i}") for i in range(8)] for a in range(2)])
    xs_tok = [k.sb([128, DI], BF16, f"xstok{i}") for i in range(NSUB)]
    B_tok = [k.sb([128, NG * NS], BF16, f"btok{i}") for i in range(NSUB)]
    zs_tok = [k.sb([128, DI], BF16, f"zstok{i}") for i in range(NSUB)]
    pbring = Ring([k.sb([128, TT_MAX + 3], BF16, f"pb{i}") for i in range(8)])
    xf_r = Ring([k.sb([128, TT_MAX], BF16, f"xf{i}") for i in range(8)])
    dgring = Ring([k.sb([128, 4, 128], BF16, f"dg{i}") for i in range(8)])
    dg_ready = {}
    hist = k.sb([128, 32, 3], BF16, "hist")
    convout = k.sb([128, 32, 3], F32, "convout")
    pp = [k.sb([128, 15 + TT_MAX], F32, f"pp{i}") for i in range(8)]
    psc = Ring([k.sb([128, 15 + TT_MAX], F32, f"psc{i}") for i in range(3)])
    small = Ring([k.sb([128, 32], F32, f"sm{i}") for i in range(32)])
    dtbuf = [k.sb([128, 32], F32, f"dt{i}") for i in range(NSUB)]
    cbm_r = Ring([k.sb([128, 128], F32, f"cbm{i}") for i in range(3)])
    W_r = Ring([k.sb([128, 128], F32, f"Wh{i}") for i in range(12)])
    M_r = Ring([k.sb([128, 4, 128], BF16, f"Mg{i}") for i in range(3)])
    xg_r = Ring([k.sb([128, 3, 256], BF16, f"xg{i}") for i in range(4)])
    ygT_all = k.sb([128, 16, TT_MAX], BF16, "ygT_all")
    gsn_bc = k.sb([128, DI], BF16, "gsn_bc")
    tmp256 = Ring([k.sb([128, 256], F32, f"t256_{i}") for i in range(2)])
    ybf_r = Ring([k.sb([128, 256], BF16, f"ybf{i}") for i in range(3)])
    yz_r = Ring([k.sb([128, DI], BF16, f"yz{i}") for i in range(2)])
    for b_ in yz_r.bufs:
        b_.subs = [Buf(b_.t) for _ in range(NG)]
    hst = k.sb([128, DI], F32, "hst")
    hbf = [k.sb([128, 256], BF16, f"hbf{g}") for g in range(NG)]
    ab_r = Ring([k.sb([128, TT_MAX], F32, f"ab{i}") for i in range(2)])
    relu_r = Ring([k.sb([128, TT_MAX], BF16, f"rl{i}") for i in range(3)])

    hst_res = [Buf(None) for _ in range(NG)]

    def memset(e, buf, val):
        k.op(e, lambda eng: eng.memset(buf[:], val), writes=[buf])

    memset("pool", ones_f, 1.0)
    memset("pool", ones_b, 1.0)
    k.op("pool", lambda eng: eng.affine_select(out=ident_b[:], in_=ones_b[:], pattern=[[-1, 128]], compare_op=ALU.is_equal,
                                               fill=0.0, base=0, channel_multiplier=1), reads=[ones_b], writes=[ident_b])
    k.op("pool", lambda eng: eng.affine_select(out=U_f[:], in_=ones_f[:], pattern=[[1, 128]], compare_op=ALU.is_ge,
                                               fill=0.0, base=0, channel_multiplier=-1), reads=[ones_f], writes=[U_f])
    k.op("pool", lambda eng: eng.affine_select(out=L_f[:], in_=ones_f[:], pattern=[[-1, 128]], compare_op=ALU.is_gt,
                                               fill=0.0, base=0, channel_multiplier=1), reads=[ones_f], writes=[L_f])

    cast("w_ada", w_ada_d, wb_ada, D, 6 * D, 2048, 256)
    cast("w_in", w_in_d, wb_in, D, DPROJ, 1156, 256)
    cast("w_pool", w_pool_d, wb_pool, D, 256, 256, 1024)
    cast("w_so", w_so_d, wb_so, DI, D, 1024, 1024)
    cast("w_o", w_o_d, wb_o, D, D, 1024, 1024)
    cast("w_up", w_up_d, wb_up, D, DFF, 2048, 512)
    cast("w_down", w_down_d, wb_down, DFF, D, 1024, 1024)

    def wait_cast(q, name):
        kn = k.known[q]
        for sg, v in cast_parts[name]:
            if kn.get(sg, 0) < v:
                k.eng[q].wait_ge(sg.sem, v)
                kn[sg] = v

    k.dma("sp", ld_ring, colp[:], colp_d, writes=[colp])
    k.dma("sp", ld_ring, rhead[:], rhead_d, writes=[rhead])
    k.dma("sp", ld_ring, ccol[:], ccol_d, writes=[ccol])
    k.dma("pool", st_ring, gsn_bc[:], rgsn_d, writes=[gsn_bc])
    k.op("act", lambda eng: eng.activation(out=arow[:], in_=rhead[:, 32:64], func=AF.Exp), reads=[rhead], writes=[arow])
    k.op("dve", lambda eng: eng.tensor_scalar(out=arow[:], in0=arow[:], scalar1=-1.0, scalar2=None, op0=ALU.mult),
         reads=[arow], writes=[arow])
    iot = small.take()
    k.op("pool", lambda eng: eng.iota(iot[:, 0:16], pattern=[[1, 16]], base=1, channel_multiplier=0,
                                      allow_small_or_imprecise_dtypes=True), writes=[iot])
    for gi, w in enumerate((2, 4, 8, 16)):
        t_ = small.take()
        k.op("dve", lambda eng, t_=t_, w=w: eng.tensor_scalar(out=t_[:, 0:16], in0=iot[:, 0:16], scalar1=float(w), scalar2=None,
                                                              op0=ALU.min), reads=[iot], writes=[t_])
        k.op("dve", lambda eng, t_=t_, gi=gi: eng.reciprocal(out=icnt[:, gi, :], in_=t_[:, 0:16]), reads=[t_], writes=[icnt])

    k.op("act", lambda eng: eng.activation(out=scol[:], in_=ccol[:], func=AF.Silu), reads=[ccol], writes=[scol])
    scol_b = k.sb([128, 8, 2], BF16, "scol_b")
    k.op("dve", lambda eng: eng.tensor_copy(out=scol_b[:], in_=scol[:]), reads=[scol], writes=[scol_b])
    screp = []
    for s in range(2):
        v = xs_tok[s].t[:, 0:1024].rearrange("p (a b) -> p a b", b=128)
        screp.append(v)
        k.op("dve", lambda eng, s=s, v=v: eng.tensor_copy(out=v, in_=scol[:, :, s:s + 1].broadcast_to([128, 8, 128])),
             reads=[scol], writes=[xs_tok[s]])
    ps_col = psx
    col_chunks = list(range(0, 16)) + list(range(24, 40))
    colidx = {ch: mi for mi, ch in enumerate(col_chunks)}
    wb_ada_v = wb_ada.rearrange("(kc p) c -> p kc c", p=128)
    gload = {0: (f32tmp.bufs[0], f32tmp.bufs[1]), 1: (xbuf.bufs[0], xbuf.bufs[1])}
    for which in range(2):
        rbt, rgt = gload[which]
        k.dma("sp", ld_ring, rbt[:], rbada_d[:, which * D:(which + 1) * D], writes=[rbt])
        k.dma("sp", ld_ring, rgt[:], rgpost_d[:, which * D:(which + 1) * D], writes=[rgt])
    wait_cast("sp", "w_ada")
    for blk in range(12):
        wsl = wslots.take()
        wv = wsl.t[:, 0:4096].rearrange("p (a b) -> p a b", b=512)
        k.dma("sp", ld_ring, wv, wb_ada_v[:, :, blk * 512:(blk + 1) * 512], writes=[wsl])
        if blk * 4 in colidx:
            for cc in range(4):
                mi = colidx[blk * 4 + cc]
                for kc in range(8):
                    fn = lambda eng, cc=cc, mi=mi, kc=kc: eng.matmul(ps_col[:, mi * 2:mi * 2 + 2], lhsT=wv[:, kc, cc * 128:(cc + 1) * 128],
                                                                     rhs=scol_b[:, kc, :], start=(kc == 0), stop=(kc == 7))
                    if kc < 7:
                        k.pe_quiet(fn, reads=[wsl, scol_b], writes=[ps_col])
                    else:
                        k.op("pe", fn, reads=[wsl, scol_b], writes=[ps_col])
        else:
            which, q = (0, blk - 4) if blk < 6 else (1, blk - 10)
            rbt, rgt = gload[which]
            for s in range(2):
                psr = psf.take()
                for kc in range(8):
                    fn = lambda eng, s=s, kc=kc, psr=psr: eng.matmul(psr[:, 0:512], lhsT=screp[s][:, kc, :], rhs=wv[:, kc, :],
                                                                     start=(kc == 0), stop=(kc == 7))
                    if kc < 7:
                        k.pe_quiet(fn, reads=[wsl, xs_tok[s]], writes=[psr])
                    else:
                        k.op("pe", fn, reads=[wsl, xs_tok[s]], writes=[psr])
                if s == 0:
                    Gb = G1 if which == 0 else G2
                    dst = Gb[:, q * 512:(q + 1) * 512]
                    wr = [Gb]
                else:
                    dst = hst[:, which * D + q * 512:which * D + (q + 1) * 512]
                    wr = hst_res
                k.op("dve", lambda eng, psr=psr, dst=dst, q=q, rbt=rbt: eng.tensor_tensor(out=dst, in0=psr[:, 0:512],
                                                                                          in1=rbt[:, q * 512:(q + 1) * 512], op=ALU.add),
                     reads=[psr, rbt], writes=wr)
                k.op("dve", lambda eng, dst=dst, q=q, rgt=rgt: eng.tensor_tensor(out=dst, in0=dst, in1=rgt[:, q * 512:(q + 1) * 512], op=ALU.mult),
                     reads=wr + [rgt], writes=wr)
    k.dma("sp", ld_ring, gs_scr.rearrange("w p d -> p w d"), hst[:].rearrange("p (w d) -> p w d", w=2), reads=hst_res, writes=[gs_res])
    for qi, (c0, b0) in enumerate(((0, 0), (8, 8), (16, 24), (24, 32))):
        k.op("dve", lambda eng, c0=c0, b0=b0: eng.tensor_tensor(
            out=modc[:, c0:c0 + 8, :], in0=ps_col[:, c0 * 2:(c0 + 8) * 2].rearrange("p (a b) -> p a b", b=2),
            in1=colp[:, C_BADA + b0:C_BADA + b0 + 8].unsqueeze(2).broadcast_to([128, 8, 2]), op=ALU.add),
            reads=[ps_col, colp], writes=[modc])
    for (Aout, Bout, sh0, sc0, gcol) in ((A1, B1, 0, 8, C_GPM), (A2, B2, 16, 24, C_GPL)):
        k.op("dve", lambda eng, Aout=Aout, sc0=sc0: eng.tensor_scalar(out=Aout[:], in0=modc[:, sc0:sc0 + 8, :], scalar1=1.0, scalar2=None,
                                                                       op0=ALU.add), reads=[modc], writes=[Aout])
        k.op("dve", lambda eng, Aout=Aout, gcol=gcol: eng.tensor_tensor(out=Aout[:], in0=Aout[:],
                                                                         in1=colp[:, gcol:gcol + 8].unsqueeze(2).broadcast_to([128, 8, 2]),
                                                                         op=ALU.mult), reads=[Aout, colp], writes=[Aout])
        k.op("dve", lambda eng, Bout=Bout, sh0=sh0: eng.tensor_copy(out=Bout[:], in_=modc[:, sh0:sh0 + 8, :]), reads=[modc], writes=[Bout])

    wait_cast("sp", "w_pool")
    k.dma("sp", ld_ring, wpool[:], wb_pool.rearrange("(a p) c -> p a c", p=128), writes=[wpool])

    wb_in_v = wb_in.rearrange("(kc p) c -> p kc c", p=128)
    wb_so_v = wb_so.rearrange("(kc p) c -> p kc c", p=128)
    wb_o_v = wb_o.rearrange("(kc p) c -> p kc c", p=128)
    wb_up_v = wb_up.rearrange("(kc p) c -> p kc c", p=128)
    wb_down_v = wb_down.rearrange("(kc p) c -> p kc c", p=128)

    Zb, MIDb, ENDb = [], [], []
    for cb in range(4):
        Zb.append(("w_in", wb_in_v[:, :, cb * 512:(cb + 1) * 512], 8, 512))
    for cb in range(8):
        MIDb.append(("w_in", wb_in_v[:, :, 2048 + cb * 512:2048 + (cb + 1) * 512], 8, 512))
    MIDb.append(("w_in", wb_in_v[:, :, 6144:6176], 8, 32))
    for cb in range(2):
        MIDb.append(("w_in", wb_in_v[:, :, 6176 + cb * 512:6176 + (cb + 1) * 512], 8, 512))
    for cb in range(4):
        MIDb.append(("w_in", wb_in_v[:, :, 7200 + cb * 512:7200 + (cb + 1) * 512], 8, 512))
    for cb in range(4):
        MIDb.append(("w_so", wb_so_v[:, :, cb * 256:(cb + 1) * 256], 16, 256))
    for cb in range(2):
        MIDb.append(("w_o", wb_o_v[:, :, cb * 512:(cb + 1) * 512], 8, 512))
    for cb in range(8):
        ENDb.append(("w_up", wb_up_v[:, :, cb * 512:(cb + 1) * 512], 8, 512))
    for cb in range(4):
        for kh in range(2):
            ENDb.append(("w_down", wb_down_v[:, kh * 16:(kh + 1) * 16, cb * 256:(cb + 1) * 256], 16, 256))
    n_pass = n_ptiles + (1 if do_sample else 0)
    wseq = list(Zb)
    for n_ in range(n_pass):
        wseq += MIDb
        if n_ + 1 < n_pass:
            wseq += Zb
        wseq += ENDb
    wstate = {"issued": 0, "next": 0, "bufs": {}, "seen": set()}
    total_blocks = len(wseq)

    def w_issue(upto):
        while wstate["issued"] < min(upto, total_blocks):
            i = wstate["issued"]
            name, src, nk, ncol = wseq[i]
            if name not in wstate["seen"]:
                wstate["seen"].add(name)
                wait_cast("sp", name)
            sl = wslots.take()
            dst = sl[:, 0:nk * ncol].rearrange("p (a b) -> p a b", b=ncol)
            k.dma("sp", ld_ring, dst, src, writes=[sl])
            wstate["bufs"][i] = (sl, nk, ncol)
            wstate["issued"] += 1

    def wnext():
        i = wstate["next"]
        wstate["next"] += 1
        w_issue(i + NWSLOT)
        sl, nk, ncol = wstate["bufs"].pop(i)
        view = sl[:, 0:nk * ncol].rearrange("p (a b) -> p a b", b=ncol)
        return sl, view

    def mm_group(out_ap, pairs, reads, psbuf):
        n = len(pairs)
        for i, (l, r) in enumerate(pairs):
            fn = lambda eng, l=l, r=r, i=i: eng.matmul(out_ap, lhsT=l, rhs=r, start=(i == 0), stop=(i == n - 1))
            if i < n - 1:
                k.pe_quiet(fn, reads=reads, writes=[psbuf])
            else:
                k.op("pe", fn, reads=reads, writes=[psbuf])

    def rstd_from_ss(ss_ap, ss_buf, n_feat, width, nt):
        ln_ = small.take()
        k.op("act", lambda eng: eng.activation(out=ln_[0:nt, 0:width], in_=ss_ap, func=AF.Ln, bias=EPS, scale=1.0 / n_feat),
             reads=[ss_buf], writes=[ln_])
        r_ = small.take()
        k.op("act", lambda eng: eng.activation(out=r_[0:nt, 0:width], in_=ln_[0:nt, 0:width], func=AF.Exp, scale=-0.5),
             reads=[ln_], writes=[r_])
        return r_

    def norm_phase1(xbs_, subs_):
        n = len(subs_)
        sss, rs, xns = [], [], []
        for j in range(n):
            nt = subs_[j]
            ss = small.take()
            junk_act = junk_r.take()
            k.op("act", lambda eng: eng.activation(out=junk_act[0:nt, :], in_=xbs_[j][0:nt, :], func=AF.Square, accum_out=ss[0:nt, 0:1]),
                 reads=[xbs_[j]], writes=[junk_act, ss])
            sss.append(ss)
        for j in range(n):
            nt = subs_[j]
            rs.append(rstd_from_ss(sss[j][0:nt, 0:1], sss[j], D, 1, nt))
        for j in range(n):
            nt = subs_[j]
            xn = xnbuf.take()
            k.op("dve", lambda eng: eng.tensor_scalar(out=xn[0:nt, :], in0=xbs_[j][0:nt, :], scalar1=rs[j][0:nt, 0:1], scalar2=None, op0=ALU.mult),
                 reads=[xbs_[j], rs[j]], writes=[xn])
            xns.append(xn)
        return xns

    def norm_phase2(xns, subs_, offs_, Acol, Bcol, s, dstT):
        n = len(subs_)
        pbs_ = []
        for j in range(n):
            nt = subs_[j]
            pb = psf.take()
            pv = psb_view(pb)
            for kc in range(8):
                k.op("pe", lambda eng: eng.transpose(out=pv[:, kc * 128:kc * 128 + nt], in_=xns[j][0:nt, kc * 128:(kc + 1) * 128],
                                                     identity=ident_b[0:nt, 0:nt]), reads=[xns[j], ident_b], writes=[pb])
            pbs_.append(pb)
        for j in range(n):
            nt = subs_[j]
            pv = psb_view(pbs_[j])
            c0 = offs_[j]
            for kc in range(8):
                k.op("dve", lambda eng: eng.tensor_scalar(out=dstT[kc][:, c0:c0 + nt], in0=pv[:, kc * 128:kc * 128 + nt],
                                                          scalar1=Acol[:, kc, s:s + 1], scalar2=Bcol[:, kc, s:s + 1],
                                                          op0=ALU.mult, op1=ALU.add),
                     reads=[pbs_[j], Acol, Bcol], writes=[dstT[kc]])

    def build_dg_block(cb):
        if cb in dg_ready:
            return
        dgs = []
        for cc in range(4):
            c = cb * 4 + cc
            dg = dgring.take()
            k.op("dve", lambda eng: eng.tensor_tensor(out=dg[:, :, :], in0=ident_b[:, :].unsqueeze(1).broadcast_to([128, 4, 128]),
                                                      in1=colp[:, C_CW + c * 4:C_CW + c * 4 + 4].unsqueeze(2).broadcast_to([128, 4, 128]),
                                                      op=ALU.mult), reads=[ident_b, colp], writes=[dg])
            dgs.append(dg)
        dg_ready[cb] = dgs

    def prep_load(sp):
        subs = sp["subs"]
        offs = [sum(subs[:j]) for j in range(len(subs))]
        sp["xbs"] = []
        for j, nt in enumerate(subs):
            xb = xbuf.take()
            k.dma("sp", ld_ring, xb[0:nt, :], sp["x_rows"][offs[j]:offs[j] + nt, :], writes=[xb])
            sp["xbs"].append(xb)

    def prep_ew(sp):
        sp["xns"] = norm_phase1(sp["xbs"], sp["subs"])

    def prep_pe(sp):
        subs = sp["subs"]
        offs = [sum(subs[:j]) for j in range(len(subs))]
        sp["uT"] = uT_sets.take()
        norm_phase2(sp["xns"], subs, offs, A1, B1, sp["s"], sp["uT"])

    def zproj(sp):
        subs = sp["subs"]
        offs = [sum(subs[:j]) for j in range(len(subs))]
        uT = sp["uT"]
        for cb in range(4):
            sl, W = wnext()
            for j, nt in enumerate(subs):
                ps = psf.take()
                mm_group(ps[0:nt, 0:512], [(uT[kc][:, offs[j]:offs[j] + nt], W[:, kc, :]) for kc in range(8)], [sl] + uT, ps)
                k.op("act", lambda eng, ps=ps, j=j, nt=nt, cb=cb: eng.activation(out=zs_tok[j][0:nt, cb * 512:(cb + 1) * 512], in_=ps[0:nt, 0:512],
                                                                                  func=AF.Silu), reads=[ps], writes=[zs_tok[j]])
        sp["ready"] = True

    def run_tile(sp, nxt_sp):
        s, y_rows, subs, first, last, pos0_is_zero = sp["s"], sp["y_rows"], sp["subs"], sp["first"], sp["last"], sp["pos0"]
        nsub = len(subs)
        TT = sum(subs)
        offs = [sum(subs[:j]) for j in range(nsub)]
        if not sp.get("ready"):
            prep_load(sp)
            prep_ew(sp)
            prep_pe(sp)
            zproj(sp)
        if nxt_sp is not None:
            prep_load(nxt_sp)
        xbs = sp["xbs"]
        uT = sp["uT"]

        BT = [chunk.take() for _ in range(NG)]
        CT = [chunk.take() for _ in range(NG)]
        def emit_transposes(cb_, srcs_):
            for j, nt in enumerate(subs):
                pb = psf.take()
                pv = psb_view(pb)
                for cc in range(4):
                    k.op("pe", lambda eng: eng.transpose(out=pv[0:nt, cc * 128:(cc + 1) * 128], in_=srcs_[cc][:, offs[j]:offs[j] + nt],
                                                         identity=ident_b[:, :]), reads=[srcs_[cc], ident_b], writes=[pb])
                dst = xs_tok[j][0:nt, cb_ * 512:(cb_ + 1) * 512] if cb_ < 4 else B_tok[j][0:nt, (cb_ - 4) * 512:(cb_ - 3) * 512]
                dbuf = xs_tok[j] if cb_ < 4 else B_tok[j]
                k.op("dve", lambda eng: eng.tensor_copy(out=dst, in_=pv[0:nt, 0:512]), reads=[pb], writes=[dbuf])

        def xbc_proj(cb):
            sl, W = wnext()
            pbs = []
            for cc in range(4):
                c = cb * 4 + cc
                ps = psf.take()
                mm_group(ps[:, 0:TT], [(W[:, kc, cc * 128:(cc + 1) * 128], uT[kc][:, 0:TT]) for kc in range(8)], [sl] + uT, ps)
                Pb = pbring.take()
                k.op("pool", lambda eng: eng.tensor_copy(out=Pb[:, 0:3], in_=hist[:, c, :]), reads=[hist], writes=[Pb])
                k.op("act", lambda eng: eng.activation(out=Pb[:, 3:3 + TT], in_=ps[:, 0:TT], func=AF.Copy),
                     reads=[ps], writes=[Pb])
                k.op("pool", lambda eng: eng.tensor_copy(out=hist[:, c, :], in_=Pb[:, TT:TT + 3]), reads=[Pb], writes=[hist])
                if last:
                    k.op("dve", lambda eng: eng.tensor_copy(out=convout[:, c, :], in_=ps[:, TT - 3:TT]),
                         reads=[ps], writes=[convout])
                pbs.append(Pb)
            return pbs

        def xbc_conv(cb, pbs):
            dgs = dg_ready.pop(cb)
            srcs = []
            for cc in range(4):
                c = cb * 4 + cc
                if c < 16:
                    dstb = xf_r.take()
                elif c < 24:
                    dstb = BT[c - 16]
                else:
                    dstb = CT[c - 24]
                ps3 = psf.take()
                mm_group(ps3[:, 0:TT], [(dgs[cc][:, kk, :], pbs[cc][:, kk:kk + TT]) for kk in range(4)], [pbs[cc], dgs[cc]], ps3)
                k.op("act", lambda eng: eng.activation(out=dstb[:, 0:TT], in_=ps3[:, 0:TT], func=AF.Silu,
                                                       bias=colp[:, C_CB + c:C_CB + c + 1]),
                     reads=[ps3, colp], writes=[dstb])
                srcs.append(dstb)
            return srcs

        pend_pb = {}
        pend_src = {}
        for it_ in range(8 + 2):
            if it_ < 8:
                build_dg_block(it_)
                pend_pb[it_] = xbc_proj(it_)
            if 0 <= it_ - 1 < 8:
                pend_src[it_ - 1] = xbc_conv(it_ - 1, pend_pb.pop(it_ - 1))
            if 0 <= it_ - 2 < 6:
                emit_transposes(it_ - 2, pend_src.pop(it_ - 2))
        sl, W = wnext()
        for j, nt in enumerate(subs):
            ps = psf.take()
            mm_group(ps[0:nt, 0:32], [(uT[kc][:, offs[j]:offs[j] + nt], W[:, kc, :]) for kc in range(8)], [sl] + uT, ps)
            t1 = small.take()
            k.op("dve", lambda eng, ps=ps, t1=t1, nt=nt: eng.tensor_tensor(out=t1[0:nt, :], in0=ps[0:nt, 0:32], in1=rhead[0:nt, 0:32], op=ALU.add),
                 reads=[ps, rhead], writes=[t1])
            t2 = small.take()
            k.op("act", lambda eng, t1=t1, t2=t2, nt=nt: eng.activation(out=t2[0:nt, :], in_=t1[0:nt, :], func=AF.Exp), reads=[t1], writes=[t2])
            k.op("act", lambda eng, t2=t2, j=j, nt=nt: eng.activation(out=dtbuf[j][0:nt, :], in_=t2[0:nt, :], func=AF.Ln, bias=1.0),
                 reads=[t2], writes=[dtbuf[j]])
        pmT = [chunk.take() for _ in range(8)]
        for cb in range(2):
            sl, W = wnext()
            for cc in range(4):
                pc = cb * 4 + cc
                gi = pc // 2
                w = (2, 4, 8, 16)[gi]
                ps = psf.take()
                mm_group(ps[:, 0:TT], [(W[:, kc, cc * 128:(cc + 1) * 128], uT[kc][:, 0:TT]) for kc in range(8)], [sl] + uT, ps)
                ppb = pp[pc]
                k.op("act", lambda eng, ps=ps, ppb=ppb: eng.activation(out=ppb[:, 15:15 + TT], in_=ps[:, 0:TT], func=AF.Copy),
                     reads=[ps], writes=[ppb])
                cur = ppb
                step = 1
                lo = 0
                while step < w:
                    nxt = psc.take()
                    lo2 = lo + step
                    k.op("pool", lambda eng, cur=cur, nxt=nxt, lo2=lo2, step=step: eng.tensor_tensor(
                        out=nxt[:, lo2:15 + TT], in0=cur[:, lo2:15 + TT], in1=cur[:, lo2 - step:15 + TT - step], op=ALU.add),
                        reads=[cur], writes=[nxt])
                    cur = nxt
                    lo = lo2
                    step *= 2
                k.op("dve", lambda eng, cur=cur, ppb=ppb, pc=pc, w=w: eng.scalar_tensor_tensor(
                    out=pmT[pc][:, 0:TT], in0=cur[:, 15:15 + TT], scalar=1.0 / w, in1=ppb[:, 15:15 + TT], op0=ALU.mult, op1=ALU.subtract),
                    reads=[cur, ppb], writes=[pmT[pc]])
                if pos0_is_zero:
                    t_ = small.take()
                    k.op("dve", lambda eng, cur=cur, t_=t_, gi=gi: eng.tensor_tensor(out=t_[:, 0:16], in0=cur[:, 15:31], in1=icnt[:, gi, :], op=ALU.mult),
                         reads=[cur, icnt], writes=[t_])
                    k.op("dve", lambda eng, t_=t_, ppb=ppb, pc=pc: eng.tensor_tensor(out=pmT[pc][:, 0:16], in0=t_[:, 0:16], in1=ppb[:, 15:31], op=ALU.subtract),
                         reads=[t_, ppb], writes=[pmT[pc]])
                k.op("pool", lambda eng, ppb=ppb: eng.tensor_copy(out=ppb[:, 0:15], in_=ppb[:, TT:TT + 15]), reads=[ppb], writes=[ppb])
        gT = [chunk.take() for _ in range(16)]
        for cb in range(4):
            sl, W = wnext()
            for cc in range(4):
                gc = cb * 4 + cc
                ps = psf.take()
                mm_group(ps[:, 0:TT], [(W[:, kc, cc * 128:(cc + 1) * 128], uT[kc][:, 0:TT]) for kc in range(8)], [sl] + uT, ps)
                k.op("act", lambda eng, ps=ps, gc=gc: eng.activation(out=gT[gc][:, 0:TT], in_=ps[:, 0:TT], func=AF.Sigmoid),
                     reads=[ps], writes=[gT[gc]])

        chunks = []
        for j, nt in enumerate(subs):
            o = offs[j]
            dtj = dtbuf[j]
            da = small.take()
            k.op("dve", lambda eng: eng.tensor_tensor(out=da[0:nt, :], in0=dtj[0:nt, :], in1=arow[0:nt, :], op=ALU.mult),
                 reads=[dtj, arow], writes=[da])
            pss = psf.take()
            k.op("pe", lambda eng: eng.matmul(pss[0:nt, 0:32], lhsT=U_f[0:nt, 0:nt], rhs=da[0:nt, :], start=True, stop=True),
                 reads=[U_f, da], writes=[pss])
            k.op("pe", lambda eng: eng.matmul(pss[:, 32:64], lhsT=ones_f[0:nt, :], rhs=da[0:nt, :], start=True, stop=True),
                 reads=[ones_f, da], writes=[pss])
            eacs = small.take()
            k.op("act", lambda eng: eng.activation(out=eacs[0:nt, :], in_=pss[0:nt, 0:32], func=AF.Exp), reads=[pss], writes=[eacs])
            cdB = small.take()
            k.op("act", lambda eng: eng.activation(out=cdB[:, :], in_=pss[:, 32:64], func=AF.Exp), reads=[pss], writes=[cdB])
            acs = small.take()
            k.op("act", lambda eng: eng.activation(out=acs[0:nt, :], in_=pss[0:nt, 0:32], func=AF.Copy), reads=[pss], writes=[acs])
            dif = small.take()
            k.op("dve", lambda eng: eng.tensor_tensor(out=dif[0:nt, :], in0=pss[0:nt, 32:64], in1=acs[0:nt, :], op=ALU.subtract),
                 reads=[pss, acs], writes=[dif])
            dte = small.take()
            k.op("act", lambda eng: eng.activation(out=dte[0:nt, :], in_=dif[0:nt, :], func=AF.Exp), reads=[dif], writes=[dte])
            wts = small.take()
            k.op("dve", lambda eng: eng.tensor_tensor(out=wts[0:nt, :], in0=dte[0:nt, :], in1=dtj[0:nt, :], op=ALU.mult),
                 reads=[dte, dtj], writes=[wts])
            chunks.append(dict(nt=nt, o=o, dtj=dtj, da=da, eacs=eacs, cdB=cdB, wts=wts, yz=yz_r.take(), ssg=small.take(), j=j))

        items = [(j, g) for j in range(nsub) for g in range(NG)]
        ist = {}

        def S0(it):
            j, g = it
            c = chunks[j]
            nt = c["nt"]
            Ws = []
            for hh in range(4):
                h = g * 4 + hh
                Wh = W_r.take()
                if hh < 3:
                    k.op("act", lambda eng: eng.activation(out=Wh[0:nt, 0:nt], in_=L_f[0:nt, 0:nt], func=AF.Copy, scale=c["da"][0:nt, h:h + 1]),
                         reads=[L_f, c["da"]], writes=[Wh])
                else:
                    k.op("dve", lambda eng: eng.tensor_scalar(out=Wh[0:nt, 0:nt], in0=L_f[0:nt, 0:nt], scalar1=c["da"][0:nt, h:h + 1],
                                                              scalar2=None, op0=ALU.mult), reads=[L_f, c["da"]], writes=[Wh])
                Ws.append(Wh)
            xg = xg_r.take()
            xs_g = xs_tok[j][0:nt, g * 256:(g + 1) * 256].rearrange("p (h d) -> p h d", d=HD)
            for qi, src in enumerate((c["wts"], rhead, c["dtj"])):
                col0 = 64 if qi == 1 else 0
                k.op("pool",
                     lambda eng: eng.tensor_tensor(out=xg[0:nt, qi, :].rearrange("p (h d) -> p h d", d=HD), in0=xs_g,
                                                   in1=src[0:nt, col0 + g * 4:col0 + (g + 1) * 4].unsqueeze(2).broadcast_to([nt, 4, HD]),
                                                   op=ALU.mult), reads=[xs_tok[j], src], writes=[xg])
            gs_ = slice(g * 256, (g + 1) * 256)
            hres = hst_res[g]
            k.op("pool", lambda eng: eng.tensor_tensor(out=hst[:, gs_].rearrange("p (h d) -> p h d", d=HD),
                                                       in0=hst[:, gs_].rearrange("p (h d) -> p h d", d=HD),
                                                       in1=c["cdB"][:, g * 4:(g + 1) * 4].unsqueeze(2).broadcast_to([128, 4, HD]), op=ALU.mult),
                 reads=[hres, c["cdB"]], writes=[hres])
            ist[it] = dict(W=Ws, xg=xg)

        def S1a(it):
            j, g = it
            c = chunks[j]
            nt, o = c["nt"], c["o"]
            pcb = ps_cb.take()
            k.op("pe", lambda eng: eng.matmul(pcb[0:nt, 0:nt], lhsT=BT[g][:, o:o + nt], rhs=CT[g][:, o:o + nt], start=True, stop=True),
                 reads=[BT[g], CT[g]], writes=[pcb])
            cbm = cbm_r.take()
            k.op("dve", lambda eng: eng.tensor_tensor(out=cbm[0:nt, 0:nt], in0=pcb[0:nt, 0:nt], in1=U_f[0:nt, 0:nt], op=ALU.mult),
                 reads=[pcb, U_f], writes=[cbm])
            ist[it].update(cbm=cbm)

        def S1b(it):
            j, g = it
            c = chunks[j]
            nt = c["nt"]
            pseg = ps_seg.take()
            for hh in range(4):
                Wh = ist[it]["W"][hh]
                k.op("pe", lambda eng: eng.matmul(pseg[0:nt, hh * 128:hh * 128 + nt], lhsT=Wh[0:nt, 0:nt], rhs=U_f[0:nt, 0:nt],
                                                  start=True, stop=True), reads=[Wh, U_f], writes=[pseg])
            ist[it].update(pseg=pseg)

        def S2a(it):
            j, g = it
            nt = chunks[j]["nt"]
            pseg = ist[it]["pseg"]
            segv = pseg[0:nt, :].rearrange("p (h l) -> p h l", l=128)[:, :, 0:nt]
            k.op("act", lambda eng: eng.activation(out=segv, in_=segv, func=AF.Exp), reads=[pseg], writes=[pseg])

        def S2b(it):
            j, g = it
            nt = chunks[j]["nt"]
            pseg, cbm = ist[it]["pseg"], ist[it]["cbm"]
            segv = pseg[0:nt, :].rearrange("p (h l) -> p h l", l=128)[:, :, 0:nt]
            Mg = M_r.take()
            k.op("dve", lambda eng: eng.tensor_tensor(out=Mg[0:nt, :, 0:nt], in0=segv,
                                                      in1=cbm[0:nt, 0:nt].unsqueeze(1).broadcast_to([nt, 4, nt]), op=ALU.mult),
                 reads=[pseg, cbm], writes=[Mg])
            ist[it].update(Mg=Mg)

        def S3(it):
            j, g = it
            c = chunks[j]
            nt, o = c["nt"], c["o"]
            Mg, xg = ist[it]["Mg"], ist[it]["xg"]
            yz, ssg, eacs, cdB = c["yz"], c["ssg"], c["eacs"], c["cdB"]
            gs = slice(g * 256, (g + 1) * 256)
            py = ps_y.take()
            k.pe_quiet(lambda eng: eng.matmul(py[0:nt, 0:256], lhsT=ident_b[0:nt, 0:nt], rhs=xg[0:nt, 1, :], start=True, stop=False),
                       reads=[ident_b, xg], writes=[py])
            for hh in range(4):
                fn = lambda eng: eng.matmul(py[0:nt, hh * 64:(hh + 1) * 64], lhsT=Mg[0:nt, hh, 0:nt], rhs=xg[0:nt, 2, hh * 64:(hh + 1) * 64],
                                            start=False, stop=(hh == 3))
                if hh < 3:
                    k.pe_quiet(fn, reads=[Mg, xg], writes=[py])
                else:
                    k.op("pe", fn, reads=[Mg, xg], writes=[py])
            k.op("pe", lambda eng: eng.matmul(py[0:nt, 256:512], lhsT=CT[g][:, o:o + nt], rhs=hbf[g][:, :], start=True, stop=True),
                 reads=[CT[g], hbf[g]], writes=[py])
            pst = ps_st.take()
            k.op("pe", lambda eng: eng.matmul(pst[:, 0:256], lhsT=B_tok[j][0:nt, g * 128:(g + 1) * 128], rhs=xg[0:nt, 0, :], start=True, stop=True),
                 reads=[B_tok[j], xg], writes=[pst])
            t2 = tmp256.take()
            k.op("dve", lambda eng: eng.tensor_tensor(out=t2[0:nt, :].rearrange("p (h d) -> p h d", d=HD),
                                                      in0=py[0:nt, 256:512].rearrange("p (h d) -> p h d", d=HD),
                                                      in1=eacs[0:nt, g * 4:(g + 1) * 4].unsqueeze(2).broadcast_to([nt, 4, HD]), op=ALU.mult),
                 reads=[py, eacs], writes=[t2])
            ybf = ybf_r.take()
            k.op("dve", lambda eng: eng.tensor_tensor(out=ybf[0:nt, :], in0=py[0:nt, 0:256], in1=t2[0:nt, :], op=ALU.add),
                 reads=[py, t2], writes=[ybf])
            k.op("dve", lambda eng: eng.tensor_tensor(out=yz[0:nt, gs], in0=ybf[0:nt, :], in1=zs_tok[j][0:nt, gs], op=ALU.mult),
                 reads=[ybf, zs_tok[j]], writes=[yz.subs[g]])
            ist[it].update(pst=pst)

        def S4a(it):
            j, g = it
            pst = ist[it]["pst"]
            gs = slice(g * 256, (g + 1) * 256)
            hres = hst_res[g]
            k.op("dve", lambda eng: eng.tensor_tensor(out=hst[:, gs], in0=pst[:, 0:256], in1=hst[:, gs], op=ALU.add),
                 reads=[pst, hres], writes=[hres])

        def S4b(it):
            j, g = it
            c = chunks[j]
            nt = c["nt"]
            yz, ssg = c["yz"], c["ssg"]
            gs = slice(g * 256, (g + 1) * 256)
            hres = hst_res[g]
            junk_act = junk_r.take()
            k.op("act", lambda eng: eng.activation(out=junk_act[0:nt, 0:256], in_=yz[0:nt, gs], func=AF.Square, accum_out=ssg[0:nt, g:g + 1]),
                 reads=[yz.subs[g]], writes=[junk_act, ssg])
            k.op("act", lambda eng: eng.activation(out=hbf[g][:, :], in_=hst[:, gs], func=AF.Copy), reads=[hres], writes=[hbf[g]])
            del ist[it]
            if g == NG - 1:
                epilogue(c)

        def epilogue(c):
            nt, o, yz, ssg = c["nt"], c["o"], c["yz"], c["ssg"]
            rg = rstd_from_ss(ssg[0:nt, 0:8], ssg, 256, 8, nt)
            yg = yz
            for g in range(NG):
                gs = slice(g * 256, (g + 1) * 256)
                k.op("dve", lambda eng: eng.scalar_tensor_tensor(out=yg[0:nt, gs], in0=yz[0:nt, gs], scalar=rg[0:nt, g:g + 1],
                                                                 in1=gsn_bc[0:nt, gs], op0=ALU.mult, op1=ALU.mult),
                     reads=[yz.subs[g], rg, gsn_bc], writes=[yz.subs[g]])
            for half in range(2):
                pb = ps_y.take()
                pv = psb_view(pb)
                for q in range(8):
                    fc = half * 8 + q
                    k.op("pe", lambda eng: eng.transpose(out=pv[:, q * 128:q * 128 + nt], in_=yg[0:nt, fc * 128:(fc + 1) * 128],
                                                         identity=ident_b[0:nt, 0:nt]), reads=[yz.subs[fc // 2], ident_b], writes=[pb])
                k.op("act", lambda eng: eng.activation(
                    out=ygT_all[:, half * 8:(half + 1) * 8, o:o + nt],
                    in_=pv.rearrange("p (q t) -> p q t", t=128)[:, :, 0:nt], func=AF.Copy), reads=[pb], writes=[ygT_all])

        n_it = len(items)
        ps_cb = Ring(psf.bufs[0:1])
        ps_seg = Ring(psf.bufs[1:3])
        ps_y = Ring(psf.bufs[3:5])
        ps_st = Ring(psf.bufs[5:8])
        def at(fn, idx):
            if 0 <= idx < n_it:
                fn(items[idx])

        for i in range(n_it + 4):
            at(S2a, i - 2)
            at(S4a, i - 4)
            at(S0, i)
            at(S1a, i - 1)
            at(S2b, i - 2)
            at(S3, i - 3)
            at(S4b, i - 4)
            at(S1b, i - 1)

        if nxt_sp is not None:
            prep_ew(nxt_sp)
        mixT = [chunk.take() for _ in range(8)]
        for cb in range(4):
            sl, W = wnext()
            for dl in range(2):
                dc = cb * 2 + dl
                gi = dc // 2
                ps = psf.take()
                mm_group(ps[:, 0:TT], [(W[:, kc, dl * 128:(dl + 1) * 128], ygT_all[:, kc, 0:TT]) for kc in range(16)], [sl, ygT_all], ps)
                ps2 = psf.take()
                mm_group(ps2[:, 0:TT], [(wpool[:, gi * 2 + kc, (dc % 2) * 128:(dc % 2 + 1) * 128], pmT[gi * 2 + kc][:, 0:TT]) for kc in range(2)],
                         [wpool, pmT[gi * 2], pmT[gi * 2 + 1]], ps2)
                a_ = ab_r.take()
                k.op("dve", lambda eng, ps=ps, a_=a_, dc=dc: eng.tensor_tensor(out=a_[:, 0:TT], in0=ps[:, 0:TT], in1=gT[dc][:, 0:TT], op=ALU.mult),
                     reads=[ps, gT[dc]], writes=[a_])
                b_ = ab_r.take()
                k.op("dve", lambda eng, ps2=ps2, b_=b_, dc=dc: eng.scalar_tensor_tensor(
                    out=b_[:, 0:TT], in0=ps2[:, 0:TT], scalar=colp[:, C_PSC + dc:C_PSC + dc + 1], in1=gT[8 + dc][:, 0:TT],
                    op0=ALU.mult, op1=ALU.mult), reads=[ps2, colp, gT[8 + dc]], writes=[b_])
                k.op("pool", lambda eng, a_=a_, b_=b_, dc=dc: eng.tensor_tensor(out=mixT[dc][:, 0:TT], in0=a_[:, 0:TT], in1=b_[:, 0:TT], op=ALU.add),
                     reads=[a_, b_], writes=[mixT[dc]])
        if nxt_sp is not None:
            prep_pe(nxt_sp)
        pso = [[None, None] for _ in subs]
        for ob in range(2):
            sl, W = wnext()
            for j, nt in enumerate(subs):
                ps = psf.take()
                mm_group(ps[0:nt, 0:512], [(mixT[kc][:, offs[j]:offs[j] + nt], W[:, kc, :]) for kc in range(8)], [sl] + mixT, ps)
                pso[j][ob] = ps
        for j, nt in enumerate(subs):
            ssA = small.take()
            for ob in range(2):
                junk_act = junk_r.take()
                k.op("act", lambda eng, ob=ob: eng.activation(out=junk_act[0:nt, 0:512], in_=pso[j][ob][0:nt, 0:512], func=AF.Square,
                                                              accum_out=ssA[0:nt, ob:ob + 1]), reads=[pso[j][ob]], writes=[junk_act, ssA])
            ss = small.take()
            k.op("dve", lambda eng: eng.tensor_tensor(out=ss[0:nt, 0:1], in0=ssA[0:nt, 0:1], in1=ssA[0:nt, 1:2], op=ALU.add),
                 reads=[ssA], writes=[ss])
            r_ = rstd_from_ss(ss[0:nt, 0:1], ss, D, 1, nt)
            tmp = f32tmp.take()
            for ob in range(2):
                k.op("dve", lambda eng, ob=ob: eng.scalar_tensor_tensor(
                    out=tmp[0:nt, ob * 512:(ob + 1) * 512], in0=pso[j][ob][0:nt, 0:512], scalar=r_[0:nt, 0:1],
                    in1=G1[0:nt, ob * 512:(ob + 1) * 512], op0=ALU.mult, op1=ALU.mult),
                    reads=[pso[j][ob], r_, G1], writes=[tmp])
            k.op("pool", lambda eng: eng.tensor_tensor(out=xbs[j][0:nt, :], in0=xbs[j][0:nt, :], in1=tmp[0:nt, :], op=ALU.add),
                 reads=[xbs[j], tmp], writes=[xbs[j]])

        vxn = norm_phase1(xbs, subs)
        if nxt_sp is not None:
            zproj(nxt_sp)
        vT = [chunk.take() for _ in range(8)]
        norm_phase2(vxn, subs, offs, A2, B2, s, vT)
        hdnT = [chunk.take() for _ in range(32)]
        if nxt_sp is not None:
            build_dg_block(0)
            build_dg_block(1)
        for cb in range(8):
            sl, W = wnext()
            for cc in range(4):
                fc = cb * 4 + cc
                ps = psf.take()
                mm_group(ps[:, 0:TT], [(W[:, kc, cc * 128:(cc + 1) * 128], vT[kc][:, 0:TT]) for kc in range(8)], [sl] + vT, ps)
                rl = relu_r.take()
                k.op("act", lambda eng, ps=ps, rl=rl: eng.activation(out=rl[:, 0:TT], in_=ps[:, 0:TT], func=AF.Relu), reads=[ps], writes=[rl])
                k.op("pool", lambda eng, rl=rl, fc=fc: eng.tensor_tensor(out=hdnT[fc][:, 0:TT], in0=rl[:, 0:TT], in1=rl[:, 0:TT], op=ALU.mult),
                     reads=[rl], writes=[hdnT[fc]])
        dnbuf = [f32tmp.take() for _ in subs]
        for cb in range(4):
            psd = [psf.take() for _ in subs]
            for kh in range(2):
                sl, W = wnext()
                for j, nt in enumerate(subs):
                    for kk in range(16):
                        fc = kh * 16 + kk
                        fn = lambda eng, j=j, nt=nt, kk=kk, fc=fc: eng.matmul(psd[j][0:nt, 0:256], lhsT=hdnT[fc][:, offs[j]:offs[j] + nt], rhs=W[:, kk, :],
                                                                              start=(kh == 0 and kk == 0), stop=(kh == 1 and kk == 15))
                        if kk < 15:
                            k.pe_quiet(fn, reads=[sl, hdnT[fc]], writes=[psd[j]])
                        else:
                            k.op("pe", fn, reads=[sl, hdnT[fc]], writes=[psd[j]])
            for j, nt in enumerate(subs):
                k.op("act", lambda eng, j=j, nt=nt: eng.activation(out=dnbuf[j][0:nt, cb * 256:(cb + 1) * 256], in_=psd[j][0:nt, 0:256], func=AF.Copy),
                     reads=[psd[j]], writes=[dnbuf[j]])
        for j, nt in enumerate(subs):
            ss = small.take()
            junk_act = junk_r.take()
            k.op("act", lambda eng: eng.activation(out=junk_act[0:nt, :], in_=dnbuf[j][0:nt, :], func=AF.Square, accum_out=ss[0:nt, 0:1]),
                 reads=[dnbuf[j]], writes=[junk_act, ss])
            r_ = rstd_from_ss(ss[0:nt, 0:1], ss, D, 1, nt)
            k.op("dve", lambda eng: eng.scalar_tensor_tensor(out=dnbuf[j][0:nt, :], in0=dnbuf[j][0:nt, :], scalar=r_[0:nt, 0:1], in1=G2[0:nt, :],
                                                             op0=ALU.mult, op1=ALU.mult), reads=[dnbuf[j], r_, G2], writes=[dnbuf[j]])
            k.op("dve", lambda eng: eng.tensor_tensor(out=xbs[j][0:nt, :], in0=xbs[j][0:nt, :], in1=dnbuf[j][0:nt, :], op=ALU.add),
                 reads=[xbs[j], dnbuf[j]], writes=[xbs[j]])
            k.dma("sp", sst_ring, y_rows[offs[j]:offs[j] + nt, :], xbs[j][0:nt, :], reads=[xbs[j]])

    def init_state(zero, s):
        if zero:
            for g in range(NG):
                k.op("pool", lambda eng, g=g: eng.memset(hst[:, g * 256:(g + 1) * 256], 0.0), writes=[hst_res[g]])
                k.op("pool", lambda eng, g=g: eng.memset(hbf[g][:, :], 0.0), writes=[hbf[g]])
            k.op("pool", lambda eng: eng.memset(hist[:], 0.0), writes=[hist])
            for pc in range(8):
                k.op("pool", lambda eng, pc=pc: eng.memset(pp[pc][:, 0:15], 0.0), writes=[pp[pc]])
        else:
            k.dma("sp", ld_ring, hst[:], h0T_d, writes=hst_res)
            for g in range(NG):
                k.op("act", lambda eng, g=g: eng.activation(out=hbf[g][:, :], in_=hst[:, g * 256:(g + 1) * 256], func=AF.Copy),
                     reads=[hst_res[g]], writes=[hbf[g]])
            k.dma("sp", ld_ring, convout[:], convT_d, writes=[convout])
            k.op("dve", lambda eng: eng.tensor_copy(out=hist[:], in_=convout[:]), reads=[convout], writes=[hist])
            for pc in range(8):
                k.dma("sp", ld_ring, pp[pc][:, 0:15], poolT_d[:, pc, :], writes=[pp[pc]])

    def store_state(hT_out, conv_out_d, pool_out_d):
        k.dma("pool", st_ring, hT_out, hst[:], reads=hst_res)
        k.dma("pool", st_ring, conv_out_d, convout[:], reads=[convout])
        for pc in range(8):
            k.dma("pool", st_ring, pool_out_d[:, pc, :], pp[pc][:, 0:15], reads=[pp[pc]])

    psf.bufs.append(psx)
    specs = []
    for ti in range(n_ptiles):
        specs.append(dict(s=0, x_rows=xp_d[ti * TT_MAX:(ti + 1) * TT_MAX, :], y_rows=yp_d[ti * TT_MAX:(ti + 1) * TT_MAX, :],
                          subs=[TSUB] * NSUB, first=(ti == 0), last=(ti == n_ptiles - 1), pos0=(ti == 0)))
    if do_sample:
        specs.append(dict(s=1, x_rows=xs_d, y_rows=ys_d, subs=[DEC], first=True, last=True, pos0=False))
    init_state(True, 0)
    for ti in range(n_ptiles):
        run_tile(specs[ti], specs[ti + 1] if ti + 1 < len(specs) else None)
    store_state(hTp_d, convp_d, poolp_d)
    if do_sample:
        k.dma("sp", ld_ring, G1[:], gs_scr[0], reads=[gs_res], writes=[G1])
        k.dma("sp", ld_ring, G2[:], gs_scr[1], reads=[gs_res], writes=[G2])
        init_state(False, 1)
        run_tile(specs[-1], None)
        store_state(hTs_d, convs_d, pools_d)
    k.finish()
    es.close()
    return nc


def make_in_maps(inputs, n_ptiles=SEQ // TT_MAX):
    f = lambda a: np.ascontiguousarray(np.asarray(a, dtype=np.float32))
    ntok = n_ptiles * TT_MAX

    def col(v, n):
        return f(v).reshape(n, 128).T

    colp = np.zeros((128, NCOL), np.float32)
    colp[:, C_BADA:C_BADA + 48] = col(inputs["b_ada"][0], 48)
    colp[:, C_GPM:C_GPM + 8] = col(inputs["g_pre_mix"][0], 8)
    colp[:, C_GPL:C_GPL + 8] = col(inputs["g_pre_mlp"][0], 8)
    cw = f(inputs["conv_w"][0])
    colp[:, C_CW:C_CW + 128] = cw.reshape(4, 32, 128).transpose(2, 1, 0).reshape(128, 128)
    colp[:, C_CB:C_CB + 32] = col(inputs["conv_b"][0], 32)
    colp[:, C_GSN:C_GSN + 16] = col(inputs["g_ssd_norm"][0], 16)
    colp[:, C_PSC:C_PSC + 8] = col(inputs["pool_scale"][0], 8)
    b_ada = f(inputs["b_ada"][0])
    rbada = np.broadcast_to(np.concatenate([b_ada[2048:3072], b_ada[5120:6144]])[None, :], (128, 2048))
    rgpost = np.broadcast_to(np.concatenate([f(inputs["g_post_mix"][0]), f(inputs["g_post_mlp"][0])])[None, :], (128, 2048))
    rhead = np.broadcast_to(np.concatenate([f(inputs["dt_bias"][0]), f(inputs["a_log"][0]), f(inputs["d_skip"][0])])[None, :], (128, 96))
    rgsn = np.broadcast_to(f(inputs["g_ssd_norm"][0])[None, :], (128, DI))
    shared = dict(
        colp=f(colp), rbada=f(rbada), rgpost=f(rgpost), rhead=f(rhead), rgsn=f(rgsn),
        w_ada=f(inputs["w_ada"][0]), w_in=f(inputs["w_in"][0]), w_so=f(inputs["w_ssd_out"][0]),
        w_pool=f(inputs["w_pool_group"][0]).reshape(1024, 256), w_o=f(inputs["w_o"][0]),
        w_up=f(inputs["w_up"][0]), w_down=f(inputs["w_down"][0]),
    )
    maps = []
    for i in range(8):
        m = dict(shared)
        m["xp"] = f(inputs["x_prompt"][i][:ntok])
        m["xs"] = f(inputs["x_sample"][i])
        cc = np.stack([col(inputs["c_prompt"][i], 8), col(inputs["c_sample"][i], 8)], axis=-1)
        m["ccol"] = f(cc)
        m["h0T"] = f(np.asarray(inputs["state_ssm"][0, i]).reshape(NH * HD, NS).T)
        m["convT"] = f(np.asarray(inputs["state_conv"][0, i]).reshape(3, 32, 128).transpose(2, 1, 0))
        m["poolT"] = f(np.asarray(inputs["state_pool"][0, i]).reshape(15, 8, 128).transpose(2, 1, 0))
        maps.append(m)
    return maps


def gather(results, n_ptiles=SEQ // TT_MAX):
    ntok = n_ptiles * TT_MAX
    yp = np.stack([r["yp"] for r in results]).reshape(8, ntok, D)
    ys = np.stack([r["ys"] for r in results]).reshape(8, DEC, D)

    def ssm(key):
        return np.stack([np.asarray(r[key]).reshape(128, NH, HD).transpose(1, 2, 0) for r in results])[None]

    def conv(key):
        return np.stack([np.asarray(r[key]).reshape(128, 32, 3).transpose(2, 1, 0).reshape(3, CONVD) for r in results])[None]

    def pool(key):
        return np.stack([np.asarray(r[key]).reshape(128, 8, 15).transpose(2, 1, 0).reshape(15, D) for r in results])[None]

    outs = (yp, ys, ssm("hTp"), conv("convp"), pool("poolp"), ssm("hTs"), conv("convs"), pool("pools"))
    return tuple(np.ascontiguousarray(o, dtype=np.float32) for o in outs)


_NC_CACHE = {}


def kernel(**inputs):
    if "nc" not in _NC_CACHE:
        _NC_CACHE["nc"] = build_nc()
    nc = _NC_CACHE["nc"]
    in_maps = make_in_maps(inputs)
    res = run_bass_kernel_spmd(nc, in_maps, core_ids=list(range(8)))
    return gather(res.results)
```

```python
import numpy as np
from contextlib import ExitStack
import concourse.bass as bass
import concourse.mybir as mybir
from concourse.bass_utils import run_bass_kernel_spmd

F32 = mybir.dt.float32
BF16 = mybir.dt.bfloat16
AF = mybir.ActivationFunctionType
ALU = mybir.AluOpType

D = 1024
SEQ = 4096
DEC = 16
DI = 2048
NH = 32
HD = 64
NG = 8
NS = 128
CONVD = 4096
DPROJ = 9248
DFF = 4096
EPS = 1e-6
PAST = 2048
TSUB = 128
NSUB = 2
TT_MAX = TSUB * NSUB
WSLOT_ELEMS = 4096
NWSLOT = 3

C_BADA, C_GPM, C_GPL, C_CW, C_CB, C_GSN, C_PSC, NCOL = 0, 48, 56, 64, 192, 224, 240, 248


class Res:
    __slots__ = ("w", "r")

    def __init__(self):
        self.w = None
        self.r = {}


class Sig:
    def __init__(self, sem, name):
        self.sem = sem
        self.cnt = 0
        self.name = name


class Buf:
    def __init__(self, t, res=None, psum=False):
        self.t = t
        self.res = res if res is not None else Res()
        self.psum = psum

    def __getitem__(self, idx):
        return self.t[idx]


class Ring:
    def __init__(self, bufs):
        self.bufs = bufs
        self.i = 0

    def take(self):
        b = self.bufs[self.i % len(self.bufs)]
        self.i += 1
        return b


class K:
    def __init__(self, nc, es):
        self.nc = nc
        self.es = es
        self.eng = {"pe": nc.tensor, "act": nc.scalar, "dve": nc.vector, "pool": nc.gpsimd, "sp": nc.sync}
        self.sig = {}
        for e in self.eng:
            self.sig[e] = Sig(es.enter_context(nc.semaphore("s_" + e)), e)
        self.known = {e: {} for e in self.eng}
        self.nbuf = 0
        self.dsigs = {}
        self.tag = None

    def sb(self, shape, dt, name=None):
        self.nbuf += 1
        t = self.es.enter_context(self.nc.sbuf_tensor("s_" + (name or f"sb{self.nbuf}"), list(shape), dt))
        return Buf(t)

    def ps(self, shape, dt, name=None):
        self.nbuf += 1
        t = self.es.enter_context(self.nc.psum_tensor(name or f"ps{self.nbuf}", list(shape), dt))
        return Buf(t, psum=True)

    def ring(self, n, shape, dt, name):
        return Ring([self.sb(shape, dt, f"{name}{i}") for i in range(n)])

    def dsig_ring(self, n, name):
        sigs = [Sig(self.es.enter_context(self.nc.semaphore(f"d_{name}{i}")), f"{name}{i}") for i in range(n)]
        self.dsigs[name] = sigs
        return Ring(sigs)

    def _waits(self, e, reads, writes):
        needs = {}

        def add(sigv, same_ok):
            s, v = sigv
            if s is self.sig[e] and not same_ok and e == "pe":
                return
            if needs.get(s, 0) < v:
                needs[s] = v

        for b in reads:
            if b.res.w is not None:
                add(b.res.w, True)
            if b.psum:
                for s, v in b.res.r.items():
                    if s is not self.sig[e]:
                        add((s, v), True)
        for b in writes:
            if b.res.w is not None:
                add(b.res.w, False)
            for s, v in b.res.r.items():
                add((s, v), False)
        kn = self.known[e]
        eng = self.eng[e]
        for s, v in needs.items():
            if kn.get(s, 0) >= v:
                continue
            eng.wait_ge(s.sem, v)
            kn[s] = v

    def op(self, e, fn, reads=(), writes=()):
        reads = [b for b in reads if b is not None]
        writes = [b for b in writes if b is not None]
        self._waits(e, reads, writes)
        inst = fn(self.eng[e])
        sg = self.sig[e]
        sg.cnt += 1
        inst.then_inc(sg.sem, 1)
        for b in reads:
            b.res.r[sg] = sg.cnt
        for b in writes:
            b.res.w = (sg, sg.cnt)
            b.res.r = {}
        return inst

    def pe_quiet(self, fn, reads=(), writes=()):
        reads = [b for b in reads if b is not None]
        writes = [b for b in writes if b is not None]
        self._waits("pe", reads, writes)
        fn(self.eng["pe"])
        sg = self.sig["pe"]
        nxt = sg.cnt + 1
        for b in reads:
            b.res.r[sg] = nxt
        for b in writes:
            b.res.w = (sg, nxt)
            b.res.r = {}

    def dma(self, q, ring, out, in_, reads=(), writes=()):
        reads = [b for b in reads if b is not None]
        writes = [b for b in writes if b is not None]
        self._waits(q, reads, writes)
        sg = ring.take()
        kn = self.known[q]
        if kn.get(sg, 0) < sg.cnt:
            self.eng[q].wait_ge(sg.sem, sg.cnt)
            kn[sg] = sg.cnt
        inst = self.eng[q].dma_start(out=out, in_=in_)
        sg.cnt += 16
        inst.then_inc(sg.sem, 16)
        for b in reads:
            b.res.r[sg] = sg.cnt
        for b in writes:
            b.res.w = (sg, sg.cnt)
            b.res.r = {}

    def finish(self):
        sp = self.eng["sp"]
        for sigs in self.dsigs.values():
            for sg in sigs:
                if sg.cnt > 0:
                    sp.wait_ge(sg.sem, sg.cnt)
        for e, sg in self.sig.items():
            if e != "sp" and sg.cnt > 0:
                sp.wait_ge(sg.sem, sg.cnt)


def build_nc(n_ptiles=SEQ // TT_MAX, do_sample=True):
    nc = bass.Bass("TRN2", target_bir_lowering=False)
    es = ExitStack()
    k = K(nc, es)
    NP_TOK = n_ptiles * TT_MAX

    def din(name, shape, dt=F32):
        return nc.dram_tensor(name, list(shape), dt, kind="ExternalInput").ap()

    def dout(name, shape, dt=F32):
        return nc.dram_tensor(name, list(shape), dt, kind="ExternalOutput").ap()

    def dscr(name, shape, dt=BF16):
        return nc.dram_tensor(name, list(shape), dt, kind="Internal").ap()

    xp_d = din("xp", [NP_TOK, D])
    xs_d = din("xs", [DEC, D])
    ccol_d = din("ccol", [128, 8, 2])
    colp_d = din("colp", [128, NCOL])
    rbada_d = din("rbada", [128, 2048])
    rgpost_d = din("rgpost", [128, 2048])
    rhead_d = din("rhead", [128, 96])
    rgsn_d = din("rgsn", [128, DI])
    h0T_d = din("h0T", [128, DI])
    convT_d = din("convT", [128, 32, 3])
    poolT_d = din("poolT", [128, 8, 15])
    w_ada_d = din("w_ada", [D, 6 * D])
    w_in_d = din("w_in", [D, DPROJ])
    w_so_d = din("w_so", [DI, D])
    w_pool_d = din("w_pool", [D, 256])
    w_o_d = din("w_o", [D, D])
    w_up_d = din("w_up", [D, DFF])
    w_down_d = din("w_down", [DFF, D])

    yp_d = dout("yp", [NP_TOK, D])
    ys_d = dout("ys", [DEC, D])
    hTp_d = dout("hTp", [128, DI])
    hTs_d = dout("hTs", [128, DI])
    convp_d = dout("convp", [128, 32, 3])
    convs_d = dout("convs", [128, 32, 3])
    poolp_d = dout("poolp", [128, 8, 15])
    pools_d = dout("pools", [128, 8, 15])

    wb_ada = dscr("wb_ada", [D, 6 * D])
    wb_in = dscr("wb_in", [D, DPROJ])
    wb_so = dscr("wb_so", [DI, D])
    wb_pool = dscr("wb_pool", [D, 256])
    wb_o = dscr("wb_o", [D, D])
    wb_up = dscr("wb_up", [D, DFF])
    wb_down = dscr("wb_down", [DFF, D])

    ld_ring = k.dsig_ring(12, "ld")
    st_ring = k.dsig_ring(8, "st")
    cast_ring = k.dsig_ring(20, "cast")
    sst_ring = k.dsig_ring(6, "sst")

    def cast(name, src, dst, rows, cols, inner, rstep):
        for r0 in range(0, rows, rstep):
            s_ = src[r0:r0 + rstep, :].rearrange("r (a b) -> r a b", b=inner)
            d_ = dst[r0:r0 + rstep, :].rearrange("r (a b) -> r a b", b=inner)
            sg = cast_ring.take()
            eng = k.eng["pool"]
            kn = k.known["pool"]
            if kn.get(sg, 0) < sg.cnt:
                eng.wait_ge(sg.sem, sg.cnt)
                kn[sg] = sg.cnt
            inst = eng.dma_start(out=d_, in_=s_)
            sg.cnt += 16
            inst.then_inc(sg.sem, 16)
            cast_parts.setdefault(name, []).append((sg, sg.cnt))

    cast_parts = {}

    ident_b = k.sb([128, 128], BF16, "ident_b")
    ones_f = k.sb([128, 128], F32, "ones_f")
    ones_b = k.sb([128, 128], BF16, "ones_b")
    U_f = k.sb([128, 128], F32, "U_f")
    L_f = k.sb([128, 128], F32, "L_f")
    colp = k.sb([128, NCOL], F32, "colp")
    rhead = k.sb([128, 96], F32, "rhead")
    arow = k.sb([128, 32], F32, "arow")
    icnt = k.sb([128, 4, 16], F32, "icnt")
    wpool = k.sb([128, 8, 256], BF16, "wpool")
    ccol = k.sb([128, 8, 2], F32, "ccol")
    scol = k.sb([128, 8, 2], F32, "scol")
    modc = k.sb([128, 32, 2], F32, "modc")
    A1 = k.sb([128, 8, 2], F32, "A1")
    B1 = k.sb([128, 8, 2], F32, "B1")
    A2 = k.sb([128, 8, 2], F32, "A2")
    B2 = k.sb([128, 8, 2], F32, "B2")
    G1 = k.sb([128, D], F32, "G1")
    G2 = k.sb([128, D], F32, "G2")
    gs_scr = nc.dram_tensor("gs_scr", [2, 128, D], F32, kind="Internal").ap()
    gs_res = Buf(None)

    psf = Ring([k.ps([128, 512], F32, f"psf{i}") for i in range(7)])
    psx = k.ps([128, 512], F32, "psx")

    def psb_view(b):
        return b.t[:].bitcast(BF16)

    wslots = Ring([k.sb([128, WSLOT_ELEMS], BF16, f"wsl{i}") for i in range(NWSLOT)])
    xbuf = Ring([k.sb([128, D], F32, f"xb{i}") for i in range(2 * NSUB)])
    f32tmp = Ring([k.sb([128, D], F32, f"ft{i}") for i in range(2)])
    xnbuf = Ring([k.sb([128, D], BF16, f"xn{i}") for i in range(2)])
    junk_r = Ring([k.sb([128, D], BF16, f"junk{i}") for i in range(1)])
    chunk = Ring([k.sb([128, TT_MAX], BF16, f"ch{i}") for i in range(40)])
    uT_sets = Ring([[k.sb([128, TT_MAX], BF16, f"uT{a}_{i}") for i in range(8)] for a in range(2)])
    xs_tok = [k.sb([128, DI], BF16, f"xstok{i}") for i in range(NSUB)]
    B_tok = [k.sb([128, NG * NS], BF16, f"btok{i}") for i in range(NSUB)]
    zs_tok = [k.sb([128, DI], BF16, f"zstok{i}") for i in range(NSUB)]
    pbring = Ring([k.sb([128, TT_MAX + 3], BF16, f"pb{i}") for i in range(8)])
    xf_r = Ring([k.sb([128, TT_MAX], BF16, f"xf{i}") for i in range(8)])
    dgring = Ring([k.sb([128, 4, 128], BF16, f"dg{i}") for i in range(8)])
    dg_ready = {}
    hist = k.sb([128, 32, 3], BF16, "hist")
    convout = k.sb([128, 32, 3], F32, "convout")
    pp = [k.sb([128, 15 + TT_MAX], F32, f"pp{i}") for i in range(8)]
    psc = Ring([k.sb([128, 15 + TT_MAX], F32, f"psc{i}") for i in range(3)])
    small = Ring([k.sb([128, 32], F32, f"sm{i}") for i in range(32)])
    dtbuf = [k.sb([128, 32], F32, f"dt{i}") for i in range(NSUB)]
    cbm_r = Ring([k.sb([128, 128], F32, f"cbm{i}") for i in range(3)])
    W_r = Ring([k.sb([128, 128], F32, f"Wh{i}") for i in range(12)])
    M_r = Ring([k.sb([128, 4, 128], BF16, f"Mg{i}") for i in range(3)])
    xg_r = Ring([k.sb([128, 3, 256], BF16, f"xg{i}") for i in range(4)])
    ygT_all = k.sb([128, 16, TT_MAX], BF16, "ygT_all")
    gsn_bc = k.sb([128, DI], BF16, "gsn_bc")
    tmp256 = Ring([k.sb([128, 256], F32, f"t256_{i}") for i in range(2)])
    ybf_r = Ring([k.sb([128, 256], BF16, f"ybf{i}") for i in range(3)])
    yz_r = Ring([k.sb([128, DI], BF16, f"yz{i}") for i in range(2)])
    for b_ in yz_r.bufs:
        b_.subs = [Buf(b_.t) for _ in range(NG)]
    hst = k.sb([128, DI], F32, "hst")
    hbf = [k.sb([128, 256], BF16, f"hbf{g}") for g in range(NG)]
    ab_r = Ring([k.sb([128, TT_MAX], F32, f"ab{i}") for i in range(2)])
    relu_r = Ring([k.sb([128, TT_MAX], BF16, f"rl{i}") for i in range(3)])

    hst_res = [Buf(None) for _ in range(NG)]

    def memset(e, buf, val):
        k.op(e, lambda eng: eng.memset(buf[:], val), writes=[buf])

    memset("pool", ones_f, 1.0)
    memset("pool", ones_b, 1.0)
    k.op("pool", lambda eng: eng.affine_select(out=ident_b[:], in_=ones_b[:], pattern=[[-1, 128]], compare_op=ALU.is_equal,
                                               fill=0.0, base=0, channel_multiplier=1), reads=[ones_b], writes=[ident_b])
    k.op("pool", lambda eng: eng.affine_select(out=U_f[:], in_=ones_f[:], pattern=[[1, 128]], compare_op=ALU.is_ge,
                                               fill=0.0, base=0, channel_multiplier=-1), reads=[ones_f], writes=[U_f])
    k.op("pool", lambda eng: eng.affine_select(out=L_f[:], in_=ones_f[:], pattern=[[-1, 128]], compare_op=ALU.is_gt,
                                               fill=0.0, base=0, channel_multiplier=1), reads=[ones_f], writes=[L_f])

    cast("w_ada", w_ada_d, wb_ada, D, 6 * D, 2048, 256)
    cast("w_in", w_in_d, wb_in, D, DPROJ, 1156, 256)
    cast("w_pool", w_pool_d, wb_pool, D, 256, 256, 1024)
    cast("w_so", w_so_d, wb_so, DI, D, 1024, 1024)
    cast("w_o", w_o_d, wb_o, D, D, 1024, 1024)
    cast("w_up", w_up_d, wb_up, D, DFF, 2048, 512)
    cast("w_down", w_down_d, wb_down, DFF, D, 1024, 1024)

    def wait_cast(q, name):
        kn = k.known[q]
        for sg, v in cast_parts[name]:
            if kn.get(sg, 0) < v:
                k.eng[q].wait_ge(sg.sem, v)
                kn[sg] = v

    k.dma("sp", ld_ring, colp[:], colp_d, writes=[colp])
    k.dma("sp", ld_ring, rhead[:], rhead_d, writes=[rhead])
    k.dma("sp", ld_ring, ccol[:], ccol_d, writes=[ccol])
    k.dma("pool", st_ring, gsn_bc[:], rgsn_d, writes=[gsn_bc])
    k.op("act", lambda eng: eng.activation(out=arow[:], in_=rhead[:, 32:64], func=AF.Exp), reads=[rhead], writes=[arow])
    k.op("dve", lambda eng: eng.tensor_scalar(out=arow[:], in0=arow[:], scalar1=-1.0, scalar2=None, op0=ALU.mult),
         reads=[arow], writes=[arow])
    iot = small.take()
    k.op("pool", lambda eng: eng.iota(iot[:, 0:16], pattern=[[1, 16]], base=1, channel_multiplier=0,
                                      allow_small_or_imprecise_dtypes=True), writes=[iot])
    for gi, w in enumerate((2, 4, 8, 16)):
        t_ = small.take()
        k.op("dve", lambda eng, t_=t_, w=w: eng.tensor_scalar(out=t_[:, 0:16], in0=iot[:, 0:16], scalar1=float(w), scalar2=None,
                                                              op0=ALU.min), reads=[iot], writes=[t_])
        k.op("dve", lambda eng, t_=t_, gi=gi: eng.reciprocal(out=icnt[:, gi, :], in_=t_[:, 0:16]), reads=[t_], writes=[icnt])

    k.op("act", lambda eng: eng.activation(out=scol[:], in_=ccol[:], func=AF.Silu), reads=[ccol], writes=[scol])
    scol_b = k.sb([128, 8, 2], BF16, "scol_b")
    k.op("dve", lambda eng: eng.tensor_copy(out=scol_b[:], in_=scol[:]), reads=[scol], writes=[scol_b])
    screp = []
    for s in range(2):
        v = xs_tok[s].t[:, 0:1024].rearrange("p (a b) -> p a b", b=128)
        screp.append(v)
        k.op("dve", lambda eng, s=s, v=v: eng.tensor_copy(out=v, in_=scol[:, :, s:s + 1].broadcast_to([128, 8, 128])),
             reads=[scol], writes=[xs_tok[s]])
    ps_col = psx
    col_chunks = list(range(0, 16)) + list(range(24, 40))
    colidx = {ch: mi for mi, ch in enumerate(col_chunks)}
    wb_ada_v = wb_ada.rearrange("(kc p) c -> p kc c", p=128)
    gload = {0: (f32tmp.bufs[0], f32tmp.bufs[1]), 1: (xbuf.bufs[0], xbuf.bufs[1])}
    for which in range(2):
        rbt, rgt = gload[which]
        k.dma("sp", ld_ring, rbt[:], rbada_d[:, which * D:(which + 1) * D], writes=[rbt])
        k.dma("sp", ld_ring, rgt[:], rgpost_d[:, which * D:(which + 1) * D], writes=[rgt])
    wait_cast("sp", "w_ada")
    for blk in range(12):
        wsl = wslots.take()
        wv = wsl.t[:, 0:4096].rearrange("p (a b) -> p a b", b=512)
        k.dma("sp", ld_ring, wv, wb_ada_v[:, :, blk * 512:(blk + 1) * 512], writes=[wsl])
        if blk * 4 in colidx:
            for cc in range(4):
                mi = colidx[blk * 4 + cc]
                for kc in range(8):
                    fn = lambda eng, cc=cc, mi=mi, kc=kc: eng.matmul(ps_col[:, mi * 2:mi * 2 + 2], lhsT=wv[:, kc, cc * 128:(cc + 1) * 128],
                                                                     rhs=scol_b[:, kc, :], start=(kc == 0), stop=(kc == 7))
                    if kc < 7:
                        k.pe_quiet(fn, reads=[wsl, scol_b], writes=[ps_col])
                    else:
                        k.op("pe", fn, reads=[wsl, scol_b], writes=[ps_col])
        else:
            which, q = (0, blk - 4) if blk < 6 else (1, blk - 10)
            rbt, rgt = gload[which]
            for s in range(2):
                psr = psf.take()
                for kc in range(8):
                    fn = lambda eng, s=s, kc=kc, psr=psr: eng.matmul(psr[:, 0:512], lhsT=screp[s][:, kc, :], rhs=wv[:, kc, :],
                                                                     start=(kc == 0), stop=(kc == 7))
                    if kc < 7:
                        k.pe_quiet(fn, reads=[wsl, xs_tok[s]], writes=[psr])
                    else:
                        k.op("pe", fn, reads=[wsl, xs_tok[s]], writes=[psr])
                if s == 0:
                    Gb = G1 if which == 0 else G2
                    dst = Gb[:, q * 512:(q + 1) * 512]
                    wr = [Gb]
                else:
                    dst = hst[:, which * D + q * 512:which * D + (q + 1) * 512]
                    wr = hst_res
                k.op("dve", lambda eng, psr=psr, dst=dst, q=q, rbt=rbt: eng.tensor_tensor(out=dst, in0=psr[:, 0:512],
                                                                                          in1=rbt[:, q * 512:(q + 1) * 512], op=ALU.add),
                     reads=[psr, rbt], writes=wr)
                k.op("dve", lambda eng, dst=dst, q=q, rgt=rgt: eng.tensor_tensor(out=dst, in0=dst, in1=rgt[:, q * 512:(q + 1) * 512], op=ALU.mult),
                     reads=wr + [rgt], writes=wr)
    k.dma("sp", ld_ring, gs_scr.rearrange("w p d -> p w d"), hst[:].rearrange("p (w d) -> p w d", w=2), reads=hst_res, writes=[gs_res])
    for qi, (c0, b0) in enumerate(((0, 0), (8, 8), (16, 24), (24, 32))):
        k.op("dve", lambda eng, c0=c0, b0=b0: eng.tensor_tensor(
            out=modc[:, c0:c0 + 8, :], in0=ps_col[:, c0 * 2:(c0 + 8) * 2].rearrange("p (a b) -> p a b", b=2),
            in1=colp[:, C_BADA + b0:C_BADA + b0 + 8].unsqueeze(2).broadcast_to([128, 8, 2]), op=ALU.add),
            reads=[ps_col, colp], writes=[modc])
    for (Aout, Bout, sh0, sc0, gcol) in ((A1, B1, 0, 8, C_GPM), (A2, B2, 16, 24, C_GPL)):
        k.op("dve", lambda eng, Aout=Aout, sc0=sc0: eng.tensor_scalar(out=Aout[:], in0=modc[:, sc0:sc0 + 8, :], scalar1=1.0, scalar2=None,
                                                                       op0=ALU.add), reads=[modc], writes=[Aout])
        k.op("dve", lambda eng, Aout=Aout, gcol=gcol: eng.tensor_tensor(out=Aout[:], in0=Aout[:],
                                                                         in1=colp[:, gcol:gcol + 8].unsqueeze(2).broadcast_to([128, 8, 2]),
                                                                         op=ALU.mult), reads=[Aout, colp], writes=[Aout])
        k.op("dve", lambda eng, Bout=Bout, sh0=sh0: eng.tensor_copy(out=Bout[:], in_=modc[:, sh0:sh0 + 8, :]), reads=[modc], writes=[Bout])

    wait_cast("sp", "w_pool")
    k.dma("sp", ld_ring, wpool[:], wb_pool.rearrange("(a p) c -> p a c", p=128), writes=[wpool])

    wb_in_v = wb_in.rearrange("(kc p) c -> p kc c", p=128)
    wb_so_v = wb_so.rearrange("(kc p) c -> p kc c", p=128)
    wb_o_v = wb_o.rearrange("(kc p) c -> p kc c", p=128)
    wb_up_v = wb_up.rearrange("(kc p) c -> p kc c", p=128)
    wb_down_v = wb_down.rearrange("(kc p) c -> p kc c", p=128)

    Zb, MIDb, ENDb = [], [], []
    for cb in range(4):
        Zb.append(("w_in", wb_in_v[:, :, cb * 512:(cb + 1) * 512], 8, 512))
    for cb in range(8):
        MIDb.append(("w_in", wb_in_v[:, :, 2048 + cb * 512:2048 + (cb + 1) * 512], 8, 512))
    MIDb.append(("w_in", wb_in_v[:, :, 6144:6176], 8, 32))
    for cb in range(2):
        MIDb.append(("w_in", wb_in_v[:, :, 6176 + cb * 512:6176 + (cb + 1) * 512], 8, 512))
    for cb in range(4):
        MIDb.append(("w_in", wb_in_v[:, :, 7200 + cb * 512:7200 + (cb + 1) * 512], 8, 512))
    for cb in range(4):
        MIDb.append(("w_so", wb_so_v[:, :, cb * 256:(cb + 1) * 256], 16, 256))
    for cb in range(2):
        MIDb.append(("w_o", wb_o_v[:, :, cb * 512:(cb + 1) * 512], 8, 512))
    for cb in range(8):
        ENDb.append(("w_up", wb_up_v[:, :, cb * 512:(cb + 1) * 512], 8, 512))
    for cb in range(4):
        for kh in range(2):
            ENDb.append(("w_down", wb_down_v[:, kh * 16:(kh + 1) * 16, cb * 256:(cb + 1) * 256], 16, 256))
    n_pass = n_ptiles + (1 if do_sample else 0)
    wseq = list(Zb)
    for n_ in range(n_pass):
        wseq += MIDb
        if n_ + 1 < n_pass:
            wseq += Zb
        wseq += ENDb
    wstate = {"issued": 0, "next": 0, "bufs": {}, "seen": set()}
    total_blocks = len(wseq)

    def w_issue(upto):
        while wstate["issued"] < min(upto, total_blocks):
            i = wstate["issued"]
            name, src, nk, ncol = wseq[i]
            if name not in wstate["seen"]:
                wstate["seen"].add(name)
                wait_cast("sp", name)
            sl = wslots.take()
            dst = sl[:, 0:nk * ncol].rearrange("p (a b) -> p a b", b=ncol)
            k.dma("sp", ld_ring, dst, src, writes=[sl])
            wstate["bufs"][i] = (sl, nk, ncol)
            wstate["issued"] += 1

    def wnext():
        i = wstate["next"]
        wstate["next"] += 1
        w_issue(i + NWSLOT)
        sl, nk, ncol = wstate["bufs"].pop(i)
        view = sl[:, 0:nk * ncol].rearrange("p (a b) -> p a b", b=ncol)
        return sl, view

    def mm_group(out_ap, pairs, reads, psbuf):
        n = len(pairs)
        for i, (l, r) in enumerate(pairs):
            fn = lambda eng, l=l, r=r, i=i: eng.matmul(out_ap, lhsT=l, rhs=r, start=(i == 0), stop=(i == n - 1))
            if i < n - 1:
                k.pe_quiet(fn, reads=reads, writes=[psbuf])
            else:
                k.op("pe", fn, reads=reads, writes=[psbuf])

    def rstd_from_ss(ss_ap, ss_buf, n_feat, width, nt):
        ln_ = small.take()
        k.op("act", lambda eng: eng.activation(out=ln_[0:nt, 0:width], in_=ss_ap, func=AF.Ln, bias=EPS, scale=1.0 / n_feat),
             reads=[ss_buf], writes=[ln_])
        r_ = small.take()
        k.op("act", lambda eng: eng.activation(out=r_[0:nt, 0:width], in_=ln_[0:nt, 0:width], func=AF.Exp, scale=-0.5),
             reads=[ln_], writes=[r_])
        return r_

    def norm_phase1(xbs_, subs_):
        n = len(subs_)
        sss, rs, xns = [], [], []
        for j in range(n):
            nt = subs_[j]
            ss = small.take()
            junk_act = junk_r.take()
            k.op("act", lambda eng: eng.activation(out=junk_act[0:nt, :], in_=xbs_[j][0:nt, :], func=AF.Square, accum_out=ss[0:nt, 0:1]),
                 reads=[xbs_[j]], writes=[junk_act, ss])
            sss.append(ss)
        for j in range(n):
            nt = subs_[j]
            rs.append(rstd_from_ss(sss[j][0:nt, 0:1], sss[j], D, 1, nt))
        for j in range(n):
            nt = subs_[j]
            xn = xnbuf.take()
            k.op("dve", lambda eng: eng.tensor_scalar(out=xn[0:nt, :], in0=xbs_[j][0:nt, :], scalar1=rs[j][0:nt, 0:1], scalar2=None, op0=ALU.mult),
                 reads=[xbs_[j], rs[j]], writes=[xn])
            xns.append(xn)
        return xns

    def norm_phase2(xns, subs_, offs_, Acol, Bcol, s, dstT):
        n = len(subs_)
        pbs_ = []
        for j in range(n):
            nt = subs_[j]
            pb = psf.take()
            pv = psb_view(pb)
            for kc in range(8):
                k.op("pe", lambda eng: eng.transpose(out=pv[:, kc * 128:kc * 128 + nt], in_=xns[j][0:nt, kc * 128:(kc + 1) * 128],
                                                     identity=ident_b[0:nt, 0:nt]), reads=[xns[j], ident_b], writes=[pb])
            pbs_.append(pb)
        for j in range(n):
            nt = subs_[j]
            pv = psb_view(pbs_[j])
            c0 = offs_[j]
            for kc in range(8):
                k.op("dve", lambda eng: eng.tensor_scalar(out=dstT[kc][:, c0:c0 + nt], in0=pv[:, kc * 128:kc * 128 + nt],
                                                          scalar1=Acol[:, kc, s:s + 1], scalar2=Bcol[:, kc, s:s + 1],
                                                          op0=ALU.mult, op1=ALU.add),
                     reads=[pbs_[j], Acol, Bcol], writes=[dstT[kc]])

    def build_dg_block(cb):
        if cb in dg_ready:
            return
        dgs = []
        for cc in range(4):
            c = cb * 4 + cc
            dg = dgring.take()
            k.op("dve", lambda eng: eng.tensor_tensor(out=dg[:, :, :], in0=ident_b[:, :].unsqueeze(1).broadcast_to([128, 4, 128]),
                                                      in1=colp[:, C_CW + c * 4:C_CW + c * 4 + 4].unsqueeze(2).broadcast_to([128, 4, 128]),
                                                      op=ALU.mult), reads=[ident_b, colp], writes=[dg])
            dgs.append(dg)
        dg_ready[cb] = dgs

    def prep_load(sp):
        subs = sp["subs"]
        offs = [sum(subs[:j]) for j in range(len(subs))]
        sp["xbs"] = []
        for j, nt in enumerate(subs):
            xb = xbuf.take()
            k.dma("sp", ld_ring, xb[0:nt, :], sp["x_rows"][offs[j]:offs[j] + nt, :], writes=[xb])
            sp["xbs"].append(xb)

    def prep_ew(sp):
        sp["xns"] = norm_phase1(sp["xbs"], sp["subs"])

    def prep_pe(sp):
        subs = sp["subs"]
        offs = [sum(subs[:j]) for j in range(len(subs))]
        sp["uT"] = uT_sets.take()
        norm_phase2(sp["xns"], subs, offs, A1, B1, sp["s"], sp["uT"])

    def zproj(sp):
        subs = sp["subs"]
        offs = [sum(subs[:j]) for j in range(len(subs))]
        uT = sp["uT"]
        for cb in range(4):
            sl, W = wnext()
            for j, nt in enumerate(subs):
                ps = psf.take()
                mm_group(ps[0:nt, 0:512], [(uT[kc][:, offs[j]:offs[j] + nt], W[:, kc, :]) for kc in range(8)], [sl] + uT, ps)
                k.op("act", lambda eng, ps=ps, j=j, nt=nt, cb=cb: eng.activation(out=zs_tok[j][0:nt, cb * 512:(cb + 1) * 512], in_=ps[0:nt, 0:512],
                                                                                  func=AF.Silu), reads=[ps], writes=[zs_tok[j]])
        sp["ready"] = True

    def run_tile(sp, nxt_sp):
        s, y_rows, subs, first, last, pos0_is_zero = sp["s"], sp["y_rows"], sp["subs"], sp["first"], sp["last"], sp["pos0"]
        nsub = len(subs)
        TT = sum(subs)
        offs = [sum(subs[:j]) for j in range(nsub)]
        if not sp.get("ready"):
            prep_load(sp)
            prep_ew(sp)
            prep_pe(sp)
            zproj(sp)
        if nxt_sp is not None:
            prep_load(nxt_sp)
        xbs = sp["xbs"]
        uT = sp["uT"]

        BT = [chunk.take() for _ in range(NG)]
        CT = [chunk.take() for _ in range(NG)]
        def emit_transposes(cb_, srcs_):
            for j, nt in enumerate(subs):
                pb = psf.take()
                pv = psb_view(pb)
                for cc in range(4):
                    k.op("pe", lambda eng: eng.transpose(out=pv[0:nt, cc * 128:(cc + 1) * 128], in_=srcs_[cc][:, offs[j]:offs[j] + nt],
                                                         identity=ident_b[:, :]), reads=[srcs_[cc], ident_b], writes=[pb])
                dst = xs_tok[j][0:nt, cb_ * 512:(cb_ + 1) * 512] if cb_ < 4 else B_tok[j][0:nt, (cb_ - 4) * 512:(cb_ - 3) * 512]
                dbuf = xs_tok[j] if cb_ < 4 else B_tok[j]
                k.op("dve", lambda eng: eng.tensor_copy(out=dst, in_=pv[0:nt, 0:512]), reads=[pb], writes=[dbuf])

        def xbc_proj(cb):
            sl, W = wnext()
            pbs = []
            for cc in range(4):
                c = cb * 4 + cc
                ps = psf.take()
                mm_group(ps[:, 0:TT], [(W[:, kc, cc * 128:(cc + 1) * 128], uT[kc][:, 0:TT]) for kc in range(8)], [sl] + uT, ps)
                Pb = pbring.take()
                k.op("pool", lambda eng: eng.tensor_copy(out=Pb[:, 0:3], in_=hist[:, c, :]), reads=[hist], writes=[Pb])
                k.op("act", lambda eng: eng.activation(out=Pb[:, 3:3 + TT], in_=ps[:, 0:TT], func=AF.Copy),
                     reads=[ps], writes=[Pb])
                k.op("pool", lambda eng: eng.tensor_copy(out=hist[:, c, :], in_=Pb[:, TT:TT + 3]), reads=[Pb], writes=[hist])
                if last:
                    k.op("dve", lambda eng: eng.tensor_copy(out=convout[:, c, :], in_=ps[:, TT - 3:TT]),
                         reads=[ps], writes=[convout])
                pbs.append(Pb)
            return pbs

        def xbc_conv(cb, pbs):
            dgs = dg_ready.pop(cb)
            srcs = []
            for cc in range(4):
                c = cb * 4 + cc
                if c < 16:
                    dstb = xf_r.take()
                elif c < 24:
                    dstb = BT[c - 16]
                else:
                    dstb = CT[c - 24]
                ps3 = psf.take()
                mm_group(ps3[:, 0:TT], [(dgs[cc][:, kk, :], pbs[cc][:, kk:kk + TT]) for kk in range(4)], [pbs[cc], dgs[cc]], ps3)
                k.op("act", lambda eng: eng.activation(out=dstb[:, 0:TT], in_=ps3[:, 0:TT], func=AF.Silu,
                                                       bias=colp[:, C_CB + c:C_CB + c + 1]),
                     reads=[ps3, colp], writes=[dstb])
                srcs.append(dstb)
            return srcs

        pend_pb = {}
        pend_src = {}
        for it_ in range(8 + 2):
            if it_ < 8:
                build_dg_block(it_)
                pend_pb[it_] = xbc_proj(it_)
            if 0 <= it_ - 1 < 8:
                pend_src[it_ - 1] = xbc_conv(it_ - 1, pend_pb.pop(it_ - 1))
            if 0 <= it_ - 2 < 6:
                emit_transposes(it_ - 2, pend_src.pop(it_ - 2))
        sl, W = wnext()
        for j, nt in enumerate(subs):
            ps = psf.take()
            mm_group(ps[0:nt, 0:32], [(uT[kc][:, offs[j]:offs[j] + nt], W[:, kc, :]) for kc in range(8)], [sl] + uT, ps)
            t1 = small.take()
            k.op("dve", lambda eng, ps=ps, t1=t1, nt=nt: eng.tensor_tensor(out=t1[0:nt, :], in0=ps[0:nt, 0:32], in1=rhead[0:nt, 0:32], op=ALU.add),
                 reads=[ps, rhead], writes=[t1])
            t2 = small.take()
            k.op("act", lambda eng, t1=t1, t2=t2, nt=nt: eng.activation(out=t2[0:nt, :], in_=t1[0:nt, :], func=AF.Exp), reads=[t1], writes=[t2])
            k.op("act", lambda eng, t2=t2, j=j, nt=nt: eng.activation(out=dtbuf[j][0:nt, :], in_=t2[0:nt, :], func=AF.Ln, bias=1.0),
                 reads=[t2], writes=[dtbuf[j]])
        pmT = [chunk.take() for _ in range(8)]
        for cb in range(2):
            sl, W = wnext()
            for cc in range(4):
                pc = cb * 4 + cc
                gi = pc // 2
                w = (2, 4, 8, 16)[gi]
                ps = psf.take()
                mm_group(ps[:, 0:TT], [(W[:, kc, cc * 128:(cc + 1) * 128], uT[kc][:, 0:TT]) for kc in range(8)], [sl] + uT, ps)
                ppb = pp[pc]
                k.op("act", lambda eng, ps=ps, ppb=ppb: eng.activation(out=ppb[:, 15:15 + TT], in_=ps[:, 0:TT], func=AF.Copy),
                     reads=[ps], writes=[ppb])
                cur = ppb
                step = 1
                lo = 0
                while step < w:
                    nxt = psc.take()
                    lo2 = lo + step
                    k.op("pool", lambda eng, cur=cur, nxt=nxt, lo2=lo2, step=step: eng.tensor_tensor(
                        out=nxt[:, lo2:15 + TT], in0=cur[:, lo2:15 + TT], in1=cur[:, lo2 - step:15 + TT - step], op=ALU.add),
                        reads=[cur], writes=[nxt])
                    cur = nxt
                    lo = lo2
                    step *= 2
                k.op("dve", lambda eng, cur=cur, ppb=ppb, pc=pc, w=w: eng.scalar_tensor_tensor(
                    out=pmT[pc][:, 0:TT], in0=cur[:, 15:15 + TT], scalar=1.0 / w, in1=ppb[:, 15:15 + TT], op0=ALU.mult, op1=ALU.subtract),
                    reads=[cur, ppb], writes=[pmT[pc]])
                if pos0_is_zero:
                    t_ = small.take()
                    k.op("dve", lambda eng, cur=cur, t_=t_, gi=gi: eng.tensor_tensor(out=t_[:, 0:16], in0=cur[:, 15:31], in1=icnt[:, gi, :], op=ALU.mult),
                         reads=[cur, icnt], writes=[t_])
                    k.op("dve", lambda eng, t_=t_, ppb=ppb, pc=pc: eng.tensor_tensor(out=pmT[pc][:, 0:16], in0=t_[:, 0:16], in1=ppb[:, 15:31], op=ALU.subtract),
                         reads=[t_, ppb], writes=[pmT[pc]])
                k.op("pool", lambda eng, ppb=ppb: eng.tensor_copy(out=ppb[:, 0:15], in_=ppb[:, TT:TT + 15]), reads=[ppb], writes=[ppb])
        gT = [chunk.take() for _ in range(16)]
        for cb in range(4):
            sl, W = wnext()
            for cc in range(4):
                gc = cb * 4 + cc
                ps = psf.take()
                mm_group(ps[:, 0:TT], [(W[:, kc, cc * 128:(cc + 1) * 128], uT[kc][:, 0:TT]) for kc in range(8)], [sl] + uT, ps)
                k.op("act", lambda eng, ps=ps, gc=gc: eng.activation(out=gT[gc][:, 0:TT], in_=ps[:, 0:TT], func=AF.Sigmoid),
                     reads=[ps], writes=[gT[gc]])

        chunks = []
        for j, nt in enumerate(subs):
            o = offs[j]
            dtj = dtbuf[j]
            da = small.take()
            k.op("dve", lambda eng: eng.tensor_tensor(out=da[0:nt, :], in0=dtj[0:nt, :], in1=arow[0:nt, :], op=ALU.mult),
                 reads=[dtj, arow], writes=[da])
            pss = psf.take()
            k.op("pe", lambda eng: eng.matmul(pss[0:nt, 0:32], lhsT=U_f[0:nt, 0:nt], rhs=da[0:nt, :], start=True, stop=True),
                 reads=[U_f, da], writes=[pss])
            k.op("pe", lambda eng: eng.matmul(pss[:, 32:64], lhsT=ones_f[0:nt, :], rhs=da[0:nt, :], start=True, stop=True),
                 reads=[ones_f, da], writes=[pss])
            eacs = small.take()
            k.op("act", lambda eng: eng.activation(out=eacs[0:nt, :], in_=pss[0:nt, 0:32], func=AF.Exp), reads=[pss], writes=[eacs])
            cdB = small.take()
            k.op("act", lambda eng: eng.activation(out=cdB[:, :], in_=pss[:, 32:64], func=AF.Exp), reads=[pss], writes=[cdB])
            acs = small.take()
            k.op("act", lambda eng: eng.activation(out=acs[0:nt, :], in_=pss[0:nt, 0:32], func=AF.Copy), reads=[pss], writes=[acs])
            dif = small.take()
            k.op("dve", lambda eng: eng.tensor_tensor(out=dif[0:nt, :], in0=pss[0:nt, 32:64], in1=acs[0:nt, :], op=ALU.subtract),
                 reads=[pss, acs], writes=[dif])
            dte = small.take()
            k.op("act", lambda eng: eng.activation(out=dte[0:nt, :], in_=dif[0:nt, :], func=AF.Exp), reads=[dif], writes=[dte])
            wts = small.take()
            k.op("dve", lambda eng: eng.tensor_tensor(out=wts[0:nt, :], in0=dte[0:nt, :], in1=dtj[0:nt, :], op=ALU.mult),
                 reads=[dte, dtj], writes=[wts])
            chunks.append(dict(nt=nt, o=o, dtj=dtj, da=da, eacs=eacs, cdB=cdB, wts=wts, yz=yz_r.take(), ssg=small.take(), j=j))

        items = [(j, g) for j in range(nsub) for g in range(NG)]
        ist = {}

        def S0(it):
            j, g = it
            c = chunks[j]
            nt = c["nt"]
            Ws = []
            for hh in range(4):
                h = g * 4 + hh
                Wh = W_r.take()
                if hh < 3:
                    k.op("act", lambda eng: eng.activation(out=Wh[0:nt, 0:nt], in_=L_f[0:nt, 0:nt], func=AF.Copy, scale=c["da"][0:nt, h:h + 1]),
                         reads=[L_f, c["da"]], writes=[Wh])
                else:
                    k.op("dve", lambda eng: eng.tensor_scalar(out=Wh[0:nt, 0:nt], in0=L_f[0:nt, 0:nt], scalar1=c["da"][0:nt, h:h + 1],
                                                              scalar2=None, op0=ALU.mult), reads=[L_f, c["da"]], writes=[Wh])
                Ws.append(Wh)
            xg = xg_r.take()
            xs_g = xs_tok[j][0:nt, g * 256:(g + 1) * 256].rearrange("p (h d) -> p h d", d=HD)
            for qi, src in enumerate((c["wts"], rhead, c["dtj"])):
                col0 = 64 if qi == 1 else 0
                k.op("pool",
                     lambda eng: eng.tensor_tensor(out=xg[0:nt, qi, :].rearrange("p (h d) -> p h d", d=HD), in0=xs_g,
                                                   in1=src[0:nt, col0 + g * 4:col0 + (g + 1) * 4].unsqueeze(2).broadcast_to([nt, 4, HD]),
                                                   op=ALU.mult), reads=[xs_tok[j], src], writes=[xg])
            gs_ = slice(g * 256, (g + 1) * 256)
            hres = hst_res[g]
            k.op("pool", lambda eng: eng.tensor_tensor(out=hst[:, gs_].rearrange("p (h d) -> p h d", d=HD),
                                                       in0=hst[:, gs_].rearrange("p (h d) -> p h d", d=HD),
                                                       in1=c["cdB"][:, g * 4:(g + 1) * 4].unsqueeze(2).broadcast_to([128, 4, HD]), op=ALU.mult),
                 reads=[hres, c["cdB"]], writes=[hres])
            ist[it] = dict(W=Ws, xg=xg)

        def S1a(it):
            j, g = it
            c = chunks[j]
            nt, o = c["nt"], c["o"]
            pcb = ps_cb.take()
            k.op("pe", lambda eng: eng.matmul(pcb[0:nt, 0:nt], lhsT=BT[g][:, o:o + nt], rhs=CT[g][:, o:o + nt], start=True, stop=True),
                 reads=[BT[g], CT[g]], writes=[pcb])
            cbm = cbm_r.take()
            k.op("dve", lambda eng: eng.tensor_tensor(out=cbm[0:nt, 0:nt], in0=pcb[0:nt, 0:nt], in1=U_f[0:nt, 0:nt], op=ALU.mult),
                 reads=[pcb, U_f], writes=[cbm])
            ist[it].update(cbm=cbm)

        def S1b(it):
            j, g = it
            c = chunks[j]
            nt = c["nt"]
            pseg = ps_seg.take()
            for hh in range(4):
                Wh = ist[it]["W"][hh]
                k.op("pe", lambda eng: eng.matmul(pseg[0:nt, hh * 128:hh * 128 + nt], lhsT=Wh[0:nt, 0:nt], rhs=U_f[0:nt, 0:nt],
                                                  start=True, stop=True), reads=[Wh, U_f], writes=[pseg])
            ist[it].update(pseg=pseg)

        def S2a(it):
            j, g = it
            nt = chunks[j]["nt"]
            pseg = ist[it]["pseg"]
            segv = pseg[0:nt, :].rearrange("p (h l) -> p h l", l=128)[:, :, 0:nt]
            k.op("act", lambda eng: eng.activation(out=segv, in_=segv, func=AF.Exp), reads=[pseg], writes=[pseg])

        def S2b(it):
            j, g = it
            nt = chunks[j]["nt"]
            pseg, cbm = ist[it]["pseg"], ist[it]["cbm"]
            segv = pseg[0:nt, :].rearrange("p (h l) -> p h l", l=128)[:, :, 0:nt]
            Mg = M_r.take()
            k.op("dve", lambda eng: eng.tensor_tensor(out=Mg[0:nt, :, 0:nt], in0=segv,
                                                      in1=cbm[0:nt, 0:nt].unsqueeze(1).broadcast_to([nt, 4, nt]), op=ALU.mult),
                 reads=[pseg, cbm], writes=[Mg])
            ist[it].update(Mg=Mg)

        def S3(it):
            j, g = it
            c = chunks[j]
            nt, o = c["nt"], c["o"]
            Mg, xg = ist[it]["Mg"], ist[it]["xg"]
            yz, ssg, eacs, cdB = c["yz"], c["ssg"], c["eacs"], c["cdB"]
            gs = slice(g * 256, (g + 1) * 256)
            py = ps_y.take()
            k.pe_quiet(lambda eng: eng.matmul(py[0:nt, 0:256], lhsT=ident_b[0:nt, 0:nt], rhs=xg[0:nt, 1, :], start=True, stop=False),
                       reads=[ident_b, xg], writes=[py])
            for hh in range(4):
                fn = lambda eng: eng.matmul(py[0:nt, hh * 64:(hh + 1) * 64], lhsT=Mg[0:nt, hh, 0:nt], rhs=xg[0:nt, 2, hh * 64:(hh + 1) * 64],
                                            start=False, stop=(hh == 3))
                if hh < 3:
                    k.pe_quiet(fn, reads=[Mg, xg], writes=[py])
                else:
                    k.op("pe", fn, reads=[Mg, xg], writes=[py])
            k.op("pe", lambda eng: eng.matmul(py[0:nt, 256:512], lhsT=CT[g][:, o:o + nt], rhs=hbf[g][:, :], start=True, stop=True),
                 reads=[CT[g], hbf[g]], writes=[py])
            pst = ps_st.take()
            k.op("pe", lambda eng: eng.matmul(pst[:, 0:256], lhsT=B_tok[j][0:nt, g * 128:(g + 1) * 128], rhs=xg[0:nt, 0, :], start=True, stop=True),
                 reads=[B_tok[j], xg], writes=[pst])
            t2 = tmp256.take()
            k.op("dve", lambda eng: eng.tensor_tensor(out=t2[0:nt, :].rearrange("p (h d) -> p h d", d=HD),
                                                      in0=py[0:nt, 256:512].rearrange("p (h d) -> p h d", d=HD),
                                                      in1=eacs[0:nt, g * 4:(g + 1) * 4].unsqueeze(2).broadcast_to([nt, 4, HD]), op=ALU.mult),
                 reads=[py, eacs], writes=[t2])
            ybf = ybf_r.take()
            k.op("dve", lambda eng: eng.tensor_tensor(out=ybf[0:nt, :], in0=py[0:nt, 0:256], in1=t2[0:nt, :], op=ALU.add),
                 reads=[py, t2], writes=[ybf])
            k.op("dve", lambda eng: eng.tensor_tensor(out=yz[0:nt, gs], in0=ybf[0:nt, :], in1=zs_tok[j][0:nt, gs], op=ALU.mult),
                 reads=[ybf, zs_tok[j]], writes=[yz.subs[g]])
            ist[it].update(pst=pst)

        def S4a(it):
            j, g = it
            pst = ist[it]["pst"]
            gs = slice(g * 256, (g + 1) * 256)
            hres = hst_res[g]
            k.op("dve", lambda eng: eng.tensor_tensor(out=hst[:, gs], in0=pst[:, 0:256], in1=hst[:, gs], op=ALU.add),
                 reads=[pst, hres], writes=[hres])

        def S4b(it):
            j, g = it
            c = chunks[j]
            nt = c["nt"]
            yz, ssg = c["yz"], c["ssg"]
            gs = slice(g * 256, (g + 1) * 256)
            hres = hst_res[g]
            junk_act = junk_r.take()
            k.op("act", lambda eng: eng.activation(out=junk_act[0:nt, 0:256], in_=yz[0:nt, gs], func=AF.Square, accum_out=ssg[0:nt, g:g + 1]),
                 reads=[yz.subs[g]], writes=[junk_act, ssg])
            k.op("pool", lambda eng: eng.tensor_copy(out=hbf[g][:, :], in_=hst[:, gs]), reads=[hres], writes=[hbf[g]])
            del ist[it]
            if g == NG - 1:
                epilogue(c)

        def epilogue(c):
            nt, o, yz, ssg = c["nt"], c["o"], c["yz"], c["ssg"]
            rg = rstd_from_ss(ssg[0:nt, 0:8], ssg, 256, 8, nt)
            yg = yz
            for g in range(NG):
                gs = slice(g * 256, (g + 1) * 256)
                k.op("dve", lambda eng: eng.scalar_tensor_tensor(out=yg[0:nt, gs], in0=yz[0:nt, gs], scalar=rg[0:nt, g:g + 1],
                                                                 in1=gsn_bc[0:nt, gs], op0=ALU.mult, op1=ALU.mult),
                     reads=[yz.subs[g], rg, gsn_bc], writes=[yz.subs[g]])
            for half in range(2):
                pb = ps_y.take()
                pv = psb_view(pb)
                for q in range(8):
                    fc = half * 8 + q
                    k.op("pe", lambda eng: eng.transpose(out=pv[:, q * 128:q * 128 + nt], in_=yg[0:nt, fc * 128:(fc + 1) * 128],
                                                         identity=ident_b[0:nt, 0:nt]), reads=[yz.subs[fc // 2], ident_b], writes=[pb])
                k.op("act", lambda eng: eng.activation(
                    out=ygT_all[:, half * 8:(half + 1) * 8, o:o + nt],
                    in_=pv.rearrange("p (q t) -> p q t", t=128)[:, :, 0:nt], func=AF.Copy), reads=[pb], writes=[ygT_all])

        n_it = len(items)
        ps_cb = Ring(psf.bufs[0:1])
        ps_seg = Ring(psf.bufs[1:3])
        ps_y = Ring(psf.bufs[3:5])
        ps_st = Ring(psf.bufs[5:8])
        def at(fn, idx):
            if 0 <= idx < n_it:
                fn(items[idx])

        for i in range(n_it + 4):
            at(S2a, i - 2)
            at(S4a, i - 4)
            at(S0, i)
            at(S1a, i - 1)
            at(S2b, i - 2)
            at(S3, i - 3)
            at(S4b, i - 4)
            at(S1b, i - 1)

        if nxt_sp is not None:
            prep_ew(nxt_sp)
        mixT = [chunk.take() for _ in range(8)]
        for cb in range(4):
            sl, W = wnext()
            for dl in range(2):
                dc = cb * 2 + dl
                gi = dc // 2
                ps = psf.take()
                mm_group(ps[:, 0:TT], [(W[:, kc, dl * 128:(dl + 1) * 128], ygT_all[:, kc, 0:TT]) for kc in range(16)], [sl, ygT_all], ps)
                ps2 = psf.take()
                mm_group(ps2[:, 0:TT], [(wpool[:, gi * 2 + kc, (dc % 2) * 128:(dc % 2 + 1) * 128], pmT[gi * 2 + kc][:, 0:TT]) for kc in range(2)],
                         [wpool, pmT[gi * 2], pmT[gi * 2 + 1]], ps2)
                a_ = ab_r.take()
                k.op("dve", lambda eng, ps=ps, a_=a_, dc=dc: eng.tensor_tensor(out=a_[:, 0:TT], in0=ps[:, 0:TT], in1=gT[dc][:, 0:TT], op=ALU.mult),
                     reads=[ps, gT[dc]], writes=[a_])
                b_ = ab_r.take()
                k.op("dve", lambda eng, ps2=ps2, b_=b_, dc=dc: eng.scalar_tensor_tensor(
                    out=b_[:, 0:TT], in0=ps2[:, 0:TT], scalar=colp[:, C_PSC + dc:C_PSC + dc + 1], in1=gT[8 + dc][:, 0:TT],
                    op0=ALU.mult, op1=ALU.mult), reads=[ps2, colp, gT[8 + dc]], writes=[b_])
                k.op("pool", lambda eng, a_=a_, b_=b_, dc=dc: eng.tensor_tensor(out=mixT[dc][:, 0:TT], in0=a_[:, 0:TT], in1=b_[:, 0:TT], op=ALU.add),
                     reads=[a_, b_], writes=[mixT[dc]])
        if nxt_sp is not None:
            prep_pe(nxt_sp)
        pso = [[None, None] for _ in subs]
        for ob in range(2):
            sl, W = wnext()
            for j, nt in enumerate(subs):
                ps = psf.take()
                mm_group(ps[0:nt, 0:512], [(mixT[kc][:, offs[j]:offs[j] + nt], W[:, kc, :]) for kc in range(8)], [sl] + mixT, ps)
                pso[j][ob] = ps
        for j, nt in enumerate(subs):
            ssA = small.take()
            for ob in range(2):
                junk_act = junk_r.take()
                k.op("act", lambda eng, ob=ob: eng.activation(out=junk_act[0:nt, 0:512], in_=pso[j][ob][0:nt, 0:512], func=AF.Square,
                                                              accum_out=ssA[0:nt, ob:ob + 1]), reads=[pso[j][ob]], writes=[junk_act, ssA])
            ss = small.take()
            k.op("dve", lambda eng: eng.tensor_tensor(out=ss[0:nt, 0:1], in0=ssA[0:nt, 0:1], in1=ssA[0:nt, 1:2], op=ALU.add),
                 reads=[ssA], writes=[ss])
            r_ = rstd_from_ss(ss[0:nt, 0:1], ss, D, 1, nt)
            tmp = f32tmp.take()
            for ob in range(2):
                k.op("dve", lambda eng, ob=ob: eng.scalar_tensor_tensor(
                    out=tmp[0:nt, ob * 512:(ob + 1) * 512], in0=pso[j][ob][0:nt, 0:512], scalar=r_[0:nt, 0:1],
                    in1=G1[0:nt, ob * 512:(ob + 1) * 512], op0=ALU.mult, op1=ALU.mult),
                    reads=[pso[j][ob], r_, G1], writes=[tmp])
            k.op("pool", lambda eng: eng.tensor_tensor(out=xbs[j][0:nt, :], in0=xbs[j][0:nt, :], in1=tmp[0:nt, :], op=ALU.add),
                 reads=[xbs[j], tmp], writes=[xbs[j]])

        vxn = norm_phase1(xbs, subs)
        if nxt_sp is not None:
            zproj(nxt_sp)
        vT = [chunk.take() for _ in range(8)]
        norm_phase2(vxn, subs, offs, A2, B2, s, vT)
        hdnT = [chunk.take() for _ in range(32)]
        if nxt_sp is not None:
            build_dg_block(0)
            build_dg_block(1)
        for cb in range(8):
            sl, W = wnext()
            for cc in range(4):
                fc = cb * 4 + cc
                ps = psf.take()
                mm_group(ps[:, 0:TT], [(W[:, kc, cc * 128:(cc + 1) * 128], vT[kc][:, 0:TT]) for kc in range(8)], [sl] + vT, ps)
                rl = relu_r.take()
                k.op("act", lambda eng, ps=ps, rl=rl: eng.activation(out=rl[:, 0:TT], in_=ps[:, 0:TT], func=AF.Relu), reads=[ps], writes=[rl])
                k.op("pool", lambda eng, rl=rl, fc=fc: eng.tensor_tensor(out=hdnT[fc][:, 0:TT], in0=rl[:, 0:TT], in1=rl[:, 0:TT], op=ALU.mult),
                     reads=[rl], writes=[hdnT[fc]])
        dnbuf = [f32tmp.take() for _ in subs]
        for cb in range(4):
            psd = [psf.take() for _ in subs]
            for kh in range(2):
                sl, W = wnext()
                for j, nt in enumerate(subs):
                    for kk in range(16):
                        fc = kh * 16 + kk
                        fn = lambda eng, j=j, nt=nt, kk=kk, fc=fc: eng.matmul(psd[j][0:nt, 0:256], lhsT=hdnT[fc][:, offs[j]:offs[j] + nt], rhs=W[:, kk, :],
                                                                              start=(kh == 0 and kk == 0), stop=(kh == 1 and kk == 15))
                        if kk < 15:
                            k.pe_quiet(fn, reads=[sl, hdnT[fc]], writes=[psd[j]])
                        else:
                            k.op("pe", fn, reads=[sl, hdnT[fc]], writes=[psd[j]])
            for j, nt in enumerate(subs):
                k.op("act", lambda eng, j=j, nt=nt: eng.activation(out=dnbuf[j][0:nt, cb * 256:(cb + 1) * 256], in_=psd[j][0:nt, 0:256], func=AF.Copy),
                     reads=[psd[j]], writes=[dnbuf[j]])
        for j, nt in enumerate(subs):
            ss = small.take()
            junk_act = junk_r.take()
            k.op("act", lambda eng: eng.activation(out=junk_act[0:nt, :], in_=dnbuf[j][0:nt, :], func=AF.Square, accum_out=ss[0:nt, 0:1]),
                 reads=[dnbuf[j]], writes=[junk_act, ss])
            r_ = rstd_from_ss(ss[0:nt, 0:1], ss, D, 1, nt)
            k.op("dve", lambda eng: eng.scalar_tensor_tensor(out=dnbuf[j][0:nt, :], in0=dnbuf[j][0:nt, :], scalar=r_[0:nt, 0:1], in1=G2[0:nt, :],
                                                             op0=ALU.mult, op1=ALU.mult), reads=[dnbuf[j], r_, G2], writes=[dnbuf[j]])
            k.op("dve", lambda eng: eng.tensor_tensor(out=xbs[j][0:nt, :], in0=xbs[j][0:nt, :], in1=dnbuf[j][0:nt, :], op=ALU.add),
                 reads=[xbs[j], dnbuf[j]], writes=[xbs[j]])
            k.dma("sp", sst_ring, y_rows[offs[j]:offs[j] + nt, :], xbs[j][0:nt, :], reads=[xbs[j]])

    def init_state(zero, s):
        if zero:
            for g in range(NG):
                k.op("pool", lambda eng, g=g: eng.memset(hst[:, g * 256:(g + 1) * 256], 0.0), writes=[hst_res[g]])
                k.op("pool", lambda eng, g=g: eng.memset(hbf[g][:, :], 0.0), writes=[hbf[g]])
            k.op("pool", lambda eng: eng.memset(hist[:], 0.0), writes=[hist])
            for pc in range(8):
                k.op("pool", lambda eng, pc=pc: eng.memset(pp[pc][:, 0:15], 0.0), writes=[pp[pc]])
        else:
            k.dma("sp", ld_ring, hst[:], h0T_d, writes=hst_res)
            for g in range(NG):
                k.op("act", lambda eng, g=g: eng.activation(out=hbf[g][:, :], in_=hst[:, g * 256:(g + 1) * 256], func=AF.Copy),
                     reads=[hst_res[g]], writes=[hbf[g]])
            k.dma("sp", ld_ring, convout[:], convT_d, writes=[convout])
            k.op("dve", lambda eng: eng.tensor_copy(out=hist[:], in_=convout[:]), reads=[convout], writes=[hist])
            for pc in range(8):
                k.dma("sp", ld_ring, pp[pc][:, 0:15], poolT_d[:, pc, :], writes=[pp[pc]])

    def store_state(hT_out, conv_out_d, pool_out_d):
        k.dma("pool", st_ring, hT_out, hst[:], reads=hst_res)
        k.dma("pool", st_ring, conv_out_d, convout[:], reads=[convout])
        for pc in range(8):
            k.dma("pool", st_ring, pool_out_d[:, pc, :], pp[pc][:, 0:15], reads=[pp[pc]])

    psf.bufs.append(psx)
    specs = []
    for ti in range(n_ptiles):
        specs.append(dict(s=0, x_rows=xp_d[ti * TT_MAX:(ti + 1) * TT_MAX, :], y_rows=yp_d[ti * TT_MAX:(ti + 1) * TT_MAX, :],
                          subs=[TSUB] * NSUB, first=(ti == 0), last=(ti == n_ptiles - 1), pos0=(ti == 0)))
    if do_sample:
        specs.append(dict(s=1, x_rows=xs_d, y_rows=ys_d, subs=[DEC], first=True, last=True, pos0=False))
    init_state(True, 0)
    for ti in range(n_ptiles):
        run_tile(specs[ti], specs[ti + 1] if ti + 1 < len(specs) else None)
    store_state(hTp_d, convp_d, poolp_d)
    if do_sample:
        k.dma("sp", ld_ring, G1[:], gs_scr[0], reads=[gs_res], writes=[G1])
        k.dma("sp", ld_ring, G2[:], gs_scr[1], reads=[gs_res], writes=[G2])
        init_state(False, 1)
        run_tile(specs[-1], None)
        store_state(hTs_d, convs_d, pools_d)
    k.finish()
    es.close()
    return nc


def make_in_maps(inputs, n_ptiles=SEQ // TT_MAX):
    f = lambda a: np.ascontiguousarray(np.asarray(a, dtype=np.float32))
    ntok = n_ptiles * TT_MAX

    def col(v, n):
        return f(v).reshape(n, 128).T

    colp = np.zeros((128, NCOL), np.float32)
    colp[:, C_BADA:C_BADA + 48] = col(inputs["b_ada"][0], 48)
    colp[:, C_GPM:C_GPM + 8] = col(inputs["g_pre_mix"][0], 8)
    colp[:, C_GPL:C_GPL + 8] = col(inputs["g_pre_mlp"][0], 8)
    cw = f(inputs["conv_w"][0])
    colp[:, C_CW:C_CW + 128] = cw.reshape(4, 32, 128).transpose(2, 1, 0).reshape(128, 128)
    colp[:, C_CB:C_CB + 32] = col(inputs["conv_b"][0], 32)
    colp[:, C_GSN:C_GSN + 16] = col(inputs["g_ssd_norm"][0], 16)
    colp[:, C_PSC:C_PSC + 8] = col(inputs["pool_scale"][0], 8)
    b_ada = f(inputs["b_ada"][0])
    rbada = np.broadcast_to(np.concatenate([b_ada[2048:3072], b_ada[5120:6144]])[None, :], (128, 2048))
    rgpost = np.broadcast_to(np.concatenate([f(inputs["g_post_mix"][0]), f(inputs["g_post_mlp"][0])])[None, :], (128, 2048))
    rhead = np.broadcast_to(np.concatenate([f(inputs["dt_bias"][0]), f(inputs["a_log"][0]), f(inputs["d_skip"][0])])[None, :], (128, 96))
    rgsn = np.broadcast_to(f(inputs["g_ssd_norm"][0])[None, :], (128, DI))
    shared = dict(
        colp=f(colp), rbada=f(rbada), rgpost=f(rgpost), rhead=f(rhead), rgsn=f(rgsn),
        w_ada=f(inputs["w_ada"][0]), w_in=f(inputs["w_in"][0]), w_so=f(inputs["w_ssd_out"][0]),
        w_pool=f(inputs["w_pool_group"][0]).reshape(1024, 256), w_o=f(inputs["w_o"][0]),
        w_up=f(inputs["w_up"][0]), w_down=f(inputs["w_down"][0]),
    )
    maps = []
    for i in range(8):
        m = dict(shared)
        m["xp"] = f(inputs["x_prompt"][i][:ntok])
        m["xs"] = f(inputs["x_sample"][i])
        cc = np.stack([col(inputs["c_prompt"][i], 8), col(inputs["c_sample"][i], 8)], axis=-1)
        m["ccol"] = f(cc)
        m["h0T"] = f(np.asarray(inputs["state_ssm"][0, i]).reshape(NH * HD, NS).T)
        m["convT"] = f(np.asarray(inputs["state_conv"][0, i]).reshape(3, 32, 128).transpose(2, 1, 0))
        m["poolT"] = f(np.asarray(inputs["state_pool"][0, i]).reshape(15, 8, 128).transpose(2, 1, 0))
        maps.append(m)
    return maps


def gather(results, n_ptiles=SEQ // TT_MAX):
    ntok = n_ptiles * TT_MAX
    yp = np.stack([r["yp"] for r in results]).reshape(8, ntok, D)
    ys = np.stack([r["ys"] for r in results]).reshape(8, DEC, D)

    def ssm(key):
        return np.stack([np.asarray(r[key]).reshape(128, NH, HD).transpose(1, 2, 0) for r in results])[None]

    def conv(key):
        return np.stack([np.asarray(r[key]).reshape(128, 32, 3).transpose(2, 1, 0).reshape(3, CONVD) for r in results])[None]

    def pool(key):
        return np.stack([np.asarray(r[key]).reshape(128, 8, 15).transpose(2, 1, 0).reshape(15, D) for r in results])[None]

    outs = (yp, ys, ssm("hTp"), conv("convp"), pool("poolp"), ssm("hTs"), conv("convs"), pool("pools"))
    return tuple(np.ascontiguousarray(o, dtype=np.float32) for o in outs)


_NC_CACHE = {}


def kernel(**inputs):
    if "nc" not in _NC_CACHE:
        _NC_CACHE["nc"] = build_nc()
    nc = _NC_CACHE["nc"]
    in_maps = make_in_maps(inputs)
    res = run_bass_kernel_spmd(nc, in_maps, core_ids=list(range(8)))
    return gather(res.results)
```

```python
import numpy as np
from contextlib import ExitStack
import concourse.bass as bass
import concourse.mybir as mybir
from concourse.bass_utils import run_bass_kernel_spmd

F32 = mybir.dt.float32
BF16 = mybir.dt.bfloat16
AF = mybir.ActivationFunctionType
ALU = mybir.AluOpType

D = 1024
SEQ = 4096
DEC = 16
DI = 2048
NH = 32
HD = 64
NG = 8
NS = 128
CONVD = 4096
DPROJ = 9248
DFF = 4096
EPS = 1e-6
PAST = 2048
TSUB = 128
NSUB = 2
TT_MAX = TSUB * NSUB
WSLOT_ELEMS = 4096
NWSLOT = 3

C_BADA, C_GPM, C_GPL, C_CW, C_CB, C_GSN, C_PSC, NCOL = 0, 48, 56, 64, 192, 224, 240, 248


class Res:
    __slots__ = ("w", "r")

    def __init__(self):
        self.w = None
        self.r = {}


class Sig:
    def __init__(self, sem, name):
        self.sem = sem
        self.cnt = 0
        self.name = name


class Buf:
    def __init__(self, t, res=None, psum=False):
        self.t = t
        self.res = res if res is not None else Res()
        self.psum = psum

    def __getitem__(self, idx):
        return self.t[idx]


class Ring:
    def __init__(self, bufs):
        self.bufs = bufs
        self.i = 0

    def take(self):
        b = self.bufs[self.i % len(self.bufs)]
        self.i += 1
        return b


class K:
    def __init__(self, nc, es):
        self.nc = nc
        self.es = es
        self.eng = {"pe": nc.tensor, "act": nc.scalar, "dve": nc.vector, "pool": nc.gpsimd, "sp": nc.sync}
        self.sig = {}
        for e in self.eng:
            self.sig[e] = Sig(es.enter_context(nc.semaphore("s_" + e)), e)
        self.known = {e: {} for e in self.eng}
        self.nbuf = 0
        self.dsigs = {}
        self.tag = None

    def sb(self, shape, dt, name=None):
        self.nbuf += 1
        t = self.es.enter_context(self.nc.sbuf_tensor("s_" + (name or f"sb{self.nbuf}"), list(shape), dt))
        return Buf(t)

    def ps(self, shape, dt, name=None):
        self.nbuf += 1
        t = self.es.enter_context(self.nc.psum_tensor(name or f"ps{self.nbuf}", list(shape), dt))
        return Buf(t, psum=True)

    def ring(self, n, shape, dt, name):
        return Ring([self.sb(shape, dt, f"{name}{i}") for i in range(n)])

    def dsig_ring(self, n, name):
        sigs = [Sig(self.es.enter_context(self.nc.semaphore(f"d_{name}{i}")), f"{name}{i}") for i in range(n)]
        self.dsigs[name] = sigs
        return Ring(sigs)

    def _waits(self, e, reads, writes):
        needs = {}

        def add(sigv, same_ok):
            s, v = sigv
            if s is self.sig[e] and not same_ok and e == "pe":
                return
            if needs.get(s, 0) < v:
                needs[s] = v

        for b in reads:
            if b.res.w is not None:
                add(b.res.w, True)
            if b.psum:
                for s, v in b.res.r.items():
                    if s is not self.sig[e]:
                        add((s, v), True)
        for b in writes:
            if b.res.w is not None:
                add(b.res.w, False)
            for s, v in b.res.r.items():
                add((s, v), False)
        kn = self.known[e]
        eng = self.eng[e]
        for s, v in needs.items():
            if kn.get(s, 0) >= v:
                continue
            eng.wait_ge(s.sem, v)
            kn[s] = v

    def op(self, e, fn, reads=(), writes=()):
        reads = [b for b in reads if b is not None]
        writes = [b for b in writes if b is not None]
        self._waits(e, reads, writes)
        inst = fn(self.eng[e])
        sg = self.sig[e]
        sg.cnt += 1
        inst.then_inc(sg.sem, 1)
        for b in reads:
            b.res.r[sg] = sg.cnt
        for b in writes:
            b.res.w = (sg, sg.cnt)
            b.res.r = {}
        return inst

    def pe_quiet(self, fn, reads=(), writes=()):
        reads = [b for b in reads if b is not None]
        writes = [b for b in writes if b is not None]
        self._waits("pe", reads, writes)
        fn(self.eng["pe"])
        sg = self.sig["pe"]
        nxt = sg.cnt + 1
        for b in reads:
            b.res.r[sg] = nxt
        for b in writes:
            b.res.w = (sg, nxt)
            b.res.r = {}

    def dma(self, q, ring, out, in_, reads=(), writes=()):
        reads = [b for b in reads if b is not None]
        writes = [b for b in writes if b is not None]
        self._waits(q, reads, writes)
        sg = ring.take()
        kn = self.known[q]
        if kn.get(sg, 0) < sg.cnt:
            self.eng[q].wait_ge(sg.sem, sg.cnt)
            kn[sg] = sg.cnt
        inst = self.eng[q].dma_start(out=out, in_=in_)
        sg.cnt += 16
        inst.then_inc(sg.sem, 16)
        for b in reads:
            b.res.r[sg] = sg.cnt
        for b in writes:
            b.res.w = (sg, sg.cnt)
            b.res.r = {}

    def finish(self):
        sp = self.eng["sp"]
        for sigs in self.dsigs.values():
            for sg in sigs:
                if sg.cnt > 0:
                    sp.wait_ge(sg.sem, sg.cnt)
        for e, sg in self.sig.items():
            if e != "sp" and sg.cnt > 0:
                sp.wait_ge(sg.sem, sg.cnt)


def build_nc(n_ptiles=SEQ // TT_MAX, do_sample=True):
    nc = bass.Bass("TRN2", target_bir_lowering=False)
    es = ExitStack()
    k = K(nc, es)
    NP_TOK = n_ptiles * TT_MAX

    def din(name, shape, dt=F32):
        return nc.dram_tensor(name, list(shape), dt, kind="ExternalInput").ap()

    def dout(name, shape, dt=F32):
        return nc.dram_tensor(name, list(shape), dt, kind="ExternalOutput").ap()

    def dscr(name, shape, dt=BF16):
        return nc.dram_tensor(name, list(shape), dt, kind="Internal").ap()

    xp_d = din("xp", [NP_TOK, D])
    xs_d = din("xs", [DEC, D])
    ccol_d = din("ccol", [128, 8, 2])
    colp_d = din("colp", [128, NCOL])
    rbada_d = din("rbada", [128, 2048])
    rgpost_d = din("rgpost", [128, 2048])
    rhead_d = din("rhead", [128, 96])
    rgsn_d = din("rgsn", [128, DI])
    h0T_d = din("h0T", [128, DI])
    convT_d = din("convT", [128, 32, 3])
    poolT_d = din("poolT", [128, 8, 15])
    w_ada_d = din("w_ada", [D, 6 * D])
    w_in_d = din("w_in", [D, DPROJ])
    w_so_d = din("w_so", [DI, D])
    w_pool_d = din("w_pool", [D, 256])
    w_o_d = din("w_o", [D, D])
    w_up_d = din("w_up", [D, DFF])
    w_down_d = din("w_down", [DFF, D])

    yp_d = dout("yp", [NP_TOK, D])
    ys_d = dout("ys", [DEC, D])
    hTp_d = dout("hTp", [128, DI])
    hTs_d = dout("hTs", [128, DI])
    convp_d = dout("convp", [128, 32, 3])
    convs_d = dout("convs", [128, 32, 3])
    poolp_d = dout("poolp", [128, 8, 15])
    pools_d = dout("pools", [128, 8, 15])

    wb_ada = dscr("wb_ada", [D, 6 * D])
    wb_in = dscr("wb_in", [D, DPROJ])
    wb_so = dscr("wb_so", [DI, D])
    wb_pool = dscr("wb_pool", [D, 256])
    wb_o = dscr("wb_o", [D, D])
    wb_up = dscr("wb_up", [D, DFF])
    wb_down = dscr("wb_down", [DFF, D])

    ld_ring = k.dsig_ring(12, "ld")
    st_ring = k.dsig_ring(8, "st")
    cast_ring = k.dsig_ring(20, "cast")
    sst_ring = k.dsig_ring(6, "sst")

    def cast(name, src, dst, rows, cols, inner, rstep):
        for r0 in range(0, rows, rstep):
            s_ = src[r0:r0 + rstep, :].rearrange("r (a b) -> r a b", b=inner)
            d_ = dst[r0:r0 + rstep, :].rearrange("r (a b) -> r a b", b=inner)
            sg = cast_ring.take()
            eng = k.eng["pool"]
            kn = k.known["pool"]
            if kn.get(sg, 0) < sg.cnt:
                eng.wait_ge(sg.sem, sg.cnt)
                kn[sg] = sg.cnt
            inst = eng.dma_start(out=d_, in_=s_)
            sg.cnt += 16
            inst.then_inc(sg.sem, 16)
            cast_parts.setdefault(name, []).append((sg, sg.cnt))

    cast_parts = {}

    ident_b = k.sb([128, 128], BF16, "ident_b")
    ones_f = k.sb([128, 128], F32, "ones_f")
    ones_b = k.sb([128, 128], BF16, "ones_b")
    U_f = k.sb([128, 128], F32, "U_f")
    L_f = k.sb([128, 128], F32, "L_f")
    colp = k.sb([128, NCOL], F32, "colp")
    rhead = k.sb([128, 96], F32, "rhead")
    arow = k.sb([128, 32], F32, "arow")
    icnt = k.sb([128, 4, 16], F32, "icnt")
    wpool = k.sb([128, 8, 256], BF16, "wpool")
    ccol = k.sb([128, 8, 2], F32, "ccol")
    scol = k.sb([128, 8, 2], F32, "scol")
    modc = k.sb([128, 32, 2], F32, "modc")
    A1 = k.sb([128, 8, 2], F32, "A1")
    B1 = k.sb([128, 8, 2], F32, "B1")
    A2 = k.sb([128, 8, 2], F32, "A2")
    B2 = k.sb([128, 8, 2], F32, "B2")
    G1 = k.sb([128, D], F32, "G1")
    G2 = k.sb([128, D], F32, "G2")
    gs_scr = nc.dram_tensor("gs_scr", [2, 128, D], F32, kind="Internal").ap()
    gs_res = Buf(None)

    psf = Ring([k.ps([128, 512], F32, f"psf{i}") for i in range(7)])
    psx = k.ps([128, 512], F32, "psx")

    def psb_view(b):
        return b.t[:].bitcast(BF16)

    wslots = Ring([k.sb([128, WSLOT_ELEMS], BF16, f"wsl{i}") for i in range(NWSLOT)])
    xbuf = Ring([k.sb([128, D], F32, f"xb{i}") for i in range(2 * NSUB)])
    f32tmp = Ring([k.sb([128, D], F32, f"ft{i}") for i in range(2)])
    xnbuf = Ring([k.sb([128, D], BF16, f"xn{i}") for i in range(2)])
    junk_r = Ring([k.sb([128, D], BF16, f"junk{i}") for i in range(1)])
    chunk = Ring([k.sb([128, TT_MAX], BF16, f"ch{i}") for i in range(40)])
    uT_sets = Ring([[k.sb([128, TT_MAX], BF16, f"uT{a}_{i}") for i in range(8)] for a in range(2)])
    xs_tok = [k.sb([128, DI], BF16, f"xstok{i}") for i in range(NSUB)]
    B_tok = [k.sb([128, NG * NS], BF16, f"btok{i}") for i in range(NSUB)]
    zs_tok = [k.sb([128, DI], BF16, f"zstok{i}") for i in range(NSUB)]
    pbring = Ring([k.sb([128, TT_MAX + 3], BF16, f"pb{i}") for i in range(8)])
    xf_r = Ring([k.sb([128, TT_MAX], BF16, f"xf{i}") for i in range(8)])
    dgring = Ring([k.sb([128, 4, 128], BF16, f"dg{i}") for i in range(8)])
    dg_ready = {}
    hist = k.sb([128, 32, 3], BF16, "hist")
    convout = k.sb([128, 32, 3], F32, "convout")
    pp = [k.sb([128, 15 + TT_MAX], F32, f"pp{i}") for i in range(8)]
    psc = Ring([k.sb([128, 15 + TT_MAX], F32, f"psc{i}") for i in range(3)])
    small = Ring([k.sb([128, 32], F32, f"sm{i}") for i in range(32)])
    dtbuf = [k.sb([128, 32], F32, f"dt{i}") for i in range(NSUB)]
    cbm_r = Ring([k.sb([128, 128], F32, f"cbm{i}") for i in range(3)])
    W_r = Ring([k.sb([128, 128], F32, f"Wh{i}") for i in range(12)])
    M_r = Ring([k.sb([128, 4, 128], BF16, f"Mg{i}") for i in range(3)])
    xg_r = Ring([k.sb([128, 3, 256], BF16, f"xg{i}") for i in range(4)])
    ygT_all = k.sb([128, 16, TT_MAX], BF16, "ygT_all")
    gsn_bc = k.sb([128, DI], BF16, "gsn_bc")
    tmp256 = Ring([k.sb([128, 256], F32, f"t256_{i}") for i in range(2)])
    ybf_r = Ring([k.sb([128, 256], BF16, f"ybf{i}") for i in range(3)])
    yz_r = Ring([k.sb([128, DI], BF16, f"yz{i}") for i in range(2)])
    for b_ in yz_r.bufs:
        b_.subs = [Buf(b_.t) for _ in range(NG)]
    hst = k.sb([128, DI], F32, "hst")
    hbf = [k.sb([128, 256], BF16, f"hbf{g}") for g in range(NG)]
    ab_r = Ring([k.sb([128, TT_MAX], F32, f"ab{i}") for i in range(2)])
    relu_r = Ring([k.sb([128, TT_MAX], BF16, f"rl{i}") for i in range(3)])

    hst_res = [Buf(None) for _ in range(NG)]

    def memset(e, buf, val):
        k.op(e, lambda eng: eng.memset(buf[:], val), writes=[buf])

    memset("pool", ones_f, 1.0)
    memset("pool", ones_b, 1.0)
    k.op("pool", lambda eng: eng.affine_select(out=ident_b[:], in_=ones_b[:], pattern=[[-1, 128]], compare_op=ALU.is_equal,
                                               fill=0.0, base=0, channel_multiplier=1), reads=[ones_b], writes=[ident_b])
    k.op("pool", lambda eng: eng.affine_select(out=U_f[:], in_=ones_f[:], pattern=[[1, 128]], compare_op=ALU.is_ge,
                                               fill=0.0, base=0, channel_multiplier=-1), reads=[ones_f], writes=[U_f])
    k.op("pool", lambda eng: eng.affine_select(out=L_f[:], in_=ones_f[:], pattern=[[-1, 128]], compare_op=ALU.is_gt,
                                               fill=0.0, base=0, channel_multiplier=1), reads=[ones_f], writes=[L_f])

    cast("w_ada", w_ada_d, wb_ada, D, 6 * D, 2048, 256)
    cast("w_in", w_in_d, wb_in, D, DPROJ, 1156, 256)
    cast("w_pool", w_pool_d, wb_pool, D, 256, 256, 1024)
    cast("w_so", w_so_d, wb_so, DI, D, 1024, 1024)
    cast("w_o", w_o_d, wb_o, D, D, 1024, 1024)
    cast("w_up", w_up_d, wb_up, D, DFF, 2048, 512)
    cast("w_down", w_down_d, wb_down, DFF, D, 1024, 1024)

    def wait_cast(q, name):
        kn = k.known[q]
        for sg, v in cast_parts[name]:
            if kn.get(sg, 0) < v:
                k.eng[q].wait_ge(sg.sem, v)
                kn[sg] = v

    k.dma("sp", ld_ring, colp[:], colp_d, writes=[colp])
    k.dma("sp", ld_ring, rhead[:], rhead_d, writes=[rhead])
    k.dma("sp", ld_ring, ccol[:], ccol_d, writes=[ccol])
    k.dma("pool", st_ring, gsn_bc[:], rgsn_d, writes=[gsn_bc])
    k.op("act", lambda eng: eng.activation(out=arow[:], in_=rhead[:, 32:64], func=AF.Exp), reads=[rhead], writes=[arow])
    k.op("dve", lambda eng: eng.tensor_scalar(out=arow[:], in0=arow[:], scalar1=-1.0, scalar2=None, op0=ALU.mult),
         reads=[arow], writes=[arow])
    iot = small.take()
    k.op("pool", lambda eng: eng.iota(iot[:, 0:16], pattern=[[1, 16]], base=1, channel_multiplier=0,
                                      allow_small_or_imprecise_dtypes=True), writes=[iot])
    for gi, w in enumerate((2, 4, 8, 16)):
        t_ = small.take()
        k.op("dve", lambda eng, t_=t_, w=w: eng.tensor_scalar(out=t_[:, 0:16], in0=iot[:, 0:16], scalar1=float(w), scalar2=None,
                                                              op0=ALU.min), reads=[iot], writes=[t_])
        k.op("dve", lambda eng, t_=t_, gi=gi: eng.reciprocal(out=icnt[:, gi, :], in_=t_[:, 0:16]), reads=[t_], writes=[icnt])

    k.op("act", lambda eng: eng.activation(out=scol[:], in_=ccol[:], func=AF.Silu), reads=[ccol], writes=[scol])
    scol_b = k.sb([128, 8, 2], BF16, "scol_b")
    k.op("dve", lambda eng: eng.tensor_copy(out=scol_b[:], in_=scol[:]), reads=[scol], writes=[scol_b])
    screp = []
    for s in range(2):
        v = xs_tok[s].t[:, 0:1024].rearrange("p (a b) -> p a b", b=128)
        screp.append(v)
        k.op("dve", lambda eng, s=s, v=v: eng.tensor_copy(out=v, in_=scol[:, :, s:s + 1].broadcast_to([128, 8, 128])),
             reads=[scol], writes=[xs_tok[s]])
    ps_col = psx
    col_chunks = list(range(0, 16)) + list(range(24, 40))
    colidx = {ch: mi for mi, ch in enumerate(col_chunks)}
    wb_ada_v = wb_ada.rearrange("(kc p) c -> p kc c", p=128)
    gload = {0: (f32tmp.bufs[0], f32tmp.bufs[1]), 1: (xbuf.bufs[0], xbuf.bufs[1])}
    for which in range(2):
        rbt, rgt = gload[which]
        k.dma("sp", ld_ring, rbt[:], rbada_d[:, which * D:(which + 1) * D], writes=[rbt])
        k.dma("sp", ld_ring, rgt[:], rgpost_d[:, which * D:(which + 1) * D], writes=[rgt])
    wait_cast("sp", "w_ada")
    for blk in range(12):
        wsl = wslots.take()
        wv = wsl.t[:, 0:4096].rearrange("p (a b) -> p a b", b=512)
        k.dma("sp", ld_ring, wv, wb_ada_v[:, :, blk * 512:(blk + 1) * 512], writes=[wsl])
        if blk * 4 in colidx:
            for cc in range(4):
                mi = colidx[blk * 4 + cc]
                for kc in range(8):
                    fn = lambda eng, cc=cc, mi=mi, kc=kc: eng.matmul(ps_col[:, mi * 2:mi * 2 + 2], lhsT=wv[:, kc, cc * 128:(cc + 1) * 128],
                                                                     rhs=scol_b[:, kc, :], start=(kc == 0), stop=(kc == 7))
                    if kc < 7:
                        k.pe_quiet(fn, reads=[wsl, scol_b], writes=[ps_col])
                    else:
                        k.op("pe", fn, reads=[wsl, scol_b], writes=[ps_col])
        else:
            which, q = (0, blk - 4) if blk < 6 else (1, blk - 10)
            rbt, rgt = gload[which]
            for s in range(2):
                psr = psf.take()
                for kc in range(8):
                    fn = lambda eng, s=s, kc=kc, psr=psr: eng.matmul(psr[:, 0:512], lhsT=screp[s][:, kc, :], rhs=wv[:, kc, :],
                                                                     start=(kc == 0), stop=(kc == 7))
                    if kc < 7:
                        k.pe_quiet(fn, reads=[wsl, xs_tok[s]], writes=[psr])
                    else:
                        k.op("pe", fn, reads=[wsl, xs_tok[s]], writes=[psr])
                if s == 0:
                    Gb = G1 if which == 0 else G2
                    dst = Gb[:, q * 512:(q + 1) * 512]
                    wr = [Gb]
                else:
                    dst = hst[:, which * D + q * 512:which * D + (q + 1) * 512]
                    wr = hst_res
                k.op("dve", lambda eng, psr=psr, dst=dst, q=q, rbt=rbt: eng.tensor_tensor(out=dst, in0=psr[:, 0:512],
                                                                                          in1=rbt[:, q * 512:(q + 1) * 512], op=ALU.add),
                     reads=[psr, rbt], writes=wr)
                k.op("dve", lambda eng, dst=dst, q=q, rgt=rgt: eng.tensor_tensor(out=dst, in0=dst, in1=rgt[:, q * 512:(q + 1) * 512], op=ALU.mult),
                     reads=wr + [rgt], writes=wr)
    k.dma("sp", ld_ring, gs_scr.rearrange("w p d -> p w d"), hst[:].rearrange("p (w d) -> p w d", w=2), reads=hst_res, writes=[gs_res])
    for qi, (c0, b0) in enumerate(((0, 0), (8, 8), (16, 24), (24, 32))):
        k.op("dve", lambda eng, c0=c0, b0=b0: eng.tensor_tensor(
            out=modc[:, c0:c0 + 8, :], in0=ps_col[:, c0 * 2:(c0 + 8) * 2].rearrange("p (a b) -> p a b", b=2),
            in1=colp[:, C_BADA + b0:C_BADA + b0 + 8].unsqueeze(2).broadcast_to([128, 8, 2]), op=ALU.add),
            reads=[ps_col, colp], writes=[modc])
    for (Aout, Bout, sh0, sc0, gcol) in ((A1, B1, 0, 8, C_GPM), (A2, B2, 16, 24, C_GPL)):
        k.op("dve", lambda eng, Aout=Aout, sc0=sc0: eng.tensor_scalar(out=Aout[:], in0=modc[:, sc0:sc0 + 8, :], scalar1=1.0, scalar2=None,
                                                                       op0=ALU.add), reads=[modc], writes=[Aout])
        k.op("dve", lambda eng, Aout=Aout, gcol=gcol: eng.tensor_tensor(out=Aout[:], in0=Aout[:],
                                                                         in1=colp[:, gcol:gcol + 8].unsqueeze(2).broadcast_to([128, 8, 2]),
                                                                         op=ALU.mult), reads=[Aout, colp], writes=[Aout])
        k.op("dve", lambda eng, Bout=Bout, sh0=sh0: eng.tensor_copy(out=Bout[:], in_=modc[:, sh0:sh0 + 8, :]), reads=[modc], writes=[Bout])

    wait_cast("sp", "w_pool")
    k.dma("sp", ld_ring, wpool[:], wb_pool.rearrange("(a p) c -> p a c", p=128), writes=[wpool])

    wb_in_v = wb_in.rearrange("(kc p) c -> p kc c", p=128)
    wb_so_v = wb_so.rearrange("(kc p) c -> p kc c", p=128)
    wb_o_v = wb_o.rearrange("(kc p) c -> p kc c", p=128)
    wb_up_v = wb_up.rearrange("(kc p) c -> p kc c", p=128)
    wb_down_v = wb_down.rearrange("(kc p) c -> p kc c", p=128)

    Zb, MIDb, ENDb = [], [], []
    for cb in range(4):
        Zb.append(("w_in", wb_in_v[:, :, cb * 512:(cb + 1) * 512], 8, 512))
    for cb in range(8):
        MIDb.append(("w_in", wb_in_v[:, :, 2048 + cb * 512:2048 + (cb + 1) * 512], 8, 512))
    MIDb.append(("w_in", wb_in_v[:, :, 6144:6176], 8, 32))
    for cb in range(2):
        MIDb.append(("w_in", wb_in_v[:, :, 6176 + cb * 512:6176 + (cb + 1) * 512], 8, 512))
    for cb in range(4):
        MIDb.append(("w_in", wb_in_v[:, :, 7200 + cb * 512:7200 + (cb + 1) * 512], 8, 512))
    for cb in range(4):
        MIDb.append(("w_so", wb_so_v[:, :, cb * 256:(cb + 1) * 256], 16, 256))
    for cb in range(2):
        MIDb.append(("w_o", wb_o_v[:, :, cb * 512:(cb + 1) * 512], 8, 512))
    for cb in range(8):
        ENDb.append(("w_up", wb_up_v[:, :, cb * 512:(cb + 1) * 512], 8, 512))
    for cb in range(4):
        for kh in range(2):
            ENDb.append(("w_down", wb_down_v[:, kh * 16:(kh + 1) * 16, cb * 256:(cb + 1) * 256], 16, 256))
    n_pass = n_ptiles + (1 if do_sample else 0)
    wseq = list(Zb)
    for n_ in range(n_pass):
        wseq += MIDb
        if n_ + 1 < n_pass:
            wseq += Zb
        wseq += ENDb
    wstate = {"issued": 0, "next": 0, "bufs": {}, "seen": set()}
    total_blocks = len(wseq)

    def w_issue(upto):
        while wstate["issued"] < min(upto, total_blocks):
            i = wstate["issued"]
            name, src, nk, ncol = wseq[i]
            if name not in wstate["seen"]:
                wstate["seen"].add(name)
                wait_cast("sp", name)
            sl = wslots.take()
            dst = sl[:, 0:nk * ncol].rearrange("p (a b) -> p a b", b=ncol)
            k.dma("sp", ld_ring, dst, src, writes=[sl])
            wstate["bufs"][i] = (sl, nk, ncol)
            wstate["issued"] += 1

    def wnext():
        i = wstate["next"]
        wstate["next"] += 1
        w_issue(i + NWSLOT)
        sl, nk, ncol = wstate["bufs"].pop(i)
        view = sl[:, 0:nk * ncol].rearrange("p (a b) -> p a b", b=ncol)
        return sl, view

    def mm_group(out_ap, pairs, reads, psbuf):
        n = len(pairs)
        for i, (l, r) in enumerate(pairs):
            fn = lambda eng, l=l, r=r, i=i: eng.matmul(out_ap, lhsT=l, rhs=r, start=(i == 0), stop=(i == n - 1))
            if i < n - 1:
                k.pe_quiet(fn, reads=reads, writes=[psbuf])
            else:
                k.op("pe", fn, reads=reads, writes=[psbuf])

    def rstd_from_ss(ss_ap, ss_buf, n_feat, width, nt):
        ln_ = small.take()
        k.op("act", lambda eng: eng.activation(out=ln_[0:nt, 0:width], in_=ss_ap, func=AF.Ln, bias=EPS, scale=1.0 / n_feat),
             reads=[ss_buf], writes=[ln_])
        r_ = small.take()
        k.op("act", lambda eng: eng.activation(out=r_[0:nt, 0:width], in_=ln_[0:nt, 0:width], func=AF.Exp, scale=-0.5),
             reads=[ln_], writes=[r_])
        return r_

    def norm_phase1(xbs_, subs_):
        n = len(subs_)
        sss, rs, xns = [], [], []
        for j in range(n):
            nt = subs_[j]
            ss = small.take()
            junk_act = junk_r.take()
            k.op("act", lambda eng: eng.activation(out=junk_act[0:nt, :], in_=xbs_[j][0:nt, :], func=AF.Square, accum_out=ss[0:nt, 0:1]),
                 reads=[xbs_[j]], writes=[junk_act, ss])
            sss.append(ss)
        for j in range(n):
            nt = subs_[j]
            rs.append(rstd_from_ss(sss[j][0:nt, 0:1], sss[j], D, 1, nt))
        for j in range(n):
            nt = subs_[j]
            xn = xnbuf.take()
            k.op("dve", lambda eng: eng.tensor_scalar(out=xn[0:nt, :], in0=xbs_[j][0:nt, :], scalar1=rs[j][0:nt, 0:1], scalar2=None, op0=ALU.mult),
                 reads=[xbs_[j], rs[j]], writes=[xn])
            xns.append(xn)
        return xns

    def norm_phase2(xns, subs_, offs_, Acol, Bcol, s, dstT):
        n = len(subs_)
        pbs_ = []
        for j in range(n):
            nt = subs_[j]
            pb = psf.take()
            pv = psb_view(pb)
            for kc in range(8):
                k.op("pe", lambda eng: eng.transpose(out=pv[:, kc * 128:kc * 128 + nt], in_=xns[j][0:nt, kc * 128:(kc + 1) * 128],
                                                     identity=ident_b[0:nt, 0:nt]), reads=[xns[j], ident_b], writes=[pb])
            pbs_.append(pb)
        for j in range(n):
            nt = subs_[j]
            pv = psb_view(pbs_[j])
            c0 = offs_[j]
            for kc in range(8):
                k.op("dve", lambda eng: eng.tensor_scalar(out=dstT[kc][:, c0:c0 + nt], in0=pv[:, kc * 128:kc * 128 + nt],
                                                          scalar1=Acol[:, kc, s:s + 1], scalar2=Bcol[:, kc, s:s + 1],
                                                          op0=ALU.mult, op1=ALU.add),
                     reads=[pbs_[j], Acol, Bcol], writes=[dstT[kc]])

    def build_dg_block(cb):
        if cb in dg_ready:
            return
        dgs = []
        for cc in range(4):
            c = cb * 4 + cc
            dg = dgring.take()
            k.op("dve", lambda eng: eng.tensor_tensor(out=dg[:, :, :], in0=ident_b[:, :].unsqueeze(1).broadcast_to([128, 4, 128]),
                                                      in1=colp[:, C_CW + c * 4:C_CW + c * 4 + 4].unsqueeze(2).broadcast_to([128, 4, 128]),
                                                      op=ALU.mult), reads=[ident_b, colp], writes=[dg])
            dgs.append(dg)
        dg_ready[cb] = dgs

    def prep_load(sp):
        subs = sp["subs"]
        offs = [sum(subs[:j]) for j in range(len(subs))]
        sp["xbs"] = []
        for j, nt in enumerate(subs):
            xb = xbuf.take()
            k.dma("sp", ld_ring, xb[0:nt, :], sp["x_rows"][offs[j]:offs[j] + nt, :], writes=[xb])
            sp["xbs"].append(xb)

    def prep_ew(sp):
        sp["xns"] = norm_phase1(sp["xbs"], sp["subs"])

    def prep_pe(sp):
        subs = sp["subs"]
        offs = [sum(subs[:j]) for j in range(len(subs))]
        sp["uT"] = uT_sets.take()
        norm_phase2(sp["xns"], subs, offs, A1, B1, sp["s"], sp["uT"])

    def zproj(sp):
        subs = sp["subs"]
        offs = [sum(subs[:j]) for j in range(len(subs))]
        uT = sp["uT"]
        for cb in range(4):
            sl, W = wnext()
            for j, nt in enumerate(subs):
                ps = psf.take()
                mm_group(ps[0:nt, 0:512], [(uT[kc][:, offs[j]:offs[j] + nt], W[:, kc, :]) for kc in range(8)], [sl] + uT, ps)
                k.op("act", lambda eng, ps=ps, j=j, nt=nt, cb=cb: eng.activation(out=zs_tok[j][0:nt, cb * 512:(cb + 1) * 512], in_=ps[0:nt, 0:512],
                                                                                  func=AF.Silu), reads=[ps], writes=[zs_tok[j]])
        sp["ready"] = True

    def run_tile(sp, nxt_sp):
        s, y_rows, subs, first, last, pos0_is_zero = sp["s"], sp["y_rows"], sp["subs"], sp["first"], sp["last"], sp["pos0"]
        nsub = len(subs)
        TT = sum(subs)
        offs = [sum(subs[:j]) for j in range(nsub)]
        if not sp.get("ready"):
            prep_load(sp)
            prep_ew(sp)
            prep_pe(sp)
            zproj(sp)
        if nxt_sp is not None:
            prep_load(nxt_sp)
        xbs = sp["xbs"]
        uT = sp["uT"]

        BT = [chunk.take() for _ in range(NG)]
        CT = [chunk.take() for _ in range(NG)]
        def emit_transposes(cb_, srcs_):
            for j, nt in enumerate(subs):
                pb = psf.take()
                pv = psb_view(pb)
                for cc in range(4):
                    k.op("pe", lambda eng: eng.transpose(out=pv[0:nt, cc * 128:(cc + 1) * 128], in_=srcs_[cc][:, offs[j]:offs[j] + nt],
                                                         identity=ident_b[:, :]), reads=[srcs_[cc], ident_b], writes=[pb])
                dst = xs_tok[j][0:nt, cb_ * 512:(cb_ + 1) * 512] if cb_ < 4 else B_tok[j][0:nt, (cb_ - 4) * 512:(cb_ - 3) * 512]
                dbuf = xs_tok[j] if cb_ < 4 else B_tok[j]
                k.op("dve", lambda eng: eng.tensor_copy(out=dst, in_=pv[0:nt, 0:512]), reads=[pb], writes=[dbuf])

        def xbc_proj(cb):
            sl, W = wnext()
            pbs = []
            for cc in range(4):
                c = cb * 4 + cc
                ps = psf.take()
                mm_group(ps[:, 0:TT], [(W[:, kc, cc * 128:(cc + 1) * 128], uT[kc][:, 0:TT]) for kc in range(8)], [sl] + uT, ps)
                Pb = pbring.take()
                k.op("pool", lambda eng: eng.tensor_copy(out=Pb[:, 0:3], in_=hist[:, c, :]), reads=[hist], writes=[Pb])
                k.op("act", lambda eng: eng.activation(out=Pb[:, 3:3 + TT], in_=ps[:, 0:TT], func=AF.Copy),
                     reads=[ps], writes=[Pb])
                k.op("pool", lambda eng: eng.tensor_copy(out=hist[:, c, :], in_=Pb[:, TT:TT + 3]), reads=[Pb], writes=[hist])
                if last:
                    k.op("dve", lambda eng: eng.tensor_copy(out=convout[:, c, :], in_=ps[:, TT - 3:TT]),
                         reads=[ps], writes=[convout])
                pbs.append(Pb)
            return pbs

        def xbc_conv(cb, pbs):
            dgs = dg_ready.pop(cb)
            srcs = []
            for cc in range(4):
                c = cb * 4 + cc
                if c < 16:
                    dstb = xf_r.take()
                elif c < 24:
                    dstb = BT[c - 16]
                else:
                    dstb = CT[c - 24]
                ps3 = psf.take()
                mm_group(ps3[:, 0:TT], [(dgs[cc][:, kk, :], pbs[cc][:, kk:kk + TT]) for kk in range(4)], [pbs[cc], dgs[cc]], ps3)
                k.op("act", lambda eng: eng.activation(out=dstb[:, 0:TT], in_=ps3[:, 0:TT], func=AF.Silu,
                                                       bias=colp[:, C_CB + c:C_CB + c + 1]),
                     reads=[ps3, colp], writes=[dstb])
                srcs.append(dstb)
            return srcs

        pend_pb = {}
        pend_src = {}
        for it_ in range(8 + 2):
            if it_ < 8:
                build_dg_block(it_)
                pend_pb[it_] = xbc_proj(it_)
            if 0 <= it_ - 1 < 8:
                pend_src[it_ - 1] = xbc_conv(it_ - 1, pend_pb.pop(it_ - 1))
            if 0 <= it_ - 2 < 6:
                emit_transposes(it_ - 2, pend_src.pop(it_ - 2))
        sl, W = wnext()
        for j, nt in enumerate(subs):
            ps = psf.take()
            mm_group(ps[0:nt, 0:32], [(uT[kc][:, offs[j]:offs[j] + nt], W[:, kc, :]) for kc in range(8)], [sl] + uT, ps)
            t1 = small.take()
            k.op("dve", lambda eng, ps=ps, t1=t1, nt=nt: eng.tensor_tensor(out=t1[0:nt, :], in0=ps[0:nt, 0:32], in1=rhead[0:nt, 0:32], op=ALU.add),
                 reads=[ps, rhead], writes=[t1])
            t2 = small.take()
            k.op("act", lambda eng, t1=t1, t2=t2, nt=nt: eng.activation(out=t2[0:nt, :], in_=t1[0:nt, :], func=AF.Exp), reads=[t1], writes=[t2])
            k.op("act", lambda eng, t2=t2, j=j, nt=nt: eng.activation(out=dtbuf[j][0:nt, :], in_=t2[0:nt, :], func=AF.Ln, bias=1.0),
                 reads=[t2], writes=[dtbuf[j]])
        pmT = [chunk.take() for _ in range(8)]
        for cb in range(2):
            sl, W = wnext()
            for cc in range(4):
                pc = cb * 4 + cc
                gi = pc // 2
                w = (2, 4, 8, 16)[gi]
                ps = psf.take()
                mm_group(ps[:, 0:TT], [(W[:, kc, cc * 128:(cc + 1) * 128], uT[kc][:, 0:TT]) for kc in range(8)], [sl] + uT, ps)
                ppb = pp[pc]
                k.op("act", lambda eng, ps=ps, ppb=ppb: eng.activation(out=ppb[:, 15:15 + TT], in_=ps[:, 0:TT], func=AF.Copy),
                     reads=[ps], writes=[ppb])
                cur = ppb
                step = 1
                lo = 0
                while step < w:
                    nxt = psc.take()
                    lo2 = lo + step
                    k.op("pool", lambda eng, cur=cur, nxt=nxt, lo2=lo2, step=step: eng.tensor_tensor(
                        out=nxt[:, lo2:15 + TT], in0=cur[:, lo2:15 + TT], in1=cur[:, lo2 - step:15 + TT - step], op=ALU.add),
                        reads=[cur], writes=[nxt])
                    cur = nxt
                    lo = lo2
                    step *= 2
                k.op("dve", lambda eng, cur=cur, ppb=ppb, pc=pc, w=w: eng.scalar_tensor_tensor(
                    out=pmT[pc][:, 0:TT], in0=cur[:, 15:15 + TT], scalar=1.0 / w, in1=ppb[:, 15:15 + TT], op0=ALU.mult, op1=ALU.subtract),
                    reads=[cur, ppb], writes=[pmT[pc]])
                if pos0_is_zero:
                    t_ = small.take()
                    k.op("dve", lambda eng, cur=cur, t_=t_, gi=gi: eng.tensor_tensor(out=t_[:, 0:16], in0=cur[:, 15:31], in1=icnt[:, gi, :], op=ALU.mult),
                         reads=[cur, icnt], writes=[t_])
                    k.op("dve", lambda eng, t_=t_, ppb=ppb, pc=pc: eng.tensor_tensor(out=pmT[pc][:, 0:16], in0=t_[:, 0:16], in1=ppb[:, 15:31], op=ALU.subtract),
                         reads=[t_, ppb], writes=[pmT[pc]])
                k.op("pool", lambda eng, ppb=ppb: eng.tensor_copy(out=ppb[:, 0:15], in_=ppb[:, TT:TT + 15]), reads=[ppb], writes=[ppb])
        gT = [chunk.take() for _ in range(16)]
        for cb in range(4):
            sl, W = wnext()
            for cc in range(4):
                gc = cb * 4 + cc
                ps = psf.take()
                mm_group(ps[:, 0:TT], [(W[:, kc, cc * 128:(cc + 1) * 128], uT[kc][:, 0:TT]) for kc in range(8)], [sl] + uT, ps)
                k.op("act", lambda eng, ps=ps, gc=gc: eng.activation(out=gT[gc][:, 0:TT], in_=ps[:, 0:TT], func=AF.Sigmoid),
                     reads=[ps], writes=[gT[gc]])

        chunks = []
        for j, nt in enumerate(subs):
            o = offs[j]
            dtj = dtbuf[j]
            da = small.take()
            k.op("dve", lambda eng: eng.tensor_tensor(out=da[0:nt, :], in0=dtj[0:nt, :], in1=arow[0:nt, :], op=ALU.mult),
                 reads=[dtj, arow], writes=[da])
            pss = psf.take()
            k.op("pe", lambda eng: eng.matmul(pss[0:nt, 0:32], lhsT=U_f[0:nt, 0:nt], rhs=da[0:nt, :], start=True, stop=True),
                 reads=[U_f, da], writes=[pss])
            k.op("pe", lambda eng: eng.matmul(pss[:, 32:64], lhsT=ones_f[0:nt, :], rhs=da[0:nt, :], start=True, stop=True),
                 reads=[ones_f, da], writes=[pss])
            eacs = small.take()
            k.op("act", lambda eng: eng.activation(out=eacs[0:nt, :], in_=pss[0:nt, 0:32], func=AF.Exp), reads=[pss], writes=[eacs])
            cdB = small.take()
            k.op("act", lambda eng: eng.activation(out=cdB[:, :], in_=pss[:, 32:64], func=AF.Exp), reads=[pss], writes=[cdB])
            acs = small.take()
            k.op("act", lambda eng: eng.activation(out=acs[0:nt, :], in_=pss[0:nt, 0:32], func=AF.Copy), reads=[pss], writes=[acs])
            dif = small.take()
            k.op("dve", lambda eng: eng.tensor_tensor(out=dif[0:nt, :], in0=pss[0:nt, 32:64], in1=acs[0:nt, :], op=ALU.subtract),
                 reads=[pss, acs], writes=[dif])
            dte = small.take()
            k.op("act", lambda eng: eng.activation(out=dte[0:nt, :], in_=dif[0:nt, :], func=AF.Exp), reads=[dif], writes=[dte])
            wts = small.take()
            k.op("dve", lambda eng: eng.tensor_tensor(out=wts[0:nt, :], in0=dte[0:nt, :], in1=dtj[0:nt, :], op=ALU.mult),
                 reads=[dte, dtj], writes=[wts])
            chunks.append(dict(nt=nt, o=o, dtj=dtj, da=da, eacs=eacs, cdB=cdB, wts=wts, yz=yz_r.take(), ssg=small.take(), j=j))

        items = [(j, g) for j in range(nsub) for g in range(NG)]
        ist = {}

        def S0(it):
            j, g = it
            c = chunks[j]
            nt = c["nt"]
            Ws = []
            for hh in range(4):
                h = g * 4 + hh
                Wh = W_r.take()
                if hh < 4:
                    k.op("act", lambda eng: eng.activation(out=Wh[0:nt, 0:nt], in_=L_f[0:nt, 0:nt], func=AF.Copy, scale=c["da"][0:nt, h:h + 1]),
                         reads=[L_f, c["da"]], writes=[Wh])
                else:
                    k.op("dve", lambda eng: eng.tensor_scalar(out=Wh[0:nt, 0:nt], in0=L_f[0:nt, 0:nt], scalar1=c["da"][0:nt, h:h + 1],
                                                              scalar2=None, op0=ALU.mult), reads=[L_f, c["da"]], writes=[Wh])
                Ws.append(Wh)
            xg = xg_r.take()
            xs_g = xs_tok[j][0:nt, g * 256:(g + 1) * 256].rearrange("p (h d) -> p h d", d=HD)
            for qi, src in enumerate((c["wts"], rhead, c["dtj"])):
                col0 = 64 if qi == 1 else 0
                k.op("pool",
                     lambda eng: eng.tensor_tensor(out=xg[0:nt, qi, :].rearrange("p (h d) -> p h d", d=HD), in0=xs_g,
                                                   in1=src[0:nt, col0 + g * 4:col0 + (g + 1) * 4].unsqueeze(2).broadcast_to([nt, 4, HD]),
                                                   op=ALU.mult), reads=[xs_tok[j], src], writes=[xg])
            gs_ = slice(g * 256, (g + 1) * 256)
            hres = hst_res[g]
            k.op("pool", lambda eng: eng.tensor_tensor(out=hst[:, gs_].rearrange("p (h d) -> p h d", d=HD),
                                                       in0=hst[:, gs_].rearrange("p (h d) -> p h d", d=HD),
                                                       in1=c["cdB"][:, g * 4:(g + 1) * 4].unsqueeze(2).broadcast_to([128, 4, HD]), op=ALU.mult),
                 reads=[hres, c["cdB"]], writes=[hres])
            ist[it] = dict(W=Ws, xg=xg)

        def S1a(it):
            j, g = it
            c = chunks[j]
            nt, o = c["nt"], c["o"]
            pcb = ps_cb.take()
            k.op("pe", lambda eng: eng.matmul(pcb[0:nt, 0:nt], lhsT=BT[g][:, o:o + nt], rhs=CT[g][:, o:o + nt], start=True, stop=True),
                 reads=[BT[g], CT[g]], writes=[pcb])
            cbm = cbm_r.take()
            k.op("dve", lambda eng: eng.tensor_tensor(out=cbm[0:nt, 0:nt], in0=pcb[0:nt, 0:nt], in1=U_f[0:nt, 0:nt], op=ALU.mult),
                 reads=[pcb, U_f], writes=[cbm])
            ist[it].update(cbm=cbm)

        def S1b(it):
            j, g = it
            c = chunks[j]
            nt = c["nt"]
            pseg = ps_seg.take()
            for hh in range(4):
                Wh = ist[it]["W"][hh]
                k.op("pe", lambda eng: eng.matmul(pseg[0:nt, hh * 128:hh * 128 + nt], lhsT=Wh[0:nt, 0:nt], rhs=U_f[0:nt, 0:nt],
                                                  start=True, stop=True), reads=[Wh, U_f], writes=[pseg])
            ist[it].update(pseg=pseg)

        def S2a(it):
            j, g = it
            nt = chunks[j]["nt"]
            pseg = ist[it]["pseg"]
            segv = pseg[0:nt, :].rearrange("p (h l) -> p h l", l=128)[:, :, 0:nt]
            k.op("act", lambda eng: eng.activation(out=segv, in_=segv, func=AF.Exp), reads=[pseg], writes=[pseg])

        def S2b(it):
            j, g = it
            nt = chunks[j]["nt"]
            pseg, cbm = ist[it]["pseg"], ist[it]["cbm"]
            segv = pseg[0:nt, :].rearrange("p (h l) -> p h l", l=128)[:, :, 0:nt]
            Mg = M_r.take()
            k.op("dve", lambda eng: eng.tensor_tensor(out=Mg[0:nt, :, 0:nt], in0=segv,
                                                      in1=cbm[0:nt, 0:nt].unsqueeze(1).broadcast_to([nt, 4, nt]), op=ALU.mult),
                 reads=[pseg, cbm], writes=[Mg])
            ist[it].update(Mg=Mg)

        def S3(it):
            j, g = it
            c = chunks[j]
            nt, o = c["nt"], c["o"]
            Mg, xg = ist[it]["Mg"], ist[it]["xg"]
            yz, ssg, eacs, cdB = c["yz"], c["ssg"], c["eacs"], c["cdB"]
            gs = slice(g * 256, (g + 1) * 256)
            py = ps_y.take()
            k.pe_quiet(lambda eng: eng.matmul(py[0:nt, 0:256], lhsT=ident_b[0:nt, 0:nt], rhs=xg[0:nt, 1, :], start=True, stop=False),
                       reads=[ident_b, xg], writes=[py])
            for hh in range(4):
                fn = lambda eng: eng.matmul(py[0:nt, hh * 64:(hh + 1) * 64], lhsT=Mg[0:nt, hh, 0:nt], rhs=xg[0:nt, 2, hh * 64:(hh + 1) * 64],
                                            start=False, stop=(hh == 3))
                if hh < 3:
                    k.pe_quiet(fn, reads=[Mg, xg], writes=[py])
                else:
                    k.op("pe", fn, reads=[Mg, xg], writes=[py])
            k.op("pe", lambda eng: eng.matmul(py[0:nt, 256:512], lhsT=CT[g][:, o:o + nt], rhs=hbf[g][:, :], start=True, stop=True),
                 reads=[CT[g], hbf[g]], writes=[py])
            pst = ps_st.take()
            k.op("pe", lambda eng: eng.matmul(pst[:, 0:256], lhsT=B_tok[j][0:nt, g * 128:(g + 1) * 128], rhs=xg[0:nt, 0, :], start=True, stop=True),
                 reads=[B_tok[j], xg], writes=[pst])
            t2 = tmp256.take()
            k.op("dve", lambda eng: eng.tensor_tensor(out=t2[0:nt, :].rearrange("p (h d) -> p h d", d=HD),
                                                      in0=py[0:nt, 256:512].rearrange("p (h d) -> p h d", d=HD),
                                                      in1=eacs[0:nt, g * 4:(g + 1) * 4].unsqueeze(2).broadcast_to([nt, 4, HD]), op=ALU.mult),
                 reads=[py, eacs], writes=[t2])
            ybf = ybf_r.take()
            k.op("dve", lambda eng: eng.tensor_tensor(out=ybf[0:nt, :], in0=py[0:nt, 0:256], in1=t2[0:nt, :], op=ALU.add),
                 reads=[py, t2], writes=[ybf])
            k.op("dve", lambda eng: eng.tensor_tensor(out=yz[0:nt, gs], in0=ybf[0:nt, :], in1=zs_tok[j][0:nt, gs], op=ALU.mult),
                 reads=[ybf, zs_tok[j]], writes=[yz.subs[g]])
            ist[it].update(pst=pst)

        def S4a(it):
            j, g = it
            pst = ist[it]["pst"]
            gs = slice(g * 256, (g + 1) * 256)
            hres = hst_res[g]
            k.op("dve", lambda eng: eng.tensor_tensor(out=hst[:, gs], in0=pst[:, 0:256], in1=hst[:, gs], op=ALU.add),
                 reads=[pst, hres], writes=[hres])

        def S4b(it):
            j, g = it
            c = chunks[j]
            nt = c["nt"]
            yz, ssg = c["yz"], c["ssg"]
            gs = slice(g * 256, (g + 1) * 256)
            hres = hst_res[g]
            junk_act = junk_r.take()
            k.op("act", lambda eng: eng.activation(out=junk_act[0:nt, 0:256], in_=yz[0:nt, gs], func=AF.Square, accum_out=ssg[0:nt, g:g + 1]),
                 reads=[yz.subs[g]], writes=[junk_act, ssg])
            k.op("pool", lambda eng: eng.tensor_copy(out=hbf[g][:, :], in_=hst[:, gs]), reads=[hres], writes=[hbf[g]])
            del ist[it]
            if g == NG - 1:
                epilogue(c)

        def epilogue(c):
            nt, o, yz, ssg = c["nt"], c["o"], c["yz"], c["ssg"]
            rg = rstd_from_ss(ssg[0:nt, 0:8], ssg, 256, 8, nt)
            yg = yz
            for g in range(NG):
                gs = slice(g * 256, (g + 1) * 256)
                k.op("dve", lambda eng: eng.scalar_tensor_tensor(out=yg[0:nt, gs], in0=yz[0:nt, gs], scalar=rg[0:nt, g:g + 1],
                                                                 in1=gsn_bc[0:nt, gs], op0=ALU.mult, op1=ALU.mult),
                     reads=[yz.subs[g], rg, gsn_bc], writes=[yz.subs[g]])
            for half in range(2):
                pb = ps_y.take()
                pv = psb_view(pb)
                for q in range(8):
                    fc = half * 8 + q
                    k.op("pe", lambda eng: eng.transpose(out=pv[:, q * 128:q * 128 + nt], in_=yg[0:nt, fc * 128:(fc + 1) * 128],
                                                         identity=ident_b[0:nt, 0:nt]), reads=[yz.subs[fc // 2], ident_b], writes=[pb])
                k.op("act", lambda eng: eng.activation(
                    out=ygT_all[:, half * 8:(half + 1) * 8, o:o + nt],
                    in_=pv.rearrange("p (q t) -> p q t", t=128)[:, :, 0:nt], func=AF.Copy), reads=[pb], writes=[ygT_all])

        n_it = len(items)
        ps_cb = Ring(psf.bufs[0:1])
        ps_seg = Ring(psf.bufs[1:3])
        ps_y = Ring(psf.bufs[3:5])
        ps_st = Ring(psf.bufs[5:8])
        def at(fn, idx):
            if 0 <= idx < n_it:
                fn(items[idx])

        for i in range(n_it + 4):
            at(S2a, i - 2)
            at(S4a, i - 4)
            at(S0, i)
            at(S1a, i - 1)
            at(S2b, i - 2)
            at(S3, i - 3)
            at(S4b, i - 4)
            at(S1b, i - 1)

        if nxt_sp is not None:
            prep_ew(nxt_sp)
        mixT = [chunk.take() for _ in range(8)]
        for cb in range(4):
            sl, W = wnext()
            for dl in range(2):
                dc = cb * 2 + dl
                gi = dc // 2
                ps = psf.take()
                mm_group(ps[:, 0:TT], [(W[:, kc, dl * 128:(dl + 1) * 128], ygT_all[:, kc, 0:TT]) for kc in range(16)], [sl, ygT_all], ps)
                ps2 = psf.take()
                mm_group(ps2[:, 0:TT], [(wpool[:, gi * 2 + kc, (dc % 2) * 128:(dc % 2 + 1) * 128], pmT[gi * 2 + kc][:, 0:TT]) for kc in range(2)],
                         [wpool, pmT[gi * 2], pmT[gi * 2 + 1]], ps2)
                a_ = ab_r.take()
                k.op("dve", lambda eng, ps=ps, a_=a_, dc=dc: eng.tensor_tensor(out=a_[:, 0:TT], in0=ps[:, 0:TT], in1=gT[dc][:, 0:TT], op=ALU.mult),
                     reads=[ps, gT[dc]], writes=[a_])
                b_ = ab_r.take()
                k.op("dve", lambda eng, ps2=ps2, b_=b_, dc=dc: eng.scalar_tensor_tensor(
                    out=b_[:, 0:TT], in0=ps2[:, 0:TT], scalar=colp[:, C_PSC + dc:C_PSC + dc + 1], in1=gT[8 + dc][:, 0:TT],
                    op0=ALU.mult, op1=ALU.mult), reads=[ps2, colp, gT[8 + dc]], writes=[b_])
                k.op("pool", lambda eng, a_=a_, b_=b_, dc=dc: eng.tensor_tensor(out=mixT[dc][:, 0:TT], in0=a_[:, 0:TT], in1=b_[:, 0:TT], op=ALU.add),
                     reads=[a_, b_], writes=[mixT[dc]])
        if nxt_sp is not None:
            prep_pe(nxt_sp)
        pso = [[None, None] for _ in subs]
        for ob in range(2):
            sl, W = wnext()
            for j, nt in enumerate(subs):
                ps = psf.take()
                mm_group(ps[0:nt, 0:512], [(mixT[kc][:, offs[j]:offs[j] + nt], W[:, kc, :]) for kc in range(8)], [sl] + mixT, ps)
                pso[j][ob] = ps
        for j, nt in enumerate(subs):
            ssA = small.take()
            for ob in range(2):
                junk_act = junk_r.take()
                k.op("act", lambda eng, ob=ob: eng.activation(out=junk_act[0:nt, 0:512], in_=pso[j][ob][0:nt, 0:512], func=AF.Square,
                                                              accum_out=ssA[0:nt, ob:ob + 1]), reads=[pso[j][ob]], writes=[junk_act, ssA])
            ss = small.take()
            k.op("dve", lambda eng: eng.tensor_tensor(out=ss[0:nt, 0:1], in0=ssA[0:nt, 0:1], in1=ssA[0:nt, 1:2], op=ALU.add),
                 reads=[ssA], writes=[ss])
            r_ = rstd_from_ss(ss[0:nt, 0:1], ss, D, 1, nt)
            tmp = f32tmp.take()
            for ob in range(2):
                k.op("dve", lambda eng, ob=ob: eng.scalar_tensor_tensor(
                    out=tmp[0:nt, ob * 512:(ob + 1) * 512], in0=pso[j][ob][0:nt, 0:512], scalar=r_[0:nt, 0:1],
                    in1=G1[0:nt, ob * 512:(ob + 1) * 512], op0=ALU.mult, op1=ALU.mult),
                    reads=[pso[j][ob], r_, G1], writes=[tmp])
            k.op("pool", lambda eng: eng.tensor_tensor(out=xbs[j][0:nt, :], in0=xbs[j][0:nt, :], in1=tmp[0:nt, :], op=ALU.add),
                 reads=[xbs[j], tmp], writes=[xbs[j]])

        vxn = norm_phase1(xbs, subs)
        if nxt_sp is not None:
            zproj(nxt_sp)
        vT = [chunk.take() for _ in range(8)]
        norm_phase2(vxn, subs, offs, A2, B2, s, vT)
        hdnT = [chunk.take() for _ in range(32)]
        if nxt_sp is not None:
            build_dg_block(0)
            build_dg_block(1)
        for cb in range(8):
            sl, W = wnext()
            for cc in range(4):
                fc = cb * 4 + cc
                ps = psf.take()
                mm_group(ps[:, 0:TT], [(W[:, kc, cc * 128:(cc + 1) * 128], vT[kc][:, 0:TT]) for kc in range(8)], [sl] + vT, ps)
                rl = relu_r.take()
                k.op("act", lambda eng, ps=ps, rl=rl: eng.activation(out=rl[:, 0:TT], in_=ps[:, 0:TT], func=AF.Relu), reads=[ps], writes=[rl])
                k.op("pool", lambda eng, rl=rl, fc=fc: eng.tensor_tensor(out=hdnT[fc][:, 0:TT], in0=rl[:, 0:TT], in1=rl[:, 0:TT], op=ALU.mult),
                     reads=[rl], writes=[hdnT[fc]])
        dnbuf = [f32tmp.take() for _ in subs]
        for cb in range(4):
            psd = [psf.take() for _ in subs]
            for kh in range(2):
                sl, W = wnext()
                for j, nt in enumerate(subs):
                    for kk in range(16):
                        fc = kh * 16 + kk
                        fn = lambda eng, j=j, nt=nt, kk=kk, fc=fc: eng.matmul(psd[j][0:nt, 0:256], lhsT=hdnT[fc][:, offs[j]:offs[j] + nt], rhs=W[:, kk, :],
                                                                              start=(kh == 0 and kk == 0), stop=(kh == 1 and kk == 15))
                        if kk < 15:
                            k.pe_quiet(fn, reads=[sl, hdnT[fc]], writes=[psd[j]])
                        else:
                            k.op("pe", fn, reads=[sl, hdnT[fc]], writes=[psd[j]])
            for j, nt in enumerate(subs):
                k.op("act", lambda eng, j=j, nt=nt: eng.activation(out=dnbuf[j][0:nt, cb * 256:(cb + 1) * 256], in_=psd[j][0:nt, 0:256], func=AF.Copy),
                     reads=[psd[j]], writes=[dnbuf[j]])
        for j, nt in enumerate(subs):
            ss = small.take()
            junk_act = junk_r.take()
            k.op("act", lambda eng: eng.activation(out=junk_act[0:nt, :], in_=dnbuf[j][0:nt, :], func=AF.Square, accum_out=ss[0:nt, 0:1]),
                 reads=[dnbuf[j]], writes=[junk_act, ss])
            r_ = rstd_from_ss(ss[0:nt, 0:1], ss, D, 1, nt)
            k.op("dve", lambda eng: eng.scalar_tensor_tensor(out=dnbuf[j][0:nt, :], in0=dnbuf[j][0:nt, :], scalar=r_[0:nt, 0:1], in1=G2[0:nt, :],
                                                             op0=ALU.mult, op1=ALU.mult), reads=[dnbuf[j], r_, G2], writes=[dnbuf[j]])
            k.op("dve", lambda eng: eng.tensor_tensor(out=xbs[j][0:nt, :], in0=xbs[j][0:nt, :], in1=dnbuf[j][0:nt, :], op=ALU.add),
                 reads=[xbs[j], dnbuf[j]], writes=[xbs[j]])
            k.dma("sp", sst_ring, y_rows[offs[j]:offs[j] + nt, :], xbs[j][0:nt, :], reads=[xbs[j]])

    def init_state(zero, s):
        if zero:
            for g in range(NG):
                k.op("pool", lambda eng, g=g: eng.memset(hst[:, g * 256:(g + 1) * 256], 0.0), writes=[hst_res[g]])
                k.op("pool", lambda eng, g=g: eng.memset(hbf[g][:, :], 0.0), writes=[hbf[g]])
            k.op("pool", lambda eng: eng.memset(hist[:], 0.0), writes=[hist])
            for pc in range(8):
                k.op("pool", lambda eng, pc=pc: eng.memset(pp[pc][:, 0:15], 0.0), writes=[pp[pc]])
        else:
            k.dma("sp", ld_ring, hst[:], h0T_d, writes=hst_res)
            for g in range(NG):
                k.op("act", lambda eng, g=g: eng.activation(out=hbf[g][:, :], in_=hst[:, g * 256:(g + 1) * 256], func=AF.Copy),
                     reads=[hst_res[g]], writes=[hbf[g]])
            k.dma("sp", ld_ring, convout[:], convT_d, writes=[convout])
            k.op("dve", lambda eng: eng.tensor_copy(out=hist[:], in_=convout[:]), reads=[convout], writes=[hist])
            for pc in range(8):
                k.dma("sp", ld_ring, pp[pc][:, 0:15], poolT_d[:, pc, :], writes=[pp[pc]])

    def store_state(hT_out, conv_out_d, pool_out_d):
        k.dma("pool", st_ring, hT_out, hst[:], reads=hst_res)
        k.dma("pool", st_ring, conv_out_d, convout[:], reads=[convout])
        for pc in range(8):
            k.dma("pool", st_ring, pool_out_d[:, pc, :], pp[pc][:, 0:15], reads=[pp[pc]])

    psf.bufs.append(psx)
    specs = []
    for ti in range(n_ptiles):
        specs.append(dict(s=0, x_rows=xp_d[ti * TT_MAX:(ti + 1) * TT_MAX, :], y_rows=yp_d[ti * TT_MAX:(ti + 1) * TT_MAX, :],
                          subs=[TSUB] * NSUB, first=(ti == 0), last=(ti == n_ptiles - 1), pos0=(ti == 0)))
    if do_sample:
        specs.append(dict(s=1, x_rows=xs_d, y_rows=ys_d, subs=[DEC], first=True, last=True, pos0=False))
    init_state(True, 0)
    for ti in range(n_ptiles):
        run_tile(specs[ti], specs[ti + 1] if ti + 1 < len(specs) else None)
    store_state(hTp_d, convp_d, poolp_d)
    if do_sample:
        k.dma("sp", ld_ring, G1[:], gs_scr[0], reads=[gs_res], writes=[G1])
        k.dma("sp", ld_ring, G2[:], gs_scr[1], reads=[gs_res], writes=[G2])
        init_state(False, 1)
        run_tile(specs[-1], None)
        store_state(hTs_d, convs_d, pools_d)
    k.finish()
    es.close()
    return nc


def make_in_maps(inputs, n_ptiles=SEQ // TT_MAX):
    f = lambda a: np.ascontiguousarray(np.asarray(a, dtype=np.float32))
    ntok = n_ptiles * TT_MAX

    def col(v, n):
        return f(v).reshape(n, 128).T

    colp = np.zeros((128, NCOL), np.float32)
    colp[:, C_BADA:C_BADA + 48] = col(inputs["b_ada"][0], 48)
    colp[:, C_GPM:C_GPM + 8] = col(inputs["g_pre_mix"][0], 8)
    colp[:, C_GPL:C_GPL + 8] = col(inputs["g_pre_mlp"][0], 8)
    cw = f(inputs["conv_w"][0])
    colp[:, C_CW:C_CW + 128] = cw.reshape(4, 32, 128).transpose(2, 1, 0).reshape(128, 128)
    colp[:, C_CB:C_CB + 32] = col(inputs["conv_b"][0], 32)
    colp[:, C_GSN:C_GSN + 16] = col(inputs["g_ssd_norm"][0], 16)
    colp[:, C_PSC:C_PSC + 8] = col(inputs["pool_scale"][0], 8)
    b_ada = f(inputs["b_ada"][0])
    rbada = np.broadcast_to(np.concatenate([b_ada[2048:3072], b_ada[5120:6144]])[None, :], (128, 2048))
    rgpost = np.broadcast_to(np.concatenate([f(inputs["g_post_mix"][0]), f(inputs["g_post_mlp"][0])])[None, :], (128, 2048))
    rhead = np.broadcast_to(np.concatenate([f(inputs["dt_bias"][0]), f(inputs["a_log"][0]), f(inputs["d_skip"][0])])[None, :], (128, 96))
    rgsn = np.broadcast_to(f(inputs["g_ssd_norm"][0])[None, :], (128, DI))
    shared = dict(
        colp=f(colp), rbada=f(rbada), rgpost=f(rgpost), rhead=f(rhead), rgsn=f(rgsn),
        w_ada=f(inputs["w_ada"][0]), w_in=f(inputs["w_in"][0]), w_so=f(inputs["w_ssd_out"][0]),
        w_pool=f(inputs["w_pool_group"][0]).reshape(1024, 256), w_o=f(inputs["w_o"][0]),
        w_up=f(inputs["w_up"][0]), w_down=f(inputs["w_down"][0]),
    )
    maps = []
    for i in range(8):
        m = dict(shared)
        m["xp"] = f(inputs["x_prompt"][i][:ntok])
        m["xs"] = f(inputs["x_sample"][i])
        cc = np.stack([col(inputs["c_prompt"][i], 8), col(inputs["c_sample"][i], 8)], axis=-1)
        m["ccol"] = f(cc)
        m["h0T"] = f(np.asarray(inputs["state_ssm"][0, i]).reshape(NH * HD, NS).T)
        m["convT"] = f(np.asarray(inputs["state_conv"][0, i]).reshape(3, 32, 128).transpose(2, 1, 0))
        m["poolT"] = f(np.asarray(inputs["state_pool"][0, i]).reshape(15, 8, 128).transpose(2, 1, 0))
        maps.append(m)
    return maps


def gather(results, n_ptiles=SEQ // TT_MAX):
    ntok = n_ptiles * TT_MAX
    yp = np.stack([r["yp"] for r in results]).reshape(8, ntok, D)
    ys = np.stack([r["ys"] for r in results]).reshape(8, DEC, D)

    def ssm(key):
        return np.stack([np.asarray(r[key]).reshape(128, NH, HD).transpose(1, 2, 0) for r in results])[None]

    def conv(key):
        return np.stack([np.asarray(r[key]).reshape(128, 32, 3).transpose(2, 1, 0).reshape(3, CONVD) for r in results])[None]

    def pool(key):
        return np.stack([np.asarray(r[key]).reshape(128, 8, 15).transpose(2, 1, 0).reshape(15, D) for r in results])[None]

    outs = (yp, ys, ssm("hTp"), conv("convp"), pool("poolp"), ssm("hTs"), conv("convs"), pool("pools"))
    return tuple(np.ascontiguousarray(o, dtype=np.float32) for o in outs)


_NC_CACHE = {}


def kernel(**inputs):
    if "nc" not in _NC_CACHE:
        _NC_CACHE["nc"] = build_nc()
    nc = _NC_CACHE["nc"]
    in_maps = make_in_maps(inputs)
    res = run_bass_kernel_spmd(nc, in_maps, core_ids=list(range(8)))
    return gather(res.results)
```

```python
import numpy as np
from contextlib import ExitStack
import concourse.bass as bass
import concourse.mybir as mybir
from concourse.bass_utils import run_bass_kernel_spmd

F32 = mybir.dt.float32
BF16 = mybir.dt.bfloat16
AF = mybir.ActivationFunctionType
ALU = mybir.AluOpType

D = 1024
SEQ = 4096
DEC = 16
DI = 2048
NH = 32
HD = 64
NG = 8
NS = 128
CONVD = 4096
DPROJ = 9248
DFF = 4096
EPS = 1e-6
PAST = 2048
TSUB = 128
NSUB = 2
TT_MAX = TSUB * NSUB
WSLOT_ELEMS = 4096
NWSLOT = 3

C_BADA, C_GPM, C_GPL, C_CW, C_CB, C_GSN, C_PSC, NCOL = 0, 48, 56, 64, 192, 224, 240, 248


class Res:
    __slots__ = ("w", "r")

    def __init__(self):
        self.w = None
        self.r = {}


class Sig:
    def __init__(self, sem, name):
        self.sem = sem
        self.cnt = 0
        self.name = name


class Buf:
    def __init__(self, t, res=None, psum=False):
        self.t = t
        self.res = res if res is not None else Res()
        self.psum = psum

    def __getitem__(self, idx):
        return self.t[idx]


class Ring:
    def __init__(self, bufs):
        self.bufs = bufs
        self.i = 0

    def take(self):
        b = self.bufs[self.i % len(self.bufs)]
        self.i += 1
        return b


class K:
    def __init__(self, nc, es):
        self.nc = nc
        self.es = es
        self.eng = {"pe": nc.tensor, "act": nc.scalar, "dve": nc.vector, "pool": nc.gpsimd, "sp": nc.sync}
        self.sig = {}
        for e in self.eng:
            self.sig[e] = Sig(es.enter_context(nc.semaphore("s_" + e)), e)
        self.known = {e: {} for e in self.eng}
        self.nbuf = 0
        self.dsigs = {}
        self.tag = None

    def sb(self, shape, dt, name=None):
        self.nbuf += 1
        t = self.es.enter_context(self.nc.sbuf_tensor("s_" + (name or f"sb{self.nbuf}"), list(shape), dt))
        return Buf(t)

    def ps(self, shape, dt, name=None):
        self.nbuf += 1
        t = self.es.enter_context(self.nc.psum_tensor(name or f"ps{self.nbuf}", list(shape), dt))
        return Buf(t, psum=True)

    def ring(self, n, shape, dt, name):
        return Ring([self.sb(shape, dt, f"{name}{i}") for i in range(n)])

    def dsig_ring(self, n, name):
        sigs = [Sig(self.es.enter_context(self.nc.semaphore(f"d_{name}{i}")), f"{name}{i}") for i in range(n)]
        self.dsigs[name] = sigs
        return Ring(sigs)

    def _waits(self, e, reads, writes):
        needs = {}

        def add(sigv, same_ok):
            s, v = sigv
            if s is self.sig[e] and not same_ok and e == "pe":
                return
            if needs.get(s, 0) < v:
                needs[s] = v

        for b in reads:
            if b.res.w is not None:
                add(b.res.w, True)
            if b.psum:
                for s, v in b.res.r.items():
                    if s is not self.sig[e]:
                        add((s, v), True)
        for b in writes:
            if b.res.w is not None:
                add(b.res.w, False)
            for s, v in b.res.r.items():
                add((s, v), False)
        kn = self.known[e]
        eng = self.eng[e]
        for s, v in needs.items():
            if kn.get(s, 0) >= v:
                continue
            eng.wait_ge(s.sem, v)
            kn[s] = v

    def op(self, e, fn, reads=(), writes=()):
        reads = [b for b in reads if b is not None]
        writes = [b for b in writes if b is not None]
        self._waits(e, reads, writes)
        inst = fn(self.eng[e])
        sg = self.sig[e]
        sg.cnt += 1
        inst.then_inc(sg.sem, 1)
        for b in reads:
            b.res.r[sg] = sg.cnt
        for b in writes:
            b.res.w = (sg, sg.cnt)
            b.res.r = {}
        return inst

    def pe_quiet(self, fn, reads=(), writes=()):
        reads = [b for b in reads if b is not None]
        writes = [b for b in writes if b is not None]
        self._waits("pe", reads, writes)
        fn(self.eng["pe"])
        sg = self.sig["pe"]
        nxt = sg.cnt + 1
        for b in reads:
            b.res.r[sg] = nxt
        for b in writes:
            b.res.w = (sg, nxt)
            b.res.r = {}

    def dma(self, q, ring, out, in_, reads=(), writes=()):
        reads = [b for b in reads if b is not None]
        writes = [b for b in writes if b is not None]
        self._waits(q, reads, writes)
        sg = ring.take()
        kn = self.known[q]
        if kn.get(sg, 0) < sg.cnt:
            self.eng[q].wait_ge(sg.sem, sg.cnt)
            kn[sg] = sg.cnt
        inst = self.eng[q].dma_start(out=out, in_=in_)
        sg.cnt += 16
        inst.then_inc(sg.sem, 16)
        for b in reads:
            b.res.r[sg] = sg.cnt
        for b in writes:
            b.res.w = (sg, sg.cnt)
            b.res.r = {}

    def finish(self):
        sp = self.eng["sp"]
        for sigs in self.dsigs.values():
            for sg in sigs:
                if sg.cnt > 0:
                    sp.wait_ge(sg.sem, sg.cnt)
        for e, sg in self.sig.items():
            if e != "sp" and sg.cnt > 0:
                sp.wait_ge(sg.sem, sg.cnt)


def build_nc(n_ptiles=SEQ // TT_MAX, do_sample=True):
    nc = bass.Bass("TRN2", target_bir_lowering=False)
    es = ExitStack()
    k = K(nc, es)
    NP_TOK = n_ptiles * TT_MAX

    def din(name, shape, dt=F32):
        return nc.dram_tensor(name, list(shape), dt, kind="ExternalInput").ap()

    def dout(name, shape, dt=F32):
        return nc.dram_tensor(name, list(shape), dt, kind="ExternalOutput").ap()

    def dscr(name, shape, dt=BF16):
        return nc.dram_tensor(name, list(shape), dt, kind="Internal").ap()

    xp_d = din("xp", [NP_TOK, D])
    xs_d = din("xs", [DEC, D])
    ccol_d = din("ccol", [128, 8, 2])
    colp_d = din("colp", [128, NCOL])
    rbada_d = din("rbada", [128, 2048])
    rgpost_d = din("rgpost", [128, 2048])
    rhead_d = din("rhead", [128, 96])
    rgsn_d = din("rgsn", [128, DI])
    h0T_d = din("h0T", [128, DI])
    convT_d = din("convT", [128, 32, 3])
    poolT_d = din("poolT", [128, 8, 15])
    w_ada_d = din("w_ada", [D, 6 * D])
    w_in_d = din("w_in", [D, DPROJ])
    w_so_d = din("w_so", [DI, D])
    w_pool_d = din("w_pool", [D, 256])
    w_o_d = din("w_o", [D, D])
    w_up_d = din("w_up", [D, DFF])
    w_down_d = din("w_down", [DFF, D])

    yp_d = dout("yp", [NP_TOK, D])
    ys_d = dout("ys", [DEC, D])
    hTp_d = dout("hTp", [128, DI])
    hTs_d = dout("hTs", [128, DI])
    convp_d = dout("convp", [128, 32, 3])
    convs_d = dout("convs", [128, 32, 3])
    poolp_d = dout("poolp", [128, 8, 15])
    pools_d = dout("pools", [128, 8, 15])

    wb_ada = dscr("wb_ada", [D, 6 * D])
    wb_in = dscr("wb_in", [D, DPROJ])
    wb_so = dscr("wb_so", [DI, D])
    wb_pool = dscr("wb_pool", [D, 256])
    wb_o = dscr("wb_o", [D, D])
    wb_up = dscr("wb_up", [D, DFF])
    wb_down = dscr("wb_down", [DFF, D])

    ld_ring = k.dsig_ring(12, "ld")
    st_ring = k.dsig_ring(8, "st")
    cast_ring = k.dsig_ring(20, "cast")
    sst_ring = k.dsig_ring(6, "sst")

    def cast(name, src, dst, rows, cols, inner, rstep):
        for r0 in range(0, rows, rstep):
            s_ = src[r0:r0 + rstep, :].rearrange("r (a b) -> r a b", b=inner)
            d_ = dst[r0:r0 + rstep, :].rearrange("r (a b) -> r a b", b=inner)
            sg = cast_ring.take()
            eng = k.eng["pool"]
            kn = k.known["pool"]
            if kn.get(sg, 0) < sg.cnt:
                eng.wait_ge(sg.sem, sg.cnt)
                kn[sg] = sg.cnt
            inst = eng.dma_start(out=d_, in_=s_)
            sg.cnt += 16
            inst.then_inc(sg.sem, 16)
            cast_parts.setdefault(name, []).append((sg, sg.cnt))

    cast_parts = {}

    ident_b = k.sb([128, 128], BF16, "ident_b")
    ones_f = k.sb([128, 128], F32, "ones_f")
    ones_b = k.sb([128, 128], BF16, "ones_b")
    U_f = k.sb([128, 128], F32, "U_f")
    L_f = k.sb([128, 128], F32, "L_f")
    colp = k.sb([128, NCOL], F32, "colp")
    rhead = k.sb([128, 96], F32, "rhead")
    arow = k.sb([128, 32], F32, "arow")
    icnt = k.sb([128, 4, 16], F32, "icnt")
    wpool = k.sb([128, 8, 256], BF16, "wpool")
    ccol = k.sb([128, 8, 2], F32, "ccol")
    scol = k.sb([128, 8, 2], F32, "scol")
    modc = k.sb([128, 32, 2], F32, "modc")
    A1 = k.sb([128, 8, 2], F32, "A1")
    B1 = k.sb([128, 8, 2], F32, "B1")
    A2 = k.sb([128, 8, 2], F32, "A2")
    B2 = k.sb([128, 8, 2], F32, "B2")
    G1 = k.sb([128, D], F32, "G1")
    G2 = k.sb([128, D], F32, "G2")
    gs_scr = nc.dram_tensor("gs_scr", [2, 128, D], F32, kind="Internal").ap()
    gs_res = Buf(None)

    psf = Ring([k.ps([128, 512], F32, f"psf{i}") for i in range(7)])
    psx = k.ps([128, 512], F32, "psx")

    def psb_view(b):
        return b.t[:].bitcast(BF16)

    wslots = Ring([k.sb([128, WSLOT_ELEMS], BF16, f"wsl{i}") for i in range(NWSLOT)])
    xbuf = Ring([k.sb([128, D], F32, f"xb{i}") for i in range(2 * NSUB)])
    f32tmp = Ring([k.sb([128, D], F32, f"ft{i}") for i in range(2)])
    xnbuf = Ring([k.sb([128, D], BF16, f"xn{i}") for i in range(2)])
    junk_r = Ring([k.sb([128, D], BF16, f"junk{i}") for i in range(1)])
    chunk = Ring([k.sb([128, TT_MAX], BF16, f"ch{i}") for i in range(40)])
    uT_sets = Ring([[k.sb([128, TT_MAX], BF16, f"uT{a}_{i}") for i in range(8)] for a in range(2)])
    xs_tok = [k.sb([128, DI], BF16, f"xstok{i}") for i in range(NSUB)]
    B_tok = [k.sb([128, NG * NS], BF16, f"btok{i}") for i in range(NSUB)]
    zs_tok = [k.sb([128, DI], BF16, f"zstok{i}") for i in range(NSUB)]
    pbring = Ring([k.sb([128, TT_MAX + 3], BF16, f"pb{i}") for i in range(8)])
    xf_r = Ring([k.sb([128, TT_MAX], BF16, f"xf{i}") for i in range(8)])
    dgring = Ring([k.sb([128, 4, 128], BF16, f"dg{i}") for i in range(8)])
    dg_ready = {}
    hist = k.sb([128, 32, 3], BF16, "hist")
    convout = k.sb([128, 32, 3], F32, "convout")
    pp = [k.sb([128, 15 + TT_MAX], F32, f"pp{i}") for i in range(8)]
    psc = Ring([k.sb([128, 15 + TT_MAX], F32, f"psc{i}") for i in range(3)])
    small = Ring([k.sb([128, 32], F32, f"sm{i}") for i in range(32)])
    dtbuf = [k.sb([128, 32], F32, f"dt{i}") for i in range(NSUB)]
    cbm_r = Ring([k.sb([128, 128], F32, f"cbm{i}") for i in range(3)])
    W_r = Ring([k.sb([128, 128], F32, f"Wh{i}") for i in range(12)])
    M_r = Ring([k.sb([128, 4, 128], BF16, f"Mg{i}") for i in range(3)])
    xg_r = Ring([k.sb([128, 3, 256], BF16, f"xg{i}") for i in range(4)])
    ygT_all = k.sb([128, 16, TT_MAX], BF16, "ygT_all")
    gsn_bc = k.sb([128, DI], BF16, "gsn_bc")
    tmp256 = Ring([k.sb([128, 256], F32, f"t256_{i}") for i in range(2)])
    ybf_r = Ring([k.sb([128, 256], BF16, f"ybf{i}") for i in range(3)])
    yz_r = Ring([k.sb([128, DI], BF16, f"yz{i}") for i in range(2)])
    for b_ in yz_r.bufs:
        b_.subs = [Buf(b_.t) for _ in range(NG)]
    hst = k.sb([128, DI], F32, "hst")
    hbf = [k.sb([128, 256], BF16, f"hbf{g}") for g in range(NG)]
    ab_r = Ring([k.sb([128, TT_MAX], F32, f"ab{i}") for i in range(2)])
    relu_r = Ring([k.sb([128, TT_MAX], BF16, f"rl{i}") for i in range(3)])

    hst_res = [Buf(None) for _ in range(NG)]

    def memset(e, buf, val):
        k.op(e, lambda eng: eng.memset(buf[:], val), writes=[buf])

    memset("pool", ones_f, 1.0)
    memset("pool", ones_b, 1.0)
    k.op("pool", lambda eng: eng.affine_select(out=ident_b[:], in_=ones_b[:], pattern=[[-1, 128]], compare_op=ALU.is_equal,
                                               fill=0.0, base=0, channel_multiplier=1), reads=[ones_b], writes=[ident_b])
    k.op("pool", lambda eng: eng.affine_select(out=U_f[:], in_=ones_f[:], pattern=[[1, 128]], compare_op=ALU.is_ge,
                                               fill=0.0, base=0, channel_multiplier=-1), reads=[ones_f], writes=[U_f])
    k.op("pool", lambda eng: eng.affine_select(out=L_f[:], in_=ones_f[:], pattern=[[-1, 128]], compare_op=ALU.is_gt,
                                               fill=0.0, base=0, channel_multiplier=1), reads=[ones_f], writes=[L_f])

    cast("w_ada", w_ada_d, wb_ada, D, 6 * D, 2048, 256)
    cast("w_in", w_in_d, wb_in, D, DPROJ, 1156, 256)
    cast("w_pool", w_pool_d, wb_pool, D, 256, 256, 1024)
    cast("w_so", w_so_d, wb_so, DI, D, 1024, 1024)
    cast("w_o", w_o_d, wb_o, D, D, 1024, 1024)
    cast("w_up", w_up_d, wb_up, D, DFF, 2048, 512)
    cast("w_down", w_down_d, wb_down, DFF, D, 1024, 1024)

    def wait_cast(q, name):
        kn = k.known[q]
        for sg, v in cast_parts[name]:
            if kn.get(sg, 0) < v:
                k.eng[q].wait_ge(sg.sem, v)
                kn[sg] = v

    k.dma("sp", ld_ring, colp[:], colp_d, writes=[colp])
    k.dma("sp", ld_ring, rhead[:], rhead_d, writes=[rhead])
    k.dma("sp", ld_ring, ccol[:], ccol_d, writes=[ccol])
    k.dma("pool", st_ring, gsn_bc[:], rgsn_d, writes=[gsn_bc])
    k.op("act", lambda eng: eng.activation(out=arow[:], in_=rhead[:, 32:64], func=AF.Exp), reads=[rhead], writes=[arow])
    k.op("dve", lambda eng: eng.tensor_scalar(out=arow[:], in0=arow[:], scalar1=-1.0, scalar2=None, op0=ALU.mult),
         reads=[arow], writes=[arow])
    iot = small.take()
    k.op("pool", lambda eng: eng.iota(iot[:, 0:16], pattern=[[1, 16]], base=1, channel_multiplier=0,
                                      allow_small_or_imprecise_dtypes=True), writes=[iot])
    for gi, w in enumerate((2, 4, 8, 16)):
        t_ = small.take()
        k.op("dve", lambda eng, t_=t_, w=w: eng.tensor_scalar(out=t_[:, 0:16], in0=iot[:, 0:16], scalar1=float(w), scalar2=None,
                                                              op0=ALU.min), reads=[iot], writes=[t_])
        k.op("dve", lambda eng, t_=t_, gi=gi: eng.reciprocal(out=icnt[:, gi, :], in_=t_[:, 0:16]), reads=[t_], writes=[icnt])

    k.op("act", lambda eng: eng.activation(out=scol[:], in_=ccol[:], func=AF.Silu), reads=[ccol], writes=[scol])
    scol_b = k.sb([128, 8, 2], BF16, "scol_b")
    k.op("dve", lambda eng: eng.tensor_copy(out=scol_b[:], in_=scol[:]), reads=[scol], writes=[scol_b])
    screp = []
    for s in range(2):
        v = xs_tok[s].t[:, 0:1024].rearrange("p (a b) -> p a b", b=128)
        screp.append(v)
        k.op("dve", lambda eng, s=s, v=v: eng.tensor_copy(out=v, in_=scol[:, :, s:s + 1].broadcast_to([128, 8, 128])),
             reads=[scol], writes=[xs_tok[s]])
    ps_col = psx
    col_chunks = list(range(0, 16)) + list(range(24, 40))
    colidx = {ch: mi for mi, ch in enumerate(col_chunks)}
    wb_ada_v = wb_ada.rearrange("(kc p) c -> p kc c", p=128)
    gload = {0: (f32tmp.bufs[0], f32tmp.bufs[1]), 1: (xbuf.bufs[0], xbuf.bufs[1])}
    for which in range(2):
        rbt, rgt = gload[which]
        k.dma("sp", ld_ring, rbt[:], rbada_d[:, which * D:(which + 1) * D], writes=[rbt])
        k.dma("sp", ld_ring, rgt[:], rgpost_d[:, which * D:(which + 1) * D], writes=[rgt])
    wait_cast("sp", "w_ada")
    for blk in range(12):
        wsl = wslots.take()
        wv = wsl.t[:, 0:4096].rearrange("p (a b) -> p a b", b=512)
        k.dma("sp", ld_ring, wv, wb_ada_v[:, :, blk * 512:(blk + 1) * 512], writes=[wsl])
        if blk * 4 in colidx:
            for cc in range(4):
                mi = colidx[blk * 4 + cc]
                for kc in range(8):
                    fn = lambda eng, cc=cc, mi=mi, kc=kc: eng.matmul(ps_col[:, mi * 2:mi * 2 + 2], lhsT=wv[:, kc, cc * 128:(cc + 1) * 128],
                                                                     rhs=scol_b[:, kc, :], start=(kc == 0), stop=(kc == 7))
                    if kc < 7:
                        k.pe_quiet(fn, reads=[wsl, scol_b], writes=[ps_col])
                    else:
                        k.op("pe", fn, reads=[wsl, scol_b], writes=[ps_col])
        else:
            which, q = (0, blk - 4) if blk < 6 else (1, blk - 10)
            rbt, rgt = gload[which]
            for s in range(2):
                psr = psf.take()
                for kc in range(8):
                    fn = lambda eng, s=s, kc=kc, psr=psr: eng.matmul(psr[:, 0:512], lhsT=screp[s][:, kc, :], rhs=wv[:, kc, :],
                                                                     start=(kc == 0), stop=(kc == 7))
                    if kc < 7:
                        k.pe_quiet(fn, reads=[wsl, xs_tok[s]], writes=[psr])
                    else:
                        k.op("pe", fn, reads=[wsl, xs_tok[s]], writes=[psr])
                if s == 0:
                    Gb = G1 if which == 0 else G2
                    dst = Gb[:, q * 512:(q + 1) * 512]
                    wr = [Gb]
                else:
                    dst = hst[:, which * D + q * 512:which * D + (q + 1) * 512]
                    wr = hst_res
                k.op("dve", lambda eng, psr=psr, dst=dst, q=q, rbt=rbt: eng.tensor_tensor(out=dst, in0=psr[:, 0:512],
                                                                                          in1=rbt[:, q * 512:(q + 1) * 512], op=ALU.add),
                     reads=[psr, rbt], writes=wr)
                k.op("dve", lambda eng, dst=dst, q=q, rgt=rgt: eng.tensor_tensor(out=dst, in0=dst, in1=rgt[:, q * 512:(q + 1) * 512], op=ALU.mult),
                     reads=wr + [rgt], writes=wr)
    k.dma("sp", ld_ring, gs_scr.rearrange("w p d -> p w d"), hst[:].rearrange("p (w d) -> p w d", w=2), reads=hst_res, writes=[gs_res])
    for qi, (c0, b0) in enumerate(((0, 0), (8, 8), (16, 24), (24, 32))):
        k.op("dve", lambda eng, c0=c0, b0=b0: eng.tensor_tensor(
            out=modc[:, c0:c0 + 8, :], in0=ps_col[:, c0 * 2:(c0 + 8) * 2].rearrange("p (a b) -> p a b", b=2),
            in1=colp[:, C_BADA + b0:C_BADA + b0 + 8].unsqueeze(2).broadcast_to([128, 8, 2]), op=ALU.add),
            reads=[ps_col, colp], writes=[modc])
    for (Aout, Bout, sh0, sc0, gcol) in ((A1, B1, 0, 8, C_GPM), (A2, B2, 16, 24, C_GPL)):
        k.op("dve", lambda eng, Aout=Aout, sc0=sc0: eng.tensor_scalar(out=Aout[:], in0=modc[:, sc0:sc0 + 8, :], scalar1=1.0, scalar2=None,
                                                                       op0=ALU.add), reads=[modc], writes=[Aout])
        k.op("dve", lambda eng, Aout=Aout, gcol=gcol: eng.tensor_tensor(out=Aout[:], in0=Aout[:],
                                                                         in1=colp[:, gcol:gcol + 8].unsqueeze(2).broadcast_to([128, 8, 2]),
                                                                         op=ALU.mult), reads=[Aout, colp], writes=[Aout])
        k.op("dve", lambda eng, Bout=Bout, sh0=sh0: eng.tensor_copy(out=Bout[:], in_=modc[:, sh0:sh0 + 8, :]), reads=[modc], writes=[Bout])

    wait_cast("sp", "w_pool")
    k.dma("sp", ld_ring, wpool[:], wb_pool.rearrange("(a p) c -> p a c", p=128), writes=[wpool])

    wb_in_v = wb_in.rearrange("(kc p) c -> p kc c", p=128)
    wb_so_v = wb_so.rearrange("(kc p) c -> p kc c", p=128)
    wb_o_v = wb_o.rearrange("(kc p) c -> p kc c", p=128)
    wb_up_v = wb_up.rearrange("(kc p) c -> p kc c", p=128)
    wb_down_v = wb_down.rearrange("(kc p) c -> p kc c", p=128)

    Zb, MIDb, ENDb = [], [], []
    for cb in range(4):
        Zb.append(("w_in", wb_in_v[:, :, cb * 512:(cb + 1) * 512], 8, 512))
    for cb in range(8):
        MIDb.append(("w_in", wb_in_v[:, :, 2048 + cb * 512:2048 + (cb + 1) * 512], 8, 512))
    MIDb.append(("w_in", wb_in_v[:, :, 6144:6176], 8, 32))
    for cb in range(2):
        MIDb.append(("w_in", wb_in_v[:, :, 6176 + cb * 512:6176 + (cb + 1) * 512], 8, 512))
    for cb in range(4):
        MIDb.append(("w_in", wb_in_v[:, :, 7200 + cb * 512:7200 + (cb + 1) * 512], 8, 512))
    for cb in range(4):
        MIDb.append(("w_so", wb_so_v[:, :, cb * 256:(cb + 1) * 256], 16, 256))
    for cb in range(2):
        MIDb.append(("w_o", wb_o_v[:, :, cb * 512:(cb + 1) * 512], 8, 512))
    for cb in range(8):
        ENDb.append(("w_up", wb_up_v[:, :, cb * 512:(cb + 1) * 512], 8, 512))
    for cb in range(4):
        for kh in range(2):
            ENDb.append(("w_down", wb_down_v[:, kh * 16:(kh + 1) * 16, cb * 256:(cb + 1) * 256], 16, 256))
    n_pass = n_ptiles + (1 if do_sample else 0)
    wseq = list(Zb)
    for n_ in range(n_pass):
        wseq += MIDb
        if n_ + 1 < n_pass:
            wseq += Zb
        wseq += ENDb
    wstate = {"issued": 0, "next": 0, "bufs": {}, "seen": set()}
    total_blocks = len(wseq)

    def w_issue(upto):
        while wstate["issued"] < min(upto, total_blocks):
            i = wstate["issued"]
            name, src, nk, ncol = wseq[i]
            if name not in wstate["seen"]:
                wstate["seen"].add(name)
                wait_cast("sp", name)
            sl = wslots.take()
            dst = sl[:, 0:nk * ncol].rearrange("p (a b) -> p a b", b=ncol)
            k.dma("sp", ld_ring, dst, src, writes=[sl])
            wstate["bufs"][i] = (sl, nk, ncol)
            wstate["issued"] += 1

    def wnext():
        i = wstate["next"]
        wstate["next"] += 1
        w_issue(i + NWSLOT)
        sl, nk, ncol = wstate["bufs"].pop(i)
        view = sl[:, 0:nk * ncol].rearrange("p (a b) -> p a b", b=ncol)
        return sl, view

    def mm_group(out_ap, pairs, reads, psbuf):
        n = len(pairs)
        for i, (l, r) in enumerate(pairs):
            fn = lambda eng, l=l, r=r, i=i: eng.matmul(out_ap, lhsT=l, rhs=r, start=(i == 0), stop=(i == n - 1))
            if i < n - 1:
                k.pe_quiet(fn, reads=reads, writes=[psbuf])
            else:
                k.op("pe", fn, reads=reads, writes=[psbuf])

    def rstd_from_ss(ss_ap, ss_buf, n_feat, width, nt):
        ln_ = small.take()
        k.op("act", lambda eng: eng.activation(out=ln_[0:nt, 0:width], in_=ss_ap, func=AF.Ln, bias=EPS, scale=1.0 / n_feat),
             reads=[ss_buf], writes=[ln_])
        r_ = small.take()
        k.op("act", lambda eng: eng.activation(out=r_[0:nt, 0:width], in_=ln_[0:nt, 0:width], func=AF.Exp, scale=-0.5),
             reads=[ln_], writes=[r_])
        return r_

    def norm_phase1(xbs_, subs_):
        n = len(subs_)
        sss, rs, xns = [], [], []
        for j in range(n):
            nt = subs_[j]
            ss = small.take()
            junk_act = junk_r.take()
            k.op("act", lambda eng: eng.activation(out=junk_act[0:nt, :], in_=xbs_[j][0:nt, :], func=AF.Square, accum_out=ss[0:nt, 0:1]),
                 reads=[xbs_[j]], writes=[junk_act, ss])
            sss.append(ss)
        for j in range(n):
            nt = subs_[j]
            rs.append(rstd_from_ss(sss[j][0:nt, 0:1], sss[j], D, 1, nt))
        for j in range(n):
            nt = subs_[j]
            xn = xnbuf.take()
            k.op("dve", lambda eng: eng.tensor_scalar(out=xn[0:nt, :], in0=xbs_[j][0:nt, :], scalar1=rs[j][0:nt, 0:1], scalar2=None, op0=ALU.mult),
                 reads=[xbs_[j], rs[j]], writes=[xn])
            xns.append(xn)
        return xns

    def norm_phase2(xns, subs_, offs_, Acol, Bcol, s, dstT):
        n = len(subs_)
        pbs_ = []
        for j in range(n):
            nt = subs_[j]
            pb = psf.take()
            pv = psb_view(pb)
            for kc in range(8):
                (k.op if kc == 7 else k.pe_quiet)(*(("pe",) if kc == 7 else ()),
                    lambda eng: eng.transpose(out=pv[:, kc * 128:kc * 128 + nt], in_=xns[j][0:nt, kc * 128:(kc + 1) * 128],
                                              identity=ident_b[0:nt, 0:nt]), reads=[xns[j], ident_b], writes=[pb])
            pbs_.append(pb)
        for j in range(n):
            nt = subs_[j]
            pv = psb_view(pbs_[j])
            c0 = offs_[j]
            for kc in range(8):
                k.op("dve", lambda eng: eng.tensor_scalar(out=dstT[kc][:, c0:c0 + nt], in0=pv[:, kc * 128:kc * 128 + nt],
                                                          scalar1=Acol[:, kc, s:s + 1], scalar2=Bcol[:, kc, s:s + 1],
                                                          op0=ALU.mult, op1=ALU.add),
                     reads=[pbs_[j], Acol, Bcol], writes=[dstT[kc]])

    def build_dg_block(cb):
        if cb in dg_ready:
            return
        dgs = []
        for cc in range(4):
            c = cb * 4 + cc
            dg = dgring.take()
            k.op("dve", lambda eng: eng.tensor_tensor(out=dg[:, :, :], in0=ident_b[:, :].unsqueeze(1).broadcast_to([128, 4, 128]),
                                                      in1=colp[:, C_CW + c * 4:C_CW + c * 4 + 4].unsqueeze(2).broadcast_to([128, 4, 128]),
                                                      op=ALU.mult), reads=[ident_b, colp], writes=[dg])
            dgs.append(dg)
        dg_ready[cb] = dgs

    def prep_load(sp):
        subs = sp["subs"]
        offs = [sum(subs[:j]) for j in range(len(subs))]
        sp["xbs"] = []
        for j, nt in enumerate(subs):
            xb = xbuf.take()
            k.dma("sp", ld_ring, xb[0:nt, :], sp["x_rows"][offs[j]:offs[j] + nt, :], writes=[xb])
            sp["xbs"].append(xb)

    def prep_ew(sp):
        sp["xns"] = norm_phase1(sp["xbs"], sp["subs"])

    def prep_pe(sp):
        subs = sp["subs"]
        offs = [sum(subs[:j]) for j in range(len(subs))]
        sp["uT"] = uT_sets.take()
        norm_phase2(sp["xns"], subs, offs, A1, B1, sp["s"], sp["uT"])

    def zproj(sp):
        subs = sp["subs"]
        offs = [sum(subs[:j]) for j in range(len(subs))]
        uT = sp["uT"]
        for cb in range(4):
            sl, W = wnext()
            for j, nt in enumerate(subs):
                ps = psf.take()
                mm_group(ps[0:nt, 0:512], [(uT[kc][:, offs[j]:offs[j] + nt], W[:, kc, :]) for kc in range(8)], [sl] + uT, ps)
                k.op("act", lambda eng, ps=ps, j=j, nt=nt, cb=cb: eng.activation(out=zs_tok[j][0:nt, cb * 512:(cb + 1) * 512], in_=ps[0:nt, 0:512],
                                                                                  func=AF.Silu), reads=[ps], writes=[zs_tok[j]])
        sp["ready"] = True

    def run_tile(sp, nxt_sp):
        s, y_rows, subs, first, last, pos0_is_zero = sp["s"], sp["y_rows"], sp["subs"], sp["first"], sp["last"], sp["pos0"]
        nsub = len(subs)
        TT = sum(subs)
        offs = [sum(subs[:j]) for j in range(nsub)]
        if not sp.get("ready"):
            prep_load(sp)
            prep_ew(sp)
            prep_pe(sp)
            zproj(sp)
        if nxt_sp is not None:
            prep_load(nxt_sp)
        xbs = sp["xbs"]
        uT = sp["uT"]

        BT = [chunk.take() for _ in range(NG)]
        CT = [chunk.take() for _ in range(NG)]
        def emit_transposes(cb_, srcs_):
            for j, nt in enumerate(subs):
                pb = psf.take()
                pv = psb_view(pb)
                for cc in range(4):
                    (k.op if cc == 3 else k.pe_quiet)(*(("pe",) if cc == 3 else ()),
                        lambda eng: eng.transpose(out=pv[0:nt, cc * 128:(cc + 1) * 128], in_=srcs_[cc][:, offs[j]:offs[j] + nt],
                                                  identity=ident_b[:, :]), reads=[srcs_[cc], ident_b], writes=[pb])
                dst = xs_tok[j][0:nt, cb_ * 512:(cb_ + 1) * 512] if cb_ < 4 else B_tok[j][0:nt, (cb_ - 4) * 512:(cb_ - 3) * 512]
                dbuf = xs_tok[j] if cb_ < 4 else B_tok[j]
                k.op("dve", lambda eng: eng.tensor_copy(out=dst, in_=pv[0:nt, 0:512]), reads=[pb], writes=[dbuf])

        def xbc_proj(cb):
            sl, W = wnext()
            pbs = []
            for cc in range(4):
                c = cb * 4 + cc
                ps = psf.take()
                mm_group(ps[:, 0:TT], [(W[:, kc, cc * 128:(cc + 1) * 128], uT[kc][:, 0:TT]) for kc in range(8)], [sl] + uT, ps)
                Pb = pbring.take()
                k.op("pool", lambda eng: eng.tensor_copy(out=Pb[:, 0:3], in_=hist[:, c, :]), reads=[hist], writes=[Pb])
                k.op("act", lambda eng: eng.activation(out=Pb[:, 3:3 + TT], in_=ps[:, 0:TT], func=AF.Copy),
                     reads=[ps], writes=[Pb])
                k.op("pool", lambda eng: eng.tensor_copy(out=hist[:, c, :], in_=Pb[:, TT:TT + 3]), reads=[Pb], writes=[hist])
                if last:
                    k.op("dve", lambda eng: eng.tensor_copy(out=convout[:, c, :], in_=ps[:, TT - 3:TT]),
                         reads=[ps], writes=[convout])
                pbs.append(Pb)
            return pbs

        def xbc_conv(cb, pbs):
            dgs = dg_ready.pop(cb)
            srcs = []
            for cc in range(4):
                c = cb * 4 + cc
                if c < 16:
                    dstb = xf_r.take()
                elif c < 24:
                    dstb = BT[c - 16]
                else:
                    dstb = CT[c - 24]
                ps3 = psf.take()
                mm_group(ps3[:, 0:TT], [(dgs[cc][:, kk, :], pbs[cc][:, kk:kk + TT]) for kk in range(4)], [pbs[cc], dgs[cc]], ps3)
                k.op("act", lambda eng: eng.activation(out=dstb[:, 0:TT], in_=ps3[:, 0:TT], func=AF.Silu,
                                                       bias=colp[:, C_CB + c:C_CB + c + 1]),
                     reads=[ps3, colp], writes=[dstb])
                srcs.append(dstb)
            return srcs

        pend_pb = {}
        pend_src = {}
        for it_ in range(8 + 2):
            if it_ < 8:
                build_dg_block(it_)
                pend_pb[it_] = xbc_proj(it_)
            if 0 <= it_ - 1 < 8:
                pend_src[it_ - 1] = xbc_conv(it_ - 1, pend_pb.pop(it_ - 1))
            if 0 <= it_ - 2 < 6:
                emit_transposes(it_ - 2, pend_src.pop(it_ - 2))
        sl, W = wnext()
        for j, nt in enumerate(subs):
            ps = psf.take()
            mm_group(ps[0:nt, 0:32], [(uT[kc][:, offs[j]:offs[j] + nt], W[:, kc, :]) for kc in range(8)], [sl] + uT, ps)
            t1 = small.take()
            k.op("dve", lambda eng, ps=ps, t1=t1, nt=nt: eng.tensor_tensor(out=t1[0:nt, :], in0=ps[0:nt, 0:32], in1=rhead[0:nt, 0:32], op=ALU.add),
                 reads=[ps, rhead], writes=[t1])
            t2 = small.take()
            k.op("act", lambda eng, t1=t1, t2=t2, nt=nt: eng.activation(out=t2[0:nt, :], in_=t1[0:nt, :], func=AF.Exp), reads=[t1], writes=[t2])
            k.op("act", lambda eng, t2=t2, j=j, nt=nt: eng.activation(out=dtbuf[j][0:nt, :], in_=t2[0:nt, :], func=AF.Ln, bias=1.0),
                 reads=[t2], writes=[dtbuf[j]])
        pmT = [chunk.take() for _ in range(8)]
        for cb in range(2):
            sl, W = wnext()
            for cc in range(4):
                pc = cb * 4 + cc
                gi = pc // 2
                w = (2, 4, 8, 16)[gi]
                ps = psf.take()
                mm_group(ps[:, 0:TT], [(W[:, kc, cc * 128:(cc + 1) * 128], uT[kc][:, 0:TT]) for kc in range(8)], [sl] + uT, ps)
                ppb = pp[pc]
                k.op("act", lambda eng, ps=ps, ppb=ppb: eng.activation(out=ppb[:, 15:15 + TT], in_=ps[:, 0:TT], func=AF.Copy),
                     reads=[ps], writes=[ppb])
                cur = ppb
                step = 1
                lo = 0
                while step < w:
                    nxt = psc.take()
                    lo2 = lo + step
                    k.op("pool", lambda eng, cur=cur, nxt=nxt, lo2=lo2, step=step: eng.tensor_tensor(
                        out=nxt[:, lo2:15 + TT], in0=cur[:, lo2:15 + TT], in1=cur[:, lo2 - step:15 + TT - step], op=ALU.add),
                        reads=[cur], writes=[nxt])
                    cur = nxt
                    lo = lo2
                    step *= 2
                k.op("dve", lambda eng, cur=cur, ppb=ppb, pc=pc, w=w: eng.scalar_tensor_tensor(
                    out=pmT[pc][:, 0:TT], in0=cur[:, 15:15 + TT], scalar=1.0 / w, in1=ppb[:, 15:15 + TT], op0=ALU.mult, op1=ALU.subtract),
                    reads=[cur, ppb], writes=[pmT[pc]])
                if pos0_is_zero:
                    t_ = small.take()
                    k.op("dve", lambda eng, cur=cur, t_=t_, gi=gi: eng.tensor_tensor(out=t_[:, 0:16], in0=cur[:, 15:31], in1=icnt[:, gi, :], op=ALU.mult),
                         reads=[cur, icnt], writes=[t_])
                    k.op("dve", lambda eng, t_=t_, ppb=ppb, pc=pc: eng.tensor_tensor(out=pmT[pc][:, 0:16], in0=t_[:, 0:16], in1=ppb[:, 15:31], op=ALU.subtract),
                         reads=[t_, ppb], writes=[pmT[pc]])
                k.op("pool", lambda eng, ppb=ppb: eng.tensor_copy(out=ppb[:, 0:15], in_=ppb[:, TT:TT + 15]), reads=[ppb], writes=[ppb])
        gT = [chunk.take() for _ in range(16)]
        for cb in range(4):
            sl, W = wnext()
            for cc in range(4):
                gc = cb * 4 + cc
                ps = psf.take()
                mm_group(ps[:, 0:TT], [(W[:, kc, cc * 128:(cc + 1) * 128], uT[kc][:, 0:TT]) for kc in range(8)], [sl] + uT, ps)
                k.op("act", lambda eng, ps=ps, gc=gc: eng.activation(out=gT[gc][:, 0:TT], in_=ps[:, 0:TT], func=AF.Sigmoid),
                     reads=[ps], writes=[gT[gc]])

        chunks = []
        for j, nt in enumerate(subs):
            o = offs[j]
            dtj = dtbuf[j]
            da = small.take()
            k.op("dve", lambda eng: eng.tensor_tensor(out=da[0:nt, :], in0=dtj[0:nt, :], in1=arow[0:nt, :], op=ALU.mult),
                 reads=[dtj, arow], writes=[da])
            pss = psf.take()
            k.op("pe", lambda eng: eng.matmul(pss[0:nt, 0:32], lhsT=U_f[0:nt, 0:nt], rhs=da[0:nt, :], start=True, stop=True),
                 reads=[U_f, da], writes=[pss])
            k.op("pe", lambda eng: eng.matmul(pss[:, 32:64], lhsT=ones_f[0:nt, :], rhs=da[0:nt, :], start=True, stop=True),
                 reads=[ones_f, da], writes=[pss])
            eacs = small.take()
            k.op("act", lambda eng: eng.activation(out=eacs[0:nt, :], in_=pss[0:nt, 0:32], func=AF.Exp), reads=[pss], writes=[eacs])
            cdB = small.take()
            k.op("act", lambda eng: eng.activation(out=cdB[:, :], in_=pss[:, 32:64], func=AF.Exp), reads=[pss], writes=[cdB])
            acs = small.take()
            k.op("act", lambda eng: eng.activation(out=acs[0:nt, :], in_=pss[0:nt, 0:32], func=AF.Copy), reads=[pss], writes=[acs])
            dif = small.take()
            k.op("dve", lambda eng: eng.tensor_tensor(out=dif[0:nt, :], in0=pss[0:nt, 32:64], in1=acs[0:nt, :], op=ALU.subtract),
                 reads=[pss, acs], writes=[dif])
            dte = small.take()
            k.op("act", lambda eng: eng.activation(out=dte[0:nt, :], in_=dif[0:nt, :], func=AF.Exp), reads=[dif], writes=[dte])
            wts = small.take()
            k.op("dve", lambda eng: eng.tensor_tensor(out=wts[0:nt, :], in0=dte[0:nt, :], in1=dtj[0:nt, :], op=ALU.mult),
                 reads=[dte, dtj], writes=[wts])
            chunks.append(dict(nt=nt, o=o, dtj=dtj, da=da, eacs=eacs, cdB=cdB, wts=wts, yz=yz_r.take(), ssg=small.take(), j=j))

        items = [(j, g) for j in range(nsub) for g in range(NG)]
        ist = {}

        def S0(it):
            j, g = it
            c = chunks[j]
            nt = c["nt"]
            Ws = []
            for hh in range(4):
                h = g * 4 + hh
                Wh = W_r.take()
                if hh < 3:
                    k.op("act", lambda eng: eng.activation(out=Wh[0:nt, 0:nt], in_=L_f[0:nt, 0:nt], func=AF.Copy, scale=c["da"][0:nt, h:h + 1]),
                         reads=[L_f, c["da"]], writes=[Wh])
                else:
                    k.op("dve", lambda eng: eng.tensor_scalar(out=Wh[0:nt, 0:nt], in0=L_f[0:nt, 0:nt], scalar1=c["da"][0:nt, h:h + 1],
                                                              scalar2=None, op0=ALU.mult), reads=[L_f, c["da"]], writes=[Wh])
                Ws.append(Wh)
            xg = xg_r.take()
            xs_g = xs_tok[j][0:nt, g * 256:(g + 1) * 256].rearrange("p (h d) -> p h d", d=HD)
            for qi, src in enumerate((c["wts"], rhead, c["dtj"])):
                col0 = 64 if qi == 1 else 0
                k.op("pool",
                     lambda eng: eng.tensor_tensor(out=xg[0:nt, qi, :].rearrange("p (h d) -> p h d", d=HD), in0=xs_g,
                                                   in1=src[0:nt, col0 + g * 4:col0 + (g + 1) * 4].unsqueeze(2).broadcast_to([nt, 4, HD]),
                                                   op=ALU.mult), reads=[xs_tok[j], src], writes=[xg])
            gs_ = slice(g * 256, (g + 1) * 256)
            hres = hst_res[g]
            k.op("pool", lambda eng: eng.tensor_tensor(out=hst[:, gs_].rearrange("p (h d) -> p h d", d=HD),
                                                       in0=hst[:, gs_].rearrange("p (h d) -> p h d", d=HD),
                                                       in1=c["cdB"][:, g * 4:(g + 1) * 4].unsqueeze(2).broadcast_to([128, 4, HD]), op=ALU.mult),
                 reads=[hres, c["cdB"]], writes=[hres])
            ist[it] = dict(W=Ws, xg=xg)

        def S1a(it):
            j, g = it
            c = chunks[j]
            nt, o = c["nt"], c["o"]
            pcb = ps_cb.take()
            k.op("pe", lambda eng: eng.matmul(pcb[0:nt, 0:nt], lhsT=BT[g][:, o:o + nt], rhs=CT[g][:, o:o + nt], start=True, stop=True),
                 reads=[BT[g], CT[g]], writes=[pcb])
            cbm = cbm_r.take()
            k.op("dve", lambda eng: eng.tensor_tensor(out=cbm[0:nt, 0:nt], in0=pcb[0:nt, 0:nt], in1=U_f[0:nt, 0:nt], op=ALU.mult),
                 reads=[pcb, U_f], writes=[cbm])
            ist[it].update(cbm=cbm)

        def S1b(it):
            j, g = it
            c = chunks[j]
            nt = c["nt"]
            pseg = ps_seg.take()
            for hh in range(4):
                Wh = ist[it]["W"][hh]
                (k.op if hh == 3 else k.pe_quiet)(*(("pe",) if hh == 3 else ()),
                    lambda eng: eng.matmul(pseg[0:nt, hh * 128:hh * 128 + nt], lhsT=Wh[0:nt, 0:nt], rhs=U_f[0:nt, 0:nt],
                                           start=True, stop=True), reads=[Wh, U_f], writes=[pseg])
            ist[it].update(pseg=pseg)

        def S2a(it):
            j, g = it
            nt = chunks[j]["nt"]
            pseg = ist[it]["pseg"]
            segv = pseg[0:nt, :].rearrange("p (h l) -> p h l", l=128)[:, :, 0:nt]
            k.op("act", lambda eng: eng.activation(out=segv, in_=segv, func=AF.Exp), reads=[pseg], writes=[pseg])

        def S2b(it):
            j, g = it
            nt = chunks[j]["nt"]
            pseg, cbm = ist[it]["pseg"], ist[it]["cbm"]
            segv = pseg[0:nt, :].rearrange("p (h l) -> p h l", l=128)[:, :, 0:nt]
            Mg = M_r.take()
            k.op("dve", lambda eng: eng.tensor_tensor(out=Mg[0:nt, :, 0:nt], in0=segv,
                                                      in1=cbm[0:nt, 0:nt].unsqueeze(1).broadcast_to([nt, 4, nt]), op=ALU.mult),
                 reads=[pseg, cbm], writes=[Mg])
            ist[it].update(Mg=Mg)

        def S3(it):
            j, g = it
            c = chunks[j]
            nt, o = c["nt"], c["o"]
            Mg, xg = ist[it]["Mg"], ist[it]["xg"]
            yz, ssg, eacs, cdB = c["yz"], c["ssg"], c["eacs"], c["cdB"]
            gs = slice(g * 256, (g + 1) * 256)
            py = ps_y.take()
            k.pe_quiet(lambda eng: eng.matmul(py[0:nt, 0:256], lhsT=ident_b[0:nt, 0:nt], rhs=xg[0:nt, 1, :], start=True, stop=False),
                       reads=[ident_b, xg], writes=[py])
            for hh in range(4):
                fn = lambda eng: eng.matmul(py[0:nt, hh * 64:(hh + 1) * 64], lhsT=Mg[0:nt, hh, 0:nt], rhs=xg[0:nt, 2, hh * 64:(hh + 1) * 64],
                                            start=False, stop=(hh == 3))
                if hh < 3:
                    k.pe_quiet(fn, reads=[Mg, xg], writes=[py])
                else:
                    k.op("pe", fn, reads=[Mg, xg], writes=[py])
            k.op("pe", lambda eng: eng.matmul(py[0:nt, 256:512], lhsT=CT[g][:, o:o + nt], rhs=hbf[g][:, :], start=True, stop=True),
                 reads=[CT[g], hbf[g]], writes=[py])
            pst = ps_st.take()
            k.op("pe", lambda eng: eng.matmul(pst[:, 0:256], lhsT=B_tok[j][0:nt, g * 128:(g + 1) * 128], rhs=xg[0:nt, 0, :], start=True, stop=True),
                 reads=[B_tok[j], xg], writes=[pst])
            t2 = tmp256.take()
            k.op("dve", lambda eng: eng.tensor_tensor(out=t2[0:nt, :].rearrange("p (h d) -> p h d", d=HD),
                                                      in0=py[0:nt, 256:512].rearrange("p (h d) -> p h d", d=HD),
                                                      in1=eacs[0:nt, g * 4:(g + 1) * 4].unsqueeze(2).broadcast_to([nt, 4, HD]), op=ALU.mult),
                 reads=[py, eacs], writes=[t2])
            ybf = ybf_r.take()
            k.op("dve", lambda eng: eng.tensor_tensor(out=ybf[0:nt, :], in0=py[0:nt, 0:256], in1=t2[0:nt, :], op=ALU.add),
                 reads=[py, t2], writes=[ybf])
            k.op("dve", lambda eng: eng.tensor_tensor(out=yz[0:nt, gs], in0=ybf[0:nt, :], in1=zs_tok[j][0:nt, gs], op=ALU.mult),
                 reads=[ybf, zs_tok[j]], writes=[yz.subs[g]])
            ist[it].update(pst=pst)

        def S4a(it):
            j, g = it
            pst = ist[it]["pst"]
            gs = slice(g * 256, (g + 1) * 256)
            hres = hst_res[g]
            k.op("dve", lambda eng: eng.tensor_tensor(out=hst[:, gs], in0=pst[:, 0:256], in1=hst[:, gs], op=ALU.add),
                 reads=[pst, hres], writes=[hres])

        def S4b(it):
            j, g = it
            c = chunks[j]
            nt = c["nt"]
            yz, ssg = c["yz"], c["ssg"]
            gs = slice(g * 256, (g + 1) * 256)
            hres = hst_res[g]
            junk_act = junk_r.take()
            k.op("act", lambda eng: eng.activation(out=junk_act[0:nt, 0:256], in_=yz[0:nt, gs], func=AF.Square, accum_out=ssg[0:nt, g:g + 1]),
                 reads=[yz.subs[g]], writes=[junk_act, ssg])
            k.op("pool", lambda eng: eng.tensor_copy(out=hbf[g][:, :], in_=hst[:, gs]), reads=[hres], writes=[hbf[g]])
            del ist[it]
            if g == NG - 1:
                epilogue(c)

        def epilogue(c):
            nt, o, yz, ssg = c["nt"], c["o"], c["yz"], c["ssg"]
            rg = rstd_from_ss(ssg[0:nt, 0:8], ssg, 256, 8, nt)
            yg = yz
            for g in range(NG):
                gs = slice(g * 256, (g + 1) * 256)
                k.op("dve", lambda eng: eng.scalar_tensor_tensor(out=yg[0:nt, gs], in0=yz[0:nt, gs], scalar=rg[0:nt, g:g + 1],
                                                                 in1=gsn_bc[0:nt, gs], op0=ALU.mult, op1=ALU.mult),
                     reads=[yz.subs[g], rg, gsn_bc], writes=[yz.subs[g]])
            for half in range(2):
                pb = ps_y.take()
                pv = psb_view(pb)
                for q in range(8):
                    fc = half * 8 + q
                    (k.op if q == 7 else k.pe_quiet)(*(("pe",) if q == 7 else ()),
                        lambda eng: eng.transpose(out=pv[:, q * 128:q * 128 + nt], in_=yg[0:nt, fc * 128:(fc + 1) * 128],
                                                  identity=ident_b[0:nt, 0:nt]), reads=[yz.subs[fc // 2], ident_b], writes=[pb])
                k.op("act", lambda eng: eng.activation(
                    out=ygT_all[:, half * 8:(half + 1) * 8, o:o + nt],
                    in_=pv.rearrange("p (q t) -> p q t", t=128)[:, :, 0:nt], func=AF.Copy), reads=[pb], writes=[ygT_all])

        n_it = len(items)
        ps_cb = Ring(psf.bufs[0:1])
        ps_seg = Ring(psf.bufs[1:3])
        ps_y = Ring(psf.bufs[3:5])
        ps_st = Ring(psf.bufs[5:8])
        def at(fn, idx):
            if 0 <= idx < n_it:
                fn(items[idx])

        for i in range(n_it + 4):
            at(S2a, i - 2)
            at(S4a, i - 4)
            at(S0, i)
            at(S1a, i - 1)
            at(S2b, i - 2)
            at(S3, i - 3)
            at(S4b, i - 4)
            at(S1b, i - 1)

        if nxt_sp is not None:
            prep_ew(nxt_sp)
        mixT = [chunk.take() for _ in range(8)]
        for cb in range(4):
            sl, W = wnext()
            for dl in range(2):
                dc = cb * 2 + dl
                gi = dc // 2
                ps = psf.take()
                mm_group(ps[:, 0:TT], [(W[:, kc, dl * 128:(dl + 1) * 128], ygT_all[:, kc, 0:TT]) for kc in range(16)], [sl, ygT_all], ps)
                ps2 = psf.take()
                mm_group(ps2[:, 0:TT], [(wpool[:, gi * 2 + kc, (dc % 2) * 128:(dc % 2 + 1) * 128], pmT[gi * 2 + kc][:, 0:TT]) for kc in range(2)],
                         [wpool, pmT[gi * 2], pmT[gi * 2 + 1]], ps2)
                a_ = ab_r.take()
                k.op("dve", lambda eng, ps=ps, a_=a_, dc=dc: eng.tensor_tensor(out=a_[:, 0:TT], in0=ps[:, 0:TT], in1=gT[dc][:, 0:TT], op=ALU.mult),
                     reads=[ps, gT[dc]], writes=[a_])
                b_ = ab_r.take()
                k.op("dve", lambda eng, ps2=ps2, b_=b_, dc=dc: eng.scalar_tensor_tensor(
                    out=b_[:, 0:TT], in0=ps2[:, 0:TT], scalar=colp[:, C_PSC + dc:C_PSC + dc + 1], in1=gT[8 + dc][:, 0:TT],
                    op0=ALU.mult, op1=ALU.mult), reads=[ps2, colp, gT[8 + dc]], writes=[b_])
                k.op("pool", lambda eng, a_=a_, b_=b_, dc=dc: eng.tensor_tensor(out=mixT[dc][:, 0:TT], in0=a_[:, 0:TT], in1=b_[:, 0:TT], op=ALU.add),
                     reads=[a_, b_], writes=[mixT[dc]])
        if nxt_sp is not None:
            prep_pe(nxt_sp)
        pso = [[None, None] for _ in subs]
        for ob in range(2):
            sl, W = wnext()
            for j, nt in enumerate(subs):
                ps = psf.take()
                mm_group(ps[0:nt, 0:512], [(mixT[kc][:, offs[j]:offs[j] + nt], W[:, kc, :]) for kc in range(8)], [sl] + mixT, ps)
                pso[j][ob] = ps
        for j, nt in enumerate(subs):
            ssA = small.take()
            for ob in range(2):
                junk_act = junk_r.take()
                k.op("act", lambda eng, ob=ob: eng.activation(out=junk_act[0:nt, 0:512], in_=pso[j][ob][0:nt, 0:512], func=AF.Square,
                                                              accum_out=ssA[0:nt, ob:ob + 1]), reads=[pso[j][ob]], writes=[junk_act, ssA])
            ss = small.take()
            k.op("dve", lambda eng: eng.tensor_tensor(out=ss[0:nt, 0:1], in0=ssA[0:nt, 0:1], in1=ssA[0:nt, 1:2], op=ALU.add),
                 reads=[ssA], writes=[ss])
            r_ = rstd_from_ss(ss[0:nt, 0:1], ss, D, 1, nt)
            tmp = f32tmp.take()
            for ob in range(2):
                k.op("dve", lambda eng, ob=ob: eng.scalar_tensor_tensor(
                    out=tmp[0:nt, ob * 512:(ob + 1) * 512], in0=pso[j][ob][0:nt, 0:512], scalar=r_[0:nt, 0:1],
                    in1=G1[0:nt, ob * 512:(ob + 1) * 512], op0=ALU.mult, op1=ALU.mult),
                    reads=[pso[j][ob], r_, G1], writes=[tmp])
            k.op("pool", lambda eng: eng.tensor_tensor(out=xbs[j][0:nt, :], in0=xbs[j][0:nt, :], in1=tmp[0:nt, :], op=ALU.add),
                 reads=[xbs[j], tmp], writes=[xbs[j]])

        vxn = norm_phase1(xbs, subs)
        if nxt_sp is not None:
            zproj(nxt_sp)
        vT = [chunk.take() for _ in range(8)]
        norm_phase2(vxn, subs, offs, A2, B2, s, vT)
        hdnT = [chunk.take() for _ in range(32)]
        if nxt_sp is not None:
            build_dg_block(0)
            build_dg_block(1)
        for cb in range(8):
            sl, W = wnext()
            for cc in range(4):
                fc = cb * 4 + cc
                ps = psf.take()
                mm_group(ps[:, 0:TT], [(W[:, kc, cc * 128:(cc + 1) * 128], vT[kc][:, 0:TT]) for kc in range(8)], [sl] + vT, ps)
                rl = relu_r.take()
                k.op("act", lambda eng, ps=ps, rl=rl: eng.activation(out=rl[:, 0:TT], in_=ps[:, 0:TT], func=AF.Relu), reads=[ps], writes=[rl])
                k.op("pool", lambda eng, rl=rl, fc=fc: eng.tensor_tensor(out=hdnT[fc][:, 0:TT], in0=rl[:, 0:TT], in1=rl[:, 0:TT], op=ALU.mult),
                     reads=[rl], writes=[hdnT[fc]])
        dnbuf = [f32tmp.take() for _ in subs]
        for cb in range(4):
            psd = [psf.take() for _ in subs]
            for kh in range(2):
                sl, W = wnext()
                for j, nt in enumerate(subs):
                    for kk in range(16):
                        fc = kh * 16 + kk
                        fn = lambda eng, j=j, nt=nt, kk=kk, fc=fc: eng.matmul(psd[j][0:nt, 0:256], lhsT=hdnT[fc][:, offs[j]:offs[j] + nt], rhs=W[:, kk, :],
                                                                              start=(kh == 0 and kk == 0), stop=(kh == 1 and kk == 15))
                        if kk < 15:
                            k.pe_quiet(fn, reads=[sl, hdnT[fc]], writes=[psd[j]])
                        else:
                            k.op("pe", fn, reads=[sl, hdnT[fc]], writes=[psd[j]])
            for j, nt in enumerate(subs):
                k.op("act", lambda eng, j=j, nt=nt: eng.activation(out=dnbuf[j][0:nt, cb * 256:(cb + 1) * 256], in_=psd[j][0:nt, 0:256], func=AF.Copy),
                     reads=[psd[j]], writes=[dnbuf[j]])
        for j, nt in enumerate(subs):
            ss = small.take()
            junk_act = junk_r.take()
            k.op("act", lambda eng: eng.activation(out=junk_act[0:nt, :], in_=dnbuf[j][0:nt, :], func=AF.Square, accum_out=ss[0:nt, 0:1]),
                 reads=[dnbuf[j]], writes=[junk_act, ss])
            r_ = rstd_from_ss(ss[0:nt, 0:1], ss, D, 1, nt)
            k.op("dve", lambda eng: eng.scalar_tensor_tensor(out=dnbuf[j][0:nt, :], in0=dnbuf[j][0:nt, :], scalar=r_[0:nt, 0:1], in1=G2[0:nt, :],
                                                             op0=ALU.mult, op1=ALU.mult), reads=[dnbuf[j], r_, G2], writes=[dnbuf[j]])
            k.op("dve", lambda eng: eng.tensor_tensor(out=xbs[j][0:nt, :], in0=xbs[j][0:nt, :], in1=dnbuf[j][0:nt, :], op=ALU.add),
                 reads=[xbs[j], dnbuf[j]], writes=[xbs[j]])
            k.dma("sp", sst_ring, y_rows[offs[j]:offs[j] + nt, :], xbs[j][0:nt, :], reads=[xbs[j]])

    def init_state(zero, s):
        if zero:
            for g in range(NG):
                k.op("pool", lambda eng, g=g: eng.memset(hst[:, g * 256:(g + 1) * 256], 0.0), writes=[hst_res[g]])
                k.op("pool", lambda eng, g=g: eng.memset(hbf[g][:, :], 0.0), writes=[hbf[g]])
            k.op("pool", lambda eng: eng.memset(hist[:], 0.0), writes=[hist])
            for pc in range(8):
                k.op("pool", lambda eng, pc=pc: eng.memset(pp[pc][:, 0:15], 0.0), writes=[pp[pc]])
        else:
            k.dma("sp", ld_ring, hst[:], h0T_d, writes=hst_res)
            for g in range(NG):
                k.op("act", lambda eng, g=g: eng.activation(out=hbf[g][:, :], in_=hst[:, g * 256:(g + 1) * 256], func=AF.Copy),
                     reads=[hst_res[g]], writes=[hbf[g]])
            k.dma("sp", ld_ring, convout[:], convT_d, writes=[convout])
            k.op("dve", lambda eng: eng.tensor_copy(out=hist[:], in_=convout[:]), reads=[convout], writes=[hist])
            for pc in range(8):
                k.dma("sp", ld_ring, pp[pc][:, 0:15], poolT_d[:, pc, :], writes=[pp[pc]])

    def store_state(hT_out, conv_out_d, pool_out_d):
        k.dma("pool", st_ring, hT_out, hst[:], reads=hst_res)
        k.dma("pool", st_ring, conv_out_d, convout[:], reads=[convout])
        for pc in range(8):
            k.dma("pool", st_ring, pool_out_d[:, pc, :], pp[pc][:, 0:15], reads=[pp[pc]])

    psf.bufs.append(psx)
    specs = []
    for ti in range(n_ptiles):
        specs.append(dict(s=0, x_rows=xp_d[ti * TT_MAX:(ti + 1) * TT_MAX, :], y_rows=yp_d[ti * TT_MAX:(ti + 1) * TT_MAX, :],
                          subs=[TSUB] * NSUB, first=(ti == 0), last=(ti == n_ptiles - 1), pos0=(ti == 0)))
    if do_sample:
        specs.append(dict(s=1, x_rows=xs_d, y_rows=ys_d, subs=[DEC], first=True, last=True, pos0=False))
    init_state(True, 0)
    for ti in range(n_ptiles):
        run_tile(specs[ti], specs[ti + 1] if ti + 1 < len(specs) else None)
    store_state(hTp_d, convp_d, poolp_d)
    if do_sample:
        k.dma("sp", ld_ring, G1[:], gs_scr[0], reads=[gs_res], writes=[G1])
        k.dma("sp", ld_ring, G2[:], gs_scr[1], reads=[gs_res], writes=[G2])
        init_state(False, 1)
        run_tile(specs[-1], None)
        store_state(hTs_d, convs_d, pools_d)
    k.finish()
    es.close()
    return nc


def make_in_maps(inputs, n_ptiles=SEQ // TT_MAX):
    f = lambda a: np.ascontiguousarray(np.asarray(a, dtype=np.float32))
    ntok = n_ptiles * TT_MAX

    def col(v, n):
        return f(v).reshape(n, 128).T

    colp = np.zeros((128, NCOL), np.float32)
    colp[:, C_BADA:C_BADA + 48] = col(inputs["b_ada"][0], 48)
    colp[:, C_GPM:C_GPM + 8] = col(inputs["g_pre_mix"][0], 8)
    colp[:, C_GPL:C_GPL + 8] = col(inputs["g_pre_mlp"][0], 8)
    cw = f(inputs["conv_w"][0])
    colp[:, C_CW:C_CW + 128] = cw.reshape(4, 32, 128).transpose(2, 1, 0).reshape(128, 128)
    colp[:, C_CB:C_CB + 32] = col(inputs["conv_b"][0], 32)
    colp[:, C_GSN:C_GSN + 16] = col(inputs["g_ssd_norm"][0], 16)
    colp[:, C_PSC:C_PSC + 8] = col(inputs["pool_scale"][0], 8)
    b_ada = f(inputs["b_ada"][0])
    rbada = np.broadcast_to(np.concatenate([b_ada[2048:3072], b_ada[5120:6144]])[None, :], (128, 2048))
    rgpost = np.broadcast_to(np.concatenate([f(inputs["g_post_mix"][0]), f(inputs["g_post_mlp"][0])])[None, :], (128, 2048))
    rhead = np.broadcast_to(np.concatenate([f(inputs["dt_bias"][0]), f(inputs["a_log"][0]), f(inputs["d_skip"][0])])[None, :], (128, 96))
    rgsn = np.broadcast_to(f(inputs["g_ssd_norm"][0])[None, :], (128, DI))
    shared = dict(
        colp=f(colp), rbada=f(rbada), rgpost=f(rgpost), rhead=f(rhead), rgsn=f(rgsn),
        w_ada=f(inputs["w_ada"][0]), w_in=f(inputs["w_in"][0]), w_so=f(inputs["w_ssd_out"][0]),
        w_pool=f(inputs["w_pool_group"][0]).reshape(1024, 256), w_o=f(inputs["w_o"][0]),
        w_up=f(inputs["w_up"][0]), w_down=f(inputs["w_down"][0]),
    )
    maps = []
    for i in range(8):
        m = dict(shared)
        m["xp"] = f(inputs["x_prompt"][i][:ntok])
        m["xs"] = f(inputs["x_sample"][i])
        cc = np.stack([col(inputs["c_prompt"][i], 8), col(inputs["c_sample"][i], 8)], axis=-1)
        m["ccol"] = f(cc)
        m["h0T"] = f(np.asarray(inputs["state_ssm"][0, i]).reshape(NH * HD, NS).T)
        m["convT"] = f(np.asarray(inputs["state_conv"][0, i]).reshape(3, 32, 128).transpose(2, 1, 0))
        m["poolT"] = f(np.asarray(inputs["state_pool"][0, i]).reshape(15, 8, 128).transpose(2, 1, 0))
        maps.append(m)
    return maps


def gather(results, n_ptiles=SEQ // TT_MAX):
    ntok = n_ptiles * TT_MAX
    yp = np.stack([r["yp"] for r in results]).reshape(8, ntok, D)
    ys = np.stack([r["ys"] for r in results]).reshape(8, DEC, D)

    def ssm(key):
        return np.stack([np.asarray(r[key]).reshape(128, NH, HD).transpose(1, 2, 0) for r in results])[None]

    def conv(key):
        return np.stack([np.asarray(r[key]).reshape(128, 32, 3).transpose(2, 1, 0).reshape(3, CONVD) for r in results])[None]

    def pool(key):
        return np.stack([np.asarray(r[key]).reshape(128, 8, 15).transpose(2, 1, 0).reshape(15, D) for r in results])[None]

    outs = (yp, ys, ssm("hTp"), conv("convp"), pool("poolp"), ssm("hTs"), conv("convs"), pool("pools"))
    return tuple(np.ascontiguousarray(o, dtype=np.float32) for o in outs)


_NC_CACHE = {}


def kernel(**inputs):
    if "nc" not in _NC_CACHE:
        _NC_CACHE["nc"] = build_nc()
    nc = _NC_CACHE["nc"]
    in_maps = make_in_maps(inputs)
    res = run_bass_kernel_spmd(nc, in_maps, core_ids=list(range(8)))
    return gather(res.results)
```
